# Optimizing a Trainium2 kernel written in Bass

```python
import math
import jax, jax.numpy as jnp
from jax import lax
import numpy as np

D_MODEL = 1024
BATCH = 16
SEQ = 2048
DEPTH = 1

N_ATTN_HEADS = 8
HEAD_DIM = 64
ATTN_WIDTH = N_ATTN_HEADS * HEAD_DIM
N_CONV_GROUPS = 8
CONV_WIDTH = D_MODEL // 2
CONV_K = 3
D_FF = 2816
Q_BLOCK = 128
RMS_EPS = 1e-6
FFN_RESIDUAL_WEIGHT = 0.5
FORGET_BIAS_MEAN = 3.0

IN_SPLITS = (
    ATTN_WIDTH,
    ATTN_WIDTH,
    ATTN_WIDTH,
    N_ATTN_HEADS,
    CONV_WIDTH,
    CONV_WIDTH,
    CONV_WIDTH,
    D_MODEL,
    D_MODEL,
)
IN_COLS = sum(IN_SPLITS)

kernel_name = "fox_shortconv_gated_macaron_layer"


def rms_norm(x, g):
    xf = x.astype(jnp.float32)
    inv = lax.rsqrt(jnp.mean(xf * xf, axis=-1, keepdims=True) + RMS_EPS)
    return (xf * inv).astype(x.dtype) * g


def swiglu(x, w_gate, w_up, w_down):
    return (jax.nn.silu(x @ w_gate) * (x @ w_up)) @ w_down


def forgetting_attention(q, k, v, f_logits, b_forget):
    seq = q.shape[1]
    scale = 1.0 / math.sqrt(HEAD_DIM)
    log_f = jax.nn.log_sigmoid(f_logits.astype(jnp.float32) + b_forget.astype(jnp.float32))
    cum = jnp.transpose(jnp.cumsum(log_f, axis=1), (0, 2, 1))
    outs = []
    n_blocks = seq // Q_BLOCK
    for i in range(n_blocks):
        q0, q1 = i * Q_BLOCK, (i + 1) * Q_BLOCK
        kv_len = q1
        q_blk = q[:, q0:q1]
        k_pre = k[:, :kv_len]
        v_pre = v[:, :kv_len]
        s = jnp.einsum('bqhd,bkhd->bhqk', q_blk, k_pre).astype(jnp.float32) * scale
        s = s + cum[:, :, q0:q1, None] - cum[:, :, None, :kv_len]
        q_pos = jnp.arange(q0, q1)[:, None]
        k_pos = jnp.arange(kv_len)[None, :]
        s = jnp.where(q_pos >= k_pos, s, -jnp.inf)
        p = jax.nn.softmax(s, axis=-1).astype(v.dtype)
        outs.append(jnp.einsum('bhqk,bkhd->bqhd', p, v_pre))
    return jnp.concatenate(outs, axis=1)


def short_conv_mixer(xin, gate_b, gate_c, conv_w):
    seq = xin.shape[1]
    u = gate_c * xin
    up = jnp.pad(u, ((0, 0), (CONV_K - 1, 0), (0, 0)))
    conv = (conv_w[0] * up[:, 0:seq] + conv_w[1] * up[:, 1:seq + 1]
            + conv_w[2] * up[:, 2:seq + 2])
    return gate_b * conv


def setup_inputs(seed: int = 0) -> dict:
    key = jax.random.key(seed)
    ks = jax.random.split(key, 20)
    f32 = jnp.float32

    def lin(k, fan_in, fan_out):
        return jax.random.normal(k, (fan_in, fan_out), f32) * fan_in ** -0.5

    def gain(k, n):
        return jnp.ones((n,), f32) + 0.02 * jax.random.normal(k, (n,), f32)

    return {
        "x": jax.random.normal(ks[0], (BATCH, SEQ, D_MODEL), f32),
        "ffn1_norm": gain(ks[1], D_MODEL),
        "ffn1_gate": lin(ks[2], D_MODEL, D_FF),
        "ffn1_up": lin(ks[3], D_MODEL, D_FF),
        "ffn1_down": lin(ks[4], D_FF, D_MODEL),
        "mix_norm": gain(ks[5], D_MODEL),
        "w_in": lin(ks[6], D_MODEL, IN_COLS),
        "b_forget": FORGET_BIAS_MEAN + 0.5 * jax.random.normal(ks[7], (N_ATTN_HEADS,), f32),
        "conv_w": 0.5 * jax.random.normal(ks[8], (CONV_K, CONV_WIDTH), f32),
        "w_o_attn": lin(ks[9], ATTN_WIDTH, D_MODEL),
        "w_o_conv": lin(ks[10], CONV_WIDTH, D_MODEL),
        "w_out": lin(ks[11], D_MODEL, D_MODEL),
        "ffn2_norm": gain(ks[12], D_MODEL),
        "ffn2_gate": lin(ks[13], D_MODEL, D_FF),
        "ffn2_up": lin(ks[14], D_MODEL, D_FF),
        "ffn2_down": lin(ks[15], D_FF, D_MODEL),
        "final_norm": gain(ks[16], D_MODEL),
    }


def reference(x, ffn1_norm, ffn1_gate, ffn1_up, ffn1_down, mix_norm, w_in,
              b_forget, conv_w, w_o_attn, w_o_conv, w_out, ffn2_norm,
              ffn2_gate, ffn2_up, ffn2_down, final_norm):
    bsz, seq, _ = x.shape
    for _layer in range(DEPTH):
        x = x + FFN_RESIDUAL_WEIGHT * swiglu(rms_norm(x, ffn1_norm), ffn1_gate, ffn1_up, ffn1_down)

        h = rms_norm(x, mix_norm)
        proj = h @ w_in
        offsets = list(np.cumsum(IN_SPLITS)[:-1])
        q, k, v, f_log, c_b, c_c, c_x, g_attn, g_conv = jnp.split(proj, offsets, axis=-1)

        heads = lambda t: t.reshape(bsz, seq, N_ATTN_HEADS, HEAD_DIM)
        y_attn = forgetting_attention(heads(q), heads(k), heads(v), f_log, b_forget)
        y_attn = y_attn.reshape(bsz, seq, ATTN_WIDTH) @ w_o_attn

        y_conv = short_conv_mixer(c_x, c_b, c_c, conv_w) @ w_o_conv

        merged = jax.nn.sigmoid(g_attn) * y_attn + jax.nn.sigmoid(g_conv) * y_conv
        x = x + merged @ w_out

        x = x + FFN_RESIDUAL_WEIGHT * swiglu(rms_norm(x, ffn2_norm), ffn2_gate, ffn2_up, ffn2_down)
    return rms_norm(x, final_norm)
```

```python
import os
from contextlib import ExitStack
import numpy as np
import concourse.bass as bass
import concourse.mybir as mybir
from concourse.bass_utils import run_bass_kernel_spmd

F32 = mybir.dt.float32
BF16 = mybir.dt.bfloat16
AF = mybir.ActivationFunctionType
ALU = mybir.AluOpType

ENGINES = ("pe", "act", "dve", "pool", "sp")
NSEQ = 2
SEQ = 2048
D = 1024
DFF = 2816
NCH = 22
NSLOT = 4
EPS = 1e-6
MASKVAL = -30000.0


class Buf:
    __slots__ = ("name", "last_w", "reads", "owner")

    def __init__(self, name):
        self.name = name
        self.last_w = None
        self.reads = {}
        self.owner = None


class Op:
    __slots__ = ("eng", "idx", "fn", "waits", "signal", "sigval", "dma_sem", "is_dma", "order")

    def __init__(self, eng, idx, fn, dma_sem=None):
        self.eng = eng
        self.idx = idx
        self.fn = fn
        self.waits = []
        self.signal = False
        self.sigval = None
        self.is_dma = dma_sem is not None
        self.dma_sem = dma_sem
        self.order = idx


class Sched:
    def __init__(self):
        self.prog = {e: [] for e in ENGINES}
        self.waited = {e: {} for e in ENGINES}
        self.dma_count = {}
        self.dma_last = {}
        self.same_engine_sync = {"act", "dve", "pool"}
        self.barrier_deps = []

    def op(self, eng, fn, reads=(), writes=(), dma_sem=None, extra_deps=(), exempt=False):
        o = Op(eng, len(self.prog[eng]), fn, dma_sem=dma_sem)
        if o.is_dma:
            self.dma_count[dma_sem] = self.dma_count.get(dma_sem, 0) + 1
            o.order = self.dma_count[dma_sem]
            self.dma_last[dma_sem] = o
        deps = []
        for b in reads:
            if b.last_w is not None:
                deps.append(b.last_w)
        for b in writes:
            if b.last_w is not None:
                deps.append(b.last_w)
            deps.extend(b.reads.values())
        deps.extend(extra_deps)
        if not exempt:
            deps.extend(self.barrier_deps)
        w = self.waited[eng]
        for d in deps:
            if d is o:
                continue
            if (not d.is_dma) and d.eng == eng and eng not in self.same_engine_sync:
                continue
            key = ("dma", d.dma_sem) if d.is_dma else ("eng", d.eng)
            if w.get(key, -1) >= d.order:
                continue
            w[key] = d.order
            d.signal = True
            o.waits.append(d)
        self.prog[eng].append(o)
        rkey = ("dma", dma_sem) if o.is_dma else eng
        for b in reads:
            b.reads[rkey] = o
        for b in writes:
            b.last_w = o
            b.reads = {}
        return o

    def barrier(self, include_dma_keys=()):
        deps = []
        for e in ("pe", "act", "dve", "sp"):
            for o in reversed(self.prog[e]):
                if not o.is_dma:
                    deps.append(o)
                    break
        for k in include_dma_keys:
            if k in self.dma_last:
                deps.append(self.dma_last[k])
        self.barrier_deps = deps

    def emit(self, nc, sems, dma_sems):
        for e in ENGINES:
            c = 0
            for o in self.prog[e]:
                if o.is_dma:
                    o.sigval = 16 * o.order
                elif o.signal:
                    c += 1
                    o.sigval = c

        def run(engname, eobj):
            for o in self.prog[engname]:
                for d in o.waits:
                    s = dma_sems[d.dma_sem] if d.is_dma else sems[d.eng]
                    eobj.wait_ge(s, d.sigval)
                ins = o.fn(eobj)
                if o.is_dma:
                    ins.then_inc(dma_sems[o.dma_sem], 16)
                elif o.signal:
                    ins.then_inc(sems[o.eng], 1)

        with nc.Block() as block:
            @block.tensor
            def _(e):
                run("pe", e)

            @block.scalar
            def _(e):
                run("act", e)

            @block.vector
            def _(e):
                run("dve", e)

            @block.gpsimd
            def _(e):
                run("pool", e)

            @block.sync
            def _(e):
                run("sp", e)


def build_nc(stop_after=None, nseq=NSEQ):
    nc = bass.Bass("TRN2", target_bir_lowering=False)

    def din(name, shape):
        return nc.dram_tensor(name, list(shape), F32, kind="ExternalInput").ap()

    x = din("x", [nseq, SEQ, D])
    g1_d = din("g1", [128, 8])
    gm_d = din("gm", [128, 8])
    g2_d = din("g2", [128, 8])
    gfin_d = din("gfin", [128, D])
    bfb_d = din("bfb", [128, 128])
    cw_d = din("cw", [128, 12])
    w_g = [din("ffn1_gate", [D, DFF]), din("ffn2_gate", [D, DFF])]
    w_u = [din("ffn1_up", [D, DFF]), din("ffn2_up", [D, DFF])]
    w_d = [din("ffn1_down", [DFF, D]), din("ffn2_down", [DFF, D])]
    w_in = din("w_in", [D, 5128])
    w_oa = din("w_o_attn", [512, D])
    w_oc = din("w_o_conv", [512, D])
    w_out = din("w_out", [D, D])
    out = nc.dram_tensor("out", [nseq, SEQ, D], F32, kind="ExternalOutput").ap()

    S = Sched()
    es = ExitStack()
    with es:
        def sb(name, shape, dt):
            return es.enter_context(nc.sbuf_tensor(name, shape, dt))

        xT = sb("xT", [128, 8, SEQ], F32)
        hT = sb("hT", [128, 8, SEQ], BF16)
        ring = sb("ring", [128, NSLOT, 4096], BF16)
        O = sb("O", [128, 24576], BF16)
        attnT = sb("attnT", [128, 4, SEQ], BF16)
        ident = sb("ident", [128, 128], F32)
        triT = sb("triT", [128, 128], F32)
        onesF = sb("onesF", [128, 128], F32)
        identb = sb("identb", [128, 128], BF16)
        maskb = sb("maskb", [128, 128], BF16)
        negtri = sb("negtri", [128, 128], BF16)
        negones = sb("negones", [128, 128], BF16)
        onesb = sb("onesb", [128, 128], BF16)
        gt = [sb("g1t", [128, 8], F32), sb("gmt", [128, 8], F32), sb("g2t", [128, 8], F32)]
        cwt = sb("cwt", [128, 12], F32)
        bfb = sb("bfbt", [128, 128], F32)
        wf = sb("wf", [128, 8, 8], BF16)
        epst = sb("epst", [128, 1], F32)
        sst = sb("sst", [128, 8], F32)
        sqb = [sb("sqb0", [128, 512], BF16), sb("sqb1", [128, 512], BF16)]
        rstd = sb("rstd", [128, 512], F32)
        gfin = sb("gfin_t", [128, D], F32)
        PTx = sb("PTx", [128, 2, 512], BF16)
        rc2 = sb("rc2", [128, 512], F32)
        PS = es.enter_context(nc.psum_tensor("PS", [128, 8, 512], F32))

        sems = {e: es.enter_context(nc.semaphore("s_" + e)) for e in ENGINES}
        dma_keys = ["slot%d" % i for i in range(NSLOT)] + ["stg0", "stg1", "stg2", "stg3", "augA0", "augA1", "augB0", "augB1", "cst", "cstp", "gfin"]
        dsems = {k: es.enter_context(nc.semaphore("d_" + k)) for k in dma_keys}

        def Obf(a, n):
            return O[:, a:a + n]

        def Of32(a, n):
            return O[:, a:a + 2 * n].bitcast(F32)

        actT = O[:, 0:22528].rearrange("p (c t) -> p c t", c=22)
        sil = [Of32(22528, 512), Of32(23552, 512)]
        Vaug = O[:, 0:12288].rearrange("p (t j k d) -> p t j k d", t=16, j=4, k=3)
        qA = Obf(12288, 2048)
        qB = Obf(14336, 2048)
        kA = Obf(16384, 2048)
        kB = Obf(18432, 2048)
        PT = [Obf(20480 + 512 * i, 512) for i in range(3)] + [PTx[:, i, :] for i in range(2)]
        zt = Of32(22016, 128)
        spt = Of32(22272, 128)
        spx = Of32(22528, 128)
        sp_bf = Obf(22784, 128)
        spx_bf = Obf(22912, 128)
        Ck = Of32(23040, 128)
        rc = Of32(23296, 512)
        convin = O[:, 0:4096].rearrange("p (c t) -> p c t", c=4)
        merged = O[:, 4096:12288].rearrange("p (c t) -> p c t", c=8)
        ubuf = [Of32(12288 + 1028 * i, 514) for i in range(4)]
        cxs = Of32(16400, 512)
        tconv = Of32(17424, 512)
        sa_t = Of32(18448, 512)
        sc_t = Of32(19472, 512)
        m1_t = Of32(20496, 512)
        m2_t = Of32(21520, 512)

        B_xT = [[Buf("xT%d_%d" % (k, s)) for s in range(4)] for k in range(8)]
        B_hT = [[Buf("hT%d_%d" % (k, s)) for s in range(4)] for k in range(8)]
        B_slot = [Buf("slot%d" % i) for i in range(NSLOT)]
        B_bank = [Buf("bank%d" % i) for i in range(8)]
        B_act = [[Buf("act%d_%d" % (c, s)) for s in range(2)] for c in range(NCH)]
        B_sil = [Buf("sil0"), Buf("sil1")]
        B_stg = [Buf("stg%d" % i) for i in range(4)]
        stg = [attnT[:, i, :].bitcast(F32) for i in range(4)]
        stg_i = [0]

        def next_stg():
            i = stg_i[0] % 4
            stg_i[0] += 1
            return i
        B_ssf = [Buf("ssf0"), Buf("ssf1")]
        B_gfin = Buf("gfin")
        B_V = [Buf("V%d" % t) for t in range(16)]
        B_Vones = Buf("Vones")
        B_q = {n: [Buf("%s_%d" % (n, s)) for s in range(4)] for n in ("qA", "qB", "kA", "kB")}
        B_aug = {"A": Buf("augA"), "B": Buf("augB")}
        B_qaux = Buf("qaux")
        B_zero = Buf("qkzero")
        B_PT = [Buf("PT%d" % i) for i in range(6)]
        B_z, B_sp, B_spx, B_spbf, B_spxbf, B_Ck, B_rc = (Buf(n) for n in ("z", "sp", "spx", "spbf", "spxbf", "Ck", "rc"))
        B_rc2 = Buf("rc2")
        B_attn = [[Buf("attn%d_%d" % (j, s)) for s in range(4)] for j in range(4)]
        B_convin = [[Buf("cvi%d_%d" % (j, s)) for s in range(2)] for j in range(4)]
        B_merged = [[Buf("mg%d_%d" % (c, s)) for s in range(2)] for c in range(8)]
        B_u = [Buf("u%d" % j) for j in range(4)]
        B_cxs, B_tconv, B_sa, B_sc, B_m1, B_m2 = (Buf(n) for n in ("cxs", "tconv", "sa", "sc", "m1", "m2"))
        B_sq = [Buf("sq0"), Buf("sq1")]
        B_rstd = Buf("rstd")
        B_ss = Buf("ss")
        B_const = Buf("const")

        bank_i = [0]
        reserved = set()

        def next_bank(exclude=None, pool=None):
            while True:
                b = bank_i[0]
                bank_i[0] = (b + 1) % 8
                if b == exclude or b in reserved:
                    continue
                if pool is not None and b not in pool:
                    continue
                return b

        def next_pair():
            while True:
                if bank_i[0] % 2:
                    bank_i[0] = (bank_i[0] + 1) % 8
                b = bank_i[0]
                bank_i[0] = (b + 2) % 8
                if b in reserved or (b + 1) in reserved:
                    continue
                return b

        pending = []

        def pump(n=1):
            for _ in range(n):
                if pending:
                    pending.pop(0)()

        def flush():
            while pending:
                pending.pop(0)()

        job_i = [0]

        def wload(src, k, c):
            i = job_i[0] % NSLOT
            view = ring[:, i, 0:k * c].rearrange("p (k c) -> p k c", k=k)
            S.op("pool", lambda e: e.dma_start(out=view, in_=src), writes=[B_slot[i]],
                 dma_sem="slot%d" % i, exempt=True)
            B_slot[i].owner = job_i[0]
            tok = (B_slot[i], job_i[0])
            job_i[0] += 1
            return view, tok

        def wload_multi(nelem, parts):
            i = job_i[0] % NSLOT
            flat = ring[:, i, 0:nelem]
            prev = None
            for k, (dstf, src) in enumerate(parts):
                dst = dstf(flat)
                if k == 0:
                    prev = S.op("pool", lambda e, dst=dst, src=src: e.dma_start(out=dst, in_=src), writes=[B_slot[i]],
                                dma_sem="slot%d" % i, exempt=True)
                else:
                    prev = S.op("pool", lambda e, dst=dst, src=src: e.dma_start(out=dst, in_=src),
                                dma_sem="slot%d" % i, exempt=True)
                    B_slot[i].last_w = prev
            B_slot[i].owner = job_i[0]
            tok = (B_slot[i], job_i[0])
            job_i[0] += 1
            return flat, tok

        def wload2(src_a, src_b):
            flat, tok = wload_multi(4096, [
                (lambda f: f.rearrange("p (k c) -> p k c", k=8)[:, 0:4, :], src_a),
                (lambda f: f.rearrange("p (k c) -> p k c", k=8)[:, 4:8, :], src_b)])
            return flat.rearrange("p (k c) -> p k c", k=8), tok

        def wsrc(w, r0, r1, c0, c1):
            return w[r0:r1, c0:c1].rearrange("(k p) c -> p k c", p=128)

        def mm_group(out_ap, pairs, bankbuf):
            n = len(pairs)
            for i, (l, r, rd0) in enumerate(pairs):
                rd = []
                for b in rd0:
                    if isinstance(b, tuple):
                        assert b[0].owner == b[1], "weight slot reused while still live: %s" % b[0].name
                        b = b[0]
                    rd.append(b)
                S.op("pe", lambda e, l=l, r=r, i=i: e.matmul(out_ap, lhsT=l, rhs=r, start=(i == 0), stop=(i == n - 1)),
                     reads=rd, writes=[bankbuf])

        cst_ops = []
        for dst, src in ((gt[0], g1_d), (gt[1], gm_d), (gt[2], g2_d), (cwt, cw_d), (bfb, bfb_d)):
            S.op("sp", lambda e, dst=dst, src=src: e.dma_start(out=dst[:], in_=src), writes=[B_const], dma_sem="cst")
        S.op("pool", lambda e: e.dma_start(out=wf[:], in_=w_in[:, 1536:1544].rearrange("(k p) c -> p k c", p=128)),
             writes=[B_const], dma_sem="cstp")
        S.op("sp", lambda e: e.dma_start(out=gfin[:], in_=gfin_d), writes=[B_gfin], dma_sem="gfin")
        S.op("pool", lambda e: e.memset(ident[:], 1.0), writes=[B_const])
        S.op("pool", lambda e: e.affine_select(out=ident[:], in_=ident[:], pattern=[[-1, 128]], compare_op=ALU.is_equal,
                                               fill=0.0, base=0, channel_multiplier=1), reads=[B_const], writes=[B_const])
        S.op("pool", lambda e: e.memset(onesF[:], 1.0), writes=[B_const])
        S.op("pool", lambda e: e.memset(onesb[:], 1.0), writes=[B_const])
        S.op("pool", lambda e: e.memset(negones[:], -1.0), writes=[B_const])
        S.op("pool", lambda e: e.memset(epst[:], EPS), writes=[B_const])
        S.op("pool", lambda e: e.affine_select(out=triT[:], in_=onesF[:], pattern=[[1, 128]], compare_op=ALU.is_ge,
                                               fill=0.0, base=0, channel_multiplier=-1), reads=[B_const], writes=[B_const])
        S.op("dve", lambda e: e.tensor_scalar(out=negtri[:], in0=triT[:], scalar1=-1.0, scalar2=None, op0=ALU.mult),
             reads=[B_const], writes=[B_const])
        S.op("dve", lambda e: e.tensor_scalar(out=maskb[:], in0=triT[:], scalar1=-MASKVAL, scalar2=MASKVAL,
                                              op0=ALU.mult, op1=ALU.add), reads=[B_const], writes=[B_const])
        S.op("dve", lambda e: e.tensor_copy(out=identb[:], in_=ident[:]), reads=[B_const], writes=[B_const])

        def x_steps(q, tts):
            st = {"tr": [], "ev": None}

            def step(tt):
                if st["ev"] is not None:
                    st["ev"]()
                    st["ev"] = None
                if len(st["tr"]) >= 2 or (tt is None and st["tr"]):
                    st["ev"] = st["tr"].pop(0)()
                if tt is None:
                    return
                si = next_stg()
                S.op("sp", lambda e: e.dma_start(out=stg[si], in_=x[q, tt * 128:(tt + 1) * 128, :]),
                     writes=[B_stg[si]], dma_sem="stg%d" % si)

                def tr():
                    bp = next_pair()
                    reserved.add(bp)
                    reserved.add(bp + 1)
                    for kc in range(8):
                        S.op("pe", lambda e, kc=kc: e.transpose(
                            out=PS[:, bp + kc // 4, (kc % 4) * 128:(kc % 4 + 1) * 128],
                            in_=stg[si][:, kc * 128:(kc + 1) * 128], identity=ident[:]),
                            reads=[B_stg[si], B_const], writes=[B_bank[bp + kc // 4]])

                    def ev():
                        s_ = tt // 4
                        S.op("act", lambda e: e.activation(
                            out=xT[:, 0:4, tt * 128:(tt + 1) * 128], in_=PS[:, bp, :].rearrange("p (k t) -> p k t", k=4),
                            func=AF.Copy), reads=[B_bank[bp]], writes=[B_xT[k][s_] for k in range(4)])
                        S.op("dve", lambda e: e.tensor_copy(
                            out=xT[:, 4:8, tt * 128:(tt + 1) * 128], in_=PS[:, bp + 1, :].rearrange("p (k t) -> p k t", k=4)),
                            reads=[B_bank[bp + 1]], writes=[B_xT[k][s_] for k in range(4, 8)])
                        reserved.discard(bp)
                        reserved.discard(bp + 1)
                    return ev
                st["tr"].append(tr)

            for tt in tts:
                pending.append(lambda tt=tt: step(tt))
            for _ in range(3):
                pending.append(lambda: step(None))

        def final_steps(q, tts):
            tts = list(tts)
            assert len(tts) == 8
            st = {"post": None}

            def transposes(tt):
                s_ = tt // 4
                bp = next_pair()
                reserved.add(bp)
                reserved.add(bp + 1)
                for kc in range(8):
                    S.op("pe", lambda e, kc=kc: e.transpose(
                        out=PS[:, bp + kc // 4, (kc % 4) * 128:(kc % 4 + 1) * 128],
                        in_=xT[:, kc, tt * 128:(tt + 1) * 128], identity=ident[:]),
                        reads=[B_xT[kc][s_], B_const], writes=[B_bank[bp + kc // 4]])
                return bp

            def step1(tt, i):
                if st["post"] is not None:
                    st["post"]()
                    st["post"] = None
                if tt is None:
                    return
                bp = transposes(tt)

                def post():
                    pv = PS[:, bp:bp + 2, :]
                    S.op("act", lambda e: e.activation(out=pv, in_=pv, func=AF.Square, accum_out=sst[:, i:i + 1]),
                         reads=[B_bank[bp], B_bank[bp + 1]], writes=[B_bank[bp], B_bank[bp + 1], B_ssf[0]])
                    reserved.discard(bp)
                    reserved.discard(bp + 1)
                st["post"] = post

            def rstd_step():
                S.op("act", lambda e: e.activation(out=sst[:, 0:8], in_=sst[:, 0:8], func=AF.Sqrt, scale=1.0 / D,
                                                   bias=epst[:, 0:1]), reads=[B_ssf[0], B_const], writes=[B_ssf[0]])
                S.op("dve", lambda e: e.reciprocal(out=sst[:, 0:8], in_=sst[:, 0:8]), reads=[B_ssf[0]], writes=[B_ssf[0]])

            def step2(tt, i):
                if st["post"] is not None:
                    st["post"]()
                    st["post"] = None
                if tt is None:
                    return
                bp = transposes(tt)

                def post():
                    pv = PS[:, bp:bp + 2, :]
                    si = next_stg()
                    ov = stg[si].rearrange("p (a b) -> p a b", a=2)
                    S.op("dve", lambda e: e.scalar_tensor_tensor(
                        out=ov, in0=pv, scalar=sst[:, i:i + 1], in1=gfin[:].rearrange("p (a b) -> p a b", a=2),
                        op0=ALU.mult, op1=ALU.mult),
                        reads=[B_bank[bp], B_bank[bp + 1], B_ssf[0], B_gfin], writes=[B_stg[si]])
                    S.op("sp", lambda e: e.dma_start(out=out[q, tt * 128:(tt + 1) * 128, :], in_=stg[si]),
                         reads=[B_stg[si]], dma_sem="stg%d" % si)
                    reserved.discard(bp)
                    reserved.discard(bp + 1)
                st["post"] = post

            for i, tt in enumerate(tts):
                pending.append(lambda tt=tt, i=i: step1(tt, i))
            pending.append(lambda: step1(None, 0))
            pending.append(rstd_step)
            for i, tt in enumerate(tts):
                pending.append(lambda tt=tt, i=i: step2(tt, i))
            pending.append(lambda: step2(None, 0))

        def final_steps_single(q, tts):
            for n, tt in enumerate(tts):
                def step(tt=tt, n=n):
                    s_ = tt // 4
                    i = n % 2
                    c0 = 4 * i
                    bp = next_pair()
                    for kc in range(8):
                        S.op("pe", lambda e, kc=kc: e.transpose(
                            out=PS[:, bp + kc // 4, (kc % 4) * 128:(kc % 4 + 1) * 128],
                            in_=xT[:, kc, tt * 128:(tt + 1) * 128], identity=ident[:]),
                            reads=[B_xT[kc][s_], B_const], writes=[B_bank[bp + kc // 4]])
                    pv = PS[:, bp:bp + 2, :]
                    si = next_stg()
                    ov = stg[si].rearrange("p (a b) -> p a b", a=2)
                    S.op("act", lambda e: e.activation(out=ov, in_=pv, func=AF.Square, accum_out=sst[:, c0:c0 + 1]),
                         reads=[B_bank[bp], B_bank[bp + 1]], writes=[B_stg[si], B_ssf[i]])
                    S.op("act", lambda e: e.activation(out=sst[:, c0 + 1:c0 + 2], in_=sst[:, c0:c0 + 1], func=AF.Sqrt,
                                                       scale=1.0 / D, bias=epst[:, 0:1]),
                         reads=[B_ssf[i], B_const], writes=[B_ssf[i]])
                    S.op("dve", lambda e: e.reciprocal(out=sst[:, c0 + 2:c0 + 3], in_=sst[:, c0 + 1:c0 + 2]),
                         reads=[B_ssf[i]], writes=[B_ssf[i]])
                    S.op("dve", lambda e: e.scalar_tensor_tensor(
                        out=ov, in0=pv, scalar=sst[:, c0 + 2:c0 + 3], in1=gfin[:].rearrange("p (a b) -> p a b", a=2),
                        op0=ALU.mult, op1=ALU.mult),
                        reads=[B_bank[bp], B_bank[bp + 1], B_ssf[i], B_gfin], writes=[B_stg[si]])
                    S.op("sp", lambda e: e.dma_start(out=out[q, tt * 128:(tt + 1) * 128, :], in_=stg[si]),
                         reads=[B_stg[si]], dma_sem="stg%d" % si)
                pending.append(step)

        def norm(spans, g):
            for s in spans:
                st = {}

                def sq_step(kcs, s=s, st=st):
                    if "bs" not in st:
                        st["bs"] = next_bank()
                        reserved.add(st["bs"])
                        st["mm"] = []
                    bs = st["bs"]
                    for f in st["mm"]:
                        f()
                    st["mm"] = []
                    for kc in kcs:
                        i = kc % 2
                        src = xT[:, kc, s * 512:(s + 1) * 512]
                        if kc % 2 == 0:
                            S.op("act", lambda e, i=i, src=src: e.activation(out=sqb[i][:], in_=src, func=AF.Square),
                                 reads=[B_xT[kc][s]], writes=[B_sq[i]])
                        else:
                            S.op("dve", lambda e, i=i, src=src: e.tensor_tensor(out=sqb[i][:], in0=src, in1=src, op=ALU.mult),
                                 reads=[B_xT[kc][s]], writes=[B_sq[i]])
                        st["mm"].append(lambda i=i, kc=kc, bs=bs: S.op(
                            "pe", lambda e: e.matmul(PS[:, bs, :], lhsT=onesb[:], rhs=sqb[i][:], start=(kc == 0), stop=(kc == 7)),
                            reads=[B_sq[i], B_const], writes=[B_bank[bs]]))

                def rstd_step(s=s, st=st):
                    bs = st["bs"]
                    for f in st["mm"]:
                        f()
                    st["mm"] = []
                    S.op("act", lambda e, bs=bs: e.activation(out=rstd[:], in_=PS[:, bs, :], func=AF.Ln,
                                                              scale=1.0 / D, bias=epst[:, 0:1]),
                         reads=[B_bank[bs], B_const], writes=[B_rstd])
                    reserved.discard(bs)
                    S.op("act", lambda e: e.activation(out=rstd[:], in_=rstd[:], func=AF.Exp, scale=-0.5),
                         reads=[B_rstd], writes=[B_rstd])

                def h_step(kcs, s=s):
                    for kc in kcs:
                        S.op("dve", lambda e, kc=kc, s=s: e.scalar_tensor_tensor(
                            out=hT[:, kc, s * 512:(s + 1) * 512], in0=xT[:, kc, s * 512:(s + 1) * 512], scalar=g[:, kc:kc + 1],
                            in1=rstd[:], op0=ALU.mult, op1=ALU.mult),
                            reads=[B_xT[kc][s], B_rstd, B_const], writes=[B_hT[kc][s]])

                for a in range(4):
                    pending.append(lambda a=a, f=sq_step: f([2 * a, 2 * a + 1]))
                pending.append(rstd_step)
                for a in range(4):
                    pending.append(lambda a=a, f=h_step: f([2 * a, 2 * a + 1]))

        def ffn(which, h, g, first=False, s_outer=False, on_span_done=None):
            spans = [2 * h, 2 * h + 1]
            wg, wu, wd = w_g[which], w_u[which], w_d[which]
            silc = 0
            it = 0
            for grp in range(6):
                c0 = grp * 512
                c1 = min(c0 + 512, DFF)
                Gv, Gb = wload(wsrc(wg, 0, D, c0, c1), 8, c1 - c0)
                Uv, Ub = wload(wsrc(wu, 0, D, c0, c1), 8, c1 - c0)
                for sl, s in enumerate(spans):
                    for cl in range((c1 - c0) // 128):
                        c = grp * 4 + cl
                        bg = next_bank()
                        mm_group(PS[:, bg, :], [(Gv[:, kc, cl * 128:(cl + 1) * 128], hT[:, kc, s * 512:(s + 1) * 512],
                                                 [Gb, B_hT[kc][s]]) for kc in range(8)], B_bank[bg])
                        bu = next_bank()
                        mm_group(PS[:, bu, :], [(Uv[:, kc, cl * 128:(cl + 1) * 128], hT[:, kc, s * 512:(s + 1) * 512],
                                                 [Ub, B_hT[kc][s]]) for kc in range(8)], B_bank[bu])
                        si = silc % 2
                        silc += 1
                        S.op("act", lambda e, si=si, bg=bg: e.activation(out=sil[si], in_=PS[:, bg, :], func=AF.Silu),
                             reads=[B_bank[bg]], writes=[B_sil[si]])
                        S.op("dve", lambda e, si=si, bu=bu, c=c, sl=sl: e.tensor_tensor(
                            out=actT[:, c, sl * 512:(sl + 1) * 512], in0=PS[:, bu, :], in1=sil[si], op=ALU.mult),
                            reads=[B_bank[bu], B_sil[si]], writes=[B_act[c][sl]])
                        pump(3 if (first and it < 3) else 1)
                        it += 1

            def down_unit(cq, Dv, sl, s):
                for dcl in range(2):
                    dc = cq * 2 + dcl
                    bd = next_bank()
                    mm_group(PS[:, bd, :], [(Dv[c // 11][0][:, c % 11, dcl * 128:(dcl + 1) * 128],
                                             actT[:, c, sl * 512:(sl + 1) * 512],
                                             [Dv[c // 11][1], B_act[c][sl]]) for c in range(NCH)], B_bank[bd])
                    S.op("dve", lambda e, bd=bd, dc=dc, s=s: e.scalar_tensor_tensor(
                        out=xT[:, dc, s * 512:(s + 1) * 512], in0=PS[:, bd, :], scalar=0.5,
                        in1=xT[:, dc, s * 512:(s + 1) * 512], op0=ALU.mult, op1=ALU.add),
                        reads=[B_bank[bd], B_xT[dc][s]], writes=[B_xT[dc][s]])
                    pump(1)

            def load_D(cq):
                Dv = []
                for rh in range(2):
                    r0 = rh * 11 * 128
                    Dv.append(wload(wsrc(wd, r0, r0 + 11 * 128, cq * 256, (cq + 1) * 256), 11, 256))
                return Dv

            if s_outer:
                for sl, s in enumerate(spans):
                    for cq in range(4):
                        Dv = load_D(cq)
                        down_unit(cq, Dv, sl, s)
                    if on_span_done is not None:
                        on_span_done(s)
            else:
                for cq in range(4):
                    Dv = load_D(cq)
                    for sl, s in enumerate(spans):
                        down_unit(cq, Dv, sl, s)

        def mixer_attention(q):
            Wv, Wvb = wload(wsrc(w_in, 0, D, 1024, 1536), 8, 512)
            Wq, Wqb = wload(wsrc(w_in, 0, D, 0, 512), 8, 512)
            S.op("pool", lambda e: e.memset(qA[64:128, :], 0.0), writes=[B_zero])
            S.op("pool", lambda e: e.memset(qB[0:64, :], 0.0), writes=[B_zero])
            S.op("pool", lambda e: e.memset(kA[64:96, :], 1.0), writes=[B_zero])
            S.op("pool", lambda e: e.memset(kA[96:128, :], 0.0), writes=[B_zero])
            S.op("pool", lambda e: e.memset(kB[0:32, :], 0.0), writes=[B_zero])
            S.op("pool", lambda e: e.memset(kB[32:64, :], 1.0), writes=[B_zero])
            S.op("pool", lambda e: e.memset(Vaug[:, :, :, 1, :], 1.0), writes=[B_Vones])
            Wk, Wkb = wload(wsrc(w_in, 0, D, 512, 1024), 8, 512)
            norm([2, 3], gt[1])
            for tt in range(16):
                s = tt // 4
                bv = next_bank()
                mm_group(PS[:, bv, :], [(hT[:, kc, tt * 128:(tt + 1) * 128], Wv[:, kc, :], [Wvb, B_hT[kc][s]])
                                        for kc in range(8)], B_bank[bv])
                src = PS[:, bv, :].rearrange("p (a b c) -> p a b c", a=4, b=2)
                dst = Vaug[:, tt, :, 0:3:2, :]
                if tt % 2 == 0:
                    S.op("act", lambda e, src=src, dst=dst: e.activation(out=dst, in_=src, func=AF.Copy),
                         reads=[B_bank[bv]], writes=[B_V[tt]])
                else:
                    S.op("dve", lambda e, src=src, dst=dst: e.tensor_copy(out=dst, in_=src),
                         reads=[B_bank[bv]], writes=[B_V[tt]])
                if tt < 8:
                    pump(3)
                if tt == 7:
                    flush()
            bf_ = next_bank()
            for tt in range(16):
                s = tt // 4
                mm_group(PS[:, bf_, tt * 8:(tt + 1) * 8], [(hT[:, kc, tt * 128:(tt + 1) * 128], wf[:, kc, :],
                                                             [B_const, B_hT[kc][s]]) for kc in range(8)], B_bank[bf_])
            S.op("dve", lambda e: e.tensor_tensor(out=zt, in0=PS[:, bf_, 0:128], in1=bfb[:], op=ALU.add),
                 reads=[B_bank[bf_], B_const], writes=[B_z])
            S.op("act", lambda e: e.activation(out=zt, in_=zt, func=AF.Exp, scale=-1.0), reads=[B_z], writes=[B_z])
            S.op("act", lambda e: e.activation(out=spt, in_=zt, func=AF.Ln, bias=1.0), reads=[B_z], writes=[B_sp])
            S.op("dve", lambda e: e.memset(spx[:, 0:8], 0.0), writes=[B_spx])
            for tt in range(1, 16):
                S.op("dve", lambda e, tt=tt: e.tensor_tensor(out=spx[:, tt * 8:(tt + 1) * 8], in0=spx[:, (tt - 1) * 8:tt * 8],
                                                              in1=spt[:, (tt - 1) * 8:tt * 8], op=ALU.add),
                     reads=[B_sp, B_spx], writes=[B_spx])
            S.op("dve", lambda e: e.tensor_copy(out=sp_bf, in_=spt), reads=[B_sp], writes=[B_spbf])
            S.op("dve", lambda e: e.tensor_copy(out=spx_bf, in_=spx), reads=[B_spx], writes=[B_spxbf])
            bc = next_bank()
            S.op("pe", lambda e: e.matmul(PS[:, bc, 0:128], lhsT=triT[:], rhs=spt, start=True, stop=False),
                 reads=[B_sp, B_const], writes=[B_bank[bc]])
            S.op("pe", lambda e: e.matmul(PS[:, bc, 0:128], lhsT=onesF[:], rhs=spx, start=False, stop=True),
                 reads=[B_spx, B_const], writes=[B_bank[bc]])
            S.op("dve", lambda e: e.tensor_copy(out=Ck, in_=PS[:, bc, 0:128]), reads=[B_bank[bc]], writes=[B_Ck])
            for s in range(4):
                bn = next_bank()
                for kbl in range(4):
                    kb = 4 * s + kbl
                    S.op("pe", lambda e, kb=kb, kbl=kbl, bn=bn: e.matmul(
                        PS[0:8, bn, kbl * 128:(kbl + 1) * 128], lhsT=sp_bf[:, kb * 8:(kb + 1) * 8], rhs=negtri[:],
                        start=True, stop=False), reads=[B_spbf, B_const], writes=[B_bank[bn]])
                    S.op("pe", lambda e, kb=kb, kbl=kbl, bn=bn: e.matmul(
                        PS[0:8, bn, kbl * 128:(kbl + 1) * 128], lhsT=spx_bf[:, kb * 8:(kb + 1) * 8], rhs=negones[:],
                        start=False, stop=True), reads=[B_spxbf, B_const], writes=[B_bank[bn]])
                S.op("dve", lambda e, s=s, bn=bn: e.tensor_copy(out=qA[96:104, s * 512:(s + 1) * 512], in_=PS[0:8, bn, :]),
                     reads=[B_bank[bn], B_zero], writes=[B_qaux])
            def borrow_slot():
                i = job_i[0] % NSLOT
                B_slot[i].owner = job_i[0]
                job_i[0] += 1
                return ring[:, i, :], B_slot[i]

            r0v, r0b = borrow_slot()
            r1v, r1b = borrow_slot()
            TS = [dict(qA=qA, qB=qB, kA=kA, kB=kB, gq=[], gk=[]),
                  dict(qA=r0v[:, 0:2048], qB=r0v[:, 2048:4096], kA=r1v[:, 0:2048], kB=r1v[:, 2048:4096], gq=[r0b], gk=[r1b])]
            B_q1 = {n: [Buf("%s1_%d" % (n, s)) for s in range(4)] for n in ("qA", "qB", "kA", "kB")}
            B_aug1 = {"A": Buf("augA1"), "B": Buf("augB1")}
            B_zero1 = Buf("qkzero1")
            BQ = [B_q, B_q1]
            BAUG = [B_aug, B_aug1]
            BZ = [B_zero, B_zero1]
            t1 = TS[1]
            S.op("pool", lambda e: e.memset(t1["qA"][64:128, :], 0.0), writes=[B_zero1] + t1["gq"])
            S.op("pool", lambda e: e.memset(t1["qB"][0:64, :], 0.0), writes=[B_zero1] + t1["gq"])
            S.op("pool", lambda e: e.memset(t1["kA"][64:96, :], 1.0), writes=[B_zero1] + t1["gk"])
            S.op("pool", lambda e: e.memset(t1["kA"][96:128, :], 0.0), writes=[B_zero1] + t1["gk"])
            S.op("pool", lambda e: e.memset(t1["kB"][0:32, :], 0.0), writes=[B_zero1] + t1["gk"])
            S.op("pool", lambda e: e.memset(t1["kB"][32:64, :], 1.0), writes=[B_zero1] + t1["gk"])

            pti = [0]
            oi = [0]
            SP_ = (0, 1, 2, 3, 4, 5)

            def proj_steps(j):
                ts = j % 2
                T = TS[ts]
                Bq_ = BQ[ts]
                Bz = BZ[ts]

                def first():
                    S.op("sp", lambda e: e.dma_start(out=T["qA"][64:65, :], in_=qA[96 + 2 * j:97 + 2 * j, :]),
                         reads=[B_qaux, Bz], writes=[BAUG[ts]["A"]] + T["gq"], dma_sem="augA%d" % ts)
                    S.op("sp", lambda e: e.dma_start(out=T["qB"][63:64, :], in_=qA[97 + 2 * j:98 + 2 * j, :]),
                         reads=[B_qaux, Bz], writes=[BAUG[ts]["B"]] + T["gq"], dma_sem="augB%d" % ts)

                def grp_steps(W, Wb, s, evac):
                    sl_ = slice(s * 512, (s + 1) * 512)
                    st = {}

                    def sub(a):
                        if a == 0:
                            st["b"] = next_bank(pool=SP_)
                            reserved.add(st["b"])
                        b = st["b"]
                        for kc in (2 * a, 2 * a + 1):
                            S.op("pe", lambda e, kc=kc: e.matmul(PS[:, b, :], lhsT=W[:, kc, j * 128:(j + 1) * 128],
                                                                  rhs=hT[:, kc, sl_], start=(kc == 0), stop=(kc == 7)),
                                 reads=[Wb[0], B_hT[kc][s]], writes=[B_bank[b]])
                        if a == 3:
                            reserved.discard(b)
                            evac(b, s, sl_)
                    return [lambda a=a: sub(a) for a in range(4)]

                def q_evac(bq, s, sl_):
                    S.op("dve", lambda e: e.tensor_scalar(out=T["qA"][0:64, sl_], in0=PS[0:64, bq, :],
                                                          scalar1=0.125, scalar2=None, op0=ALU.mult),
                         reads=[B_bank[bq], Bz], writes=[Bq_["qA"][s]] + T["gq"])
                    S.op("dve", lambda e: e.tensor_scalar(out=T["qB"][64:128, sl_], in0=PS[64:128, bq, :],
                                                          scalar1=0.125, scalar2=None, op0=ALU.mult),
                         reads=[B_bank[bq], Bz], writes=[Bq_["qB"][s]] + T["gq"])

                def k_evac(bk, s, sl_):
                    S.op("dve", lambda e: e.tensor_copy(out=T["kA"][0:64, sl_], in_=PS[0:64, bk, :]),
                         reads=[B_bank[bk], Bz], writes=[Bq_["kA"][s]] + T["gk"])
                    S.op("dve", lambda e: e.tensor_copy(out=T["kB"][64:128, sl_], in_=PS[64:128, bk, :]),
                         reads=[B_bank[bk], Bz], writes=[Bq_["kB"][s]] + T["gk"])

                pending.append(first)
                for s in range(4):
                    pending.extend(grp_steps(Wq, Wqb, s, q_evac))
                    pending.extend(grp_steps(Wk, Wkb, s, k_evac))

            proj_steps(0)
            flush()
            blk_cnt = [0]
            dve_defer = []
            for j in range(4):
                ts = j % 2
                T = TS[ts]
                if j + 1 < 4:
                    proj_steps(j + 1)
                for X in ("A", "B"):
                    h = 2 * j + (0 if X == "A" else 1)
                    qt = T["q" + X]
                    kt = T["k" + X]
                    Bq = BQ[ts]["q" + X]
                    Bk = BQ[ts]["k" + X]
                    Baug = BAUG[ts][X]
                    Bz = BZ[ts]
                    guards = T["gq"] + T["gk"]
                    for qs in range(4):
                        bo = 6 + (oi[0] % 2)
                        oi[0] += 1
                        nblk = 4 * qs + 4
                        pend = []

                        def emit_S(kb, qs=qs, h=h, qt=qt, kt=kt, Bq=Bq, Bk=Bk, Baug=Baug, Bz=Bz, guards=guards):
                            diag = kb >= 4 * qs
                            q0 = kb * 128 if diag else qs * 512
                            N = (qs + 1) * 512 - q0
                            bs_ = next_bank(pool=SP_)
                            pi = pti[0] % 5
                            pti[0] += 1
                            S.op("pe", lambda e: e.matmul(PS[:, bs_, 0:N], lhsT=kt[:, kb * 128:(kb + 1) * 128],
                                                          rhs=qt[:, q0:q0 + N], start=True, stop=not diag),
                                 reads=[Bk[kb // 4], Bq[qs], Baug, Bz] + guards, writes=[B_bank[bs_]])
                            if diag:
                                S.op("pe", lambda e: e.matmul(PS[:, bs_, 0:128], lhsT=identb[:], rhs=maskb[:],
                                                              start=False, stop=True),
                                     reads=[B_const], writes=[B_bank[bs_]])
                            S.op("act", lambda e: e.activation(out=PT[pi][:, 0:N], in_=PS[:, bs_, 0:N], func=AF.Exp,
                                                               bias=Ck[:, kb * 8 + h:kb * 8 + h + 1], scale=1.0),
                                 reads=[B_bank[bs_], B_Ck], writes=[B_PT[pi]])
                            return (kb, pi, q0, N)

                        def emit_PV(info, qs=qs, j=j, X=X, bo=bo, nblk=nblk):
                            kb, pi, q0, N = info
                            lo = q0 - qs * 512
                            vl = Vaug[:, kb, j, 0:2, :] if X == "A" else Vaug[:, kb, j, 1:3, :]
                            S.op("pe", lambda e: e.matmul(PS[:, bo, lo:512], lhsT=vl, rhs=PT[pi][:, 0:N],
                                                          start=(kb == 0), stop=(kb == nblk - 1)),
                                 reads=[B_PT[pi], B_V[kb], B_Vones], writes=[B_bank[bo]])

                        for kb in range(nblk):
                            pend.append(emit_S(kb))
                            if len(pend) > 4:
                                emit_PV(pend.pop(0))
                            if dve_defer:
                                dve_defer.pop(0)()
                            blk_cnt[0] += 1
                            if blk_cnt[0] % 2 == 0:
                                pump(1)
                        while pend:
                            emit_PV(pend.pop(0))
                        while dve_defer:
                            dve_defer.pop(0)()
                        sl_ = slice(qs * 512, (qs + 1) * 512)
                        if X == "A":
                            orow, drow = slice(0, 64), slice(64, 128)
                        else:
                            orow, drow = slice(64, 128), slice(0, 64)
                        if qs == 0:
                            dve_defer.append(lambda bo=bo, drow=drow: S.op(
                                "act", lambda e: e.activation(out=rc2[drow, :], in_=PS[drow, bo, :], func=AF.Ln),
                                reads=[B_bank[bo]], writes=[B_rc2]))
                            dve_defer.append(lambda drow=drow: S.op(
                                "act", lambda e: e.activation(out=rc2[drow, :], in_=rc2[drow, :], func=AF.Exp, scale=-1.0),
                                reads=[B_rc2], writes=[B_rc2]))
                            dve_defer.append(lambda orow=orow, drow=drow: S.op(
                                "dve", lambda e: e.tensor_copy(out=rc2[orow, :], in_=rc2[drow, :]),
                                reads=[B_rc2], writes=[B_rc2]))
                            dve_defer.append(lambda bo=bo, sl_=sl_, j=j, orow=orow, qs=qs: S.op(
                                "dve", lambda e: e.tensor_tensor(out=attnT[orow, j, sl_], in0=PS[orow, bo, :],
                                                                  in1=rc2[orow, :], op=ALU.mult),
                                reads=[B_bank[bo], B_rc2], writes=[B_attn[j][qs]]))
                        else:
                            for cp in range(4):
                                dve_defer.append(lambda bo=bo, orow=orow, drow=drow, cp=cp: S.op(
                                    "dve", lambda e: e.reciprocal(out=rc[orow, cp * 128:(cp + 1) * 128],
                                                                  in_=PS[drow, bo, cp * 128:(cp + 1) * 128]),
                                    reads=[B_bank[bo]], writes=[B_rc]))
                            dve_defer.append(lambda bo=bo, sl_=sl_, j=j, orow=orow, qs=qs: S.op(
                                "dve", lambda e: e.tensor_tensor(out=attnT[orow, j, sl_], in0=PS[orow, bo, :], in1=rc[orow, :],
                                                                  op=ALU.mult),
                                reads=[B_bank[bo], B_rc], writes=[B_attn[j][qs]]))
                while dve_defer:
                    dve_defer.pop(0)()
                flush()

        def mixer_post(q, hf):
            spans = [2 * hf, 2 * hf + 1]
            cv_src = w_in[:, 1544:3080].rearrange("(k p) (t j c) -> p k t j c", p=128, t=3, j=4)
            for jc in range(4):
                Wcv, Bcv = wload_multi(3072, [
                    (lambda f, t=t: f.rearrange("p (k t c) -> p k t c", k=8, t=3)[:, :, t, :], cv_src[:, :, t, jc, :])
                    for t in range(3)])
                Wcv = Wcv.rearrange("p (k t c) -> p k t c", k=8, t=3)
                for sl, s in enumerate(spans):
                    sl_ = slice(s * 512, (s + 1) * 512)
                    bx = next_bank()
                    mm_group(PS[:, bx, :], [(Wcv[:, kc, 2, :], hT[:, kc, sl_], [Bcv, B_hT[kc][s]]) for kc in range(8)], B_bank[bx])
                    bcc = next_bank()
                    mm_group(PS[:, bcc, :], [(Wcv[:, kc, 1, :], hT[:, kc, sl_], [Bcv, B_hT[kc][s]]) for kc in range(8)], B_bank[bcc])
                    bcb = next_bank()
                    mm_group(PS[:, bcb, :], [(Wcv[:, kc, 0, :], hT[:, kc, sl_], [Bcv, B_hT[kc][s]]) for kc in range(8)], B_bank[bcb])
                    u = ubuf[jc]
                    if s == 0:
                        S.op("dve", lambda e, u=u: e.memset(u[:, 0:2], 0.0), writes=[B_u[jc]])
                    S.op("act", lambda e, bx=bx: e.activation(out=cxs, in_=PS[:, bx, :], func=AF.Copy),
                         reads=[B_bank[bx]], writes=[B_cxs])
                    S.op("dve", lambda e, u=u, bcc=bcc: e.tensor_tensor(out=u[:, 2:514], in0=PS[:, bcc, :], in1=cxs, op=ALU.mult),
                         reads=[B_bank[bcc], B_cxs], writes=[B_u[jc]])
                    S.op("dve", lambda e, u=u, jc=jc: e.tensor_scalar(out=tconv, in0=u[:, 0:512], scalar1=cwt[:, jc * 3:jc * 3 + 1],
                                                                       scalar2=None, op0=ALU.mult),
                         reads=[B_u[jc], B_const], writes=[B_tconv])
                    S.op("dve", lambda e, u=u, jc=jc: e.scalar_tensor_tensor(
                        out=tconv, in0=u[:, 1:513], scalar=cwt[:, jc * 3 + 1:jc * 3 + 2], in1=tconv, op0=ALU.mult, op1=ALU.add),
                        reads=[B_u[jc], B_tconv, B_const], writes=[B_tconv])
                    S.op("dve", lambda e, u=u, jc=jc: e.scalar_tensor_tensor(
                        out=tconv, in0=u[:, 2:514], scalar=cwt[:, jc * 3 + 2:jc * 3 + 3], in1=tconv, op0=ALU.mult, op1=ALU.add),
                        reads=[B_u[jc], B_tconv, B_const], writes=[B_tconv])
                    S.op("dve", lambda e, bcb=bcb, jc=jc, sl=sl: e.tensor_tensor(
                        out=convin[:, jc, sl * 512:(sl + 1) * 512], in0=PS[:, bcb, :], in1=tconv, op=ALU.mult),
                        reads=[B_bank[bcb], B_tconv], writes=[B_convin[jc][sl]])
                    S.op("dve", lambda e, u=u: e.tensor_copy(out=u[:, 0:2], in_=u[:, 512:514]),
                         reads=[B_u[jc]], writes=[B_u[jc]])
                    pump(1)
            for mq in range(4):
                c0 = mq * 256
                gsrc = lambda base: w_in[:, base + c0:base + c0 + 256].rearrange("(k p) c -> p k c", p=128)
                Wgg, Bgg = wload_multi(4096, [
                    (lambda f: f[:, 0:2048].rearrange("p (k c) -> p k c", k=8), gsrc(3080)),
                    (lambda f: f[:, 2048:4096].rearrange("p (k c) -> p k c", k=8), gsrc(4104))])
                Wga = Wgg[:, 0:2048].rearrange("p (k c) -> p k c", k=8)
                Wgc = Wgg[:, 2048:4096].rearrange("p (k c) -> p k c", k=8)
                Bga = Bgc = Bgg
                Woo, Boca = wload_multi(2048, [
                    (lambda f: f[:, 0:1024].rearrange("p (k c) -> p k c", k=4),
                     w_oc[:, c0:c0 + 256].rearrange("(k p) c -> p k c", p=128)),
                    (lambda f: f[:, 1024:2048].rearrange("p (k c) -> p k c", k=4),
                     w_oa[:, c0:c0 + 256].rearrange("(k p) c -> p k c", p=128))])
                Woca = Woo.rearrange("p (k c) -> p k c", k=8)
                for cl in range(2):
                    c = 2 * mq + cl
                    for sl, s in enumerate(spans):
                        sl_ = slice(s * 512, (s + 1) * 512)
                        ll = slice(sl * 512, (sl + 1) * 512)
                        b_ga = next_bank()
                        mm_group(PS[:, b_ga, :], [(Wga[:, kc, cl * 128:(cl + 1) * 128], hT[:, kc, sl_], [Bga, B_hT[kc][s]])
                                                  for kc in range(8)], B_bank[b_ga])
                        b_gc = next_bank()
                        mm_group(PS[:, b_gc, :], [(Wgc[:, kc, cl * 128:(cl + 1) * 128], hT[:, kc, sl_], [Bgc, B_hT[kc][s]])
                                                  for kc in range(8)], B_bank[b_gc])
                        b_ya = next_bank()
                        mm_group(PS[:, b_ya, :], [(Woca[:, 4 + kc, cl * 128:(cl + 1) * 128], attnT[:, kc, sl_], [Boca, B_attn[kc][s]])
                                                  for kc in range(4)], B_bank[b_ya])
                        b_yc = next_bank()
                        mm_group(PS[:, b_yc, :], [(Woca[:, kc, cl * 128:(cl + 1) * 128], convin[:, kc, ll], [Boca, B_convin[kc][sl]])
                                                  for kc in range(4)], B_bank[b_yc])
                        S.op("act", lambda e, b=b_ga: e.activation(out=sa_t, in_=PS[:, b, :], func=AF.Sigmoid),
                             reads=[B_bank[b_ga]], writes=[B_sa])
                        S.op("act", lambda e, b=b_gc: e.activation(out=sc_t, in_=PS[:, b, :], func=AF.Sigmoid),
                             reads=[B_bank[b_gc]], writes=[B_sc])
                        S.op("dve", lambda e, b=b_ya: e.tensor_tensor(out=m1_t, in0=PS[:, b, :], in1=sa_t, op=ALU.mult),
                             reads=[B_bank[b_ya], B_sa], writes=[B_m1])
                        S.op("dve", lambda e, b=b_yc: e.tensor_tensor(out=m2_t, in0=PS[:, b, :], in1=sc_t, op=ALU.mult),
                             reads=[B_bank[b_yc], B_sc], writes=[B_m2])
                        dbg = os.environ.get("MK_DBG", "")
                        if dbg == "noattn":
                            S.op("dve", lambda e, c=c, ll=ll: e.tensor_copy(out=merged[:, c, ll], in_=m2_t),
                                 reads=[B_m1, B_m2], writes=[B_merged[c][sl]])
                        elif dbg == "noconv":
                            S.op("dve", lambda e, c=c, ll=ll: e.tensor_copy(out=merged[:, c, ll], in_=m1_t),
                                 reads=[B_m1, B_m2], writes=[B_merged[c][sl]])
                        else:
                            S.op("dve", lambda e, c=c, ll=ll: e.tensor_tensor(out=merged[:, c, ll], in0=m1_t, in1=m2_t, op=ALU.add),
                                 reads=[B_m1, B_m2], writes=[B_merged[c][sl]])
                        pump(1)
            for ch in range(2):
                Wo, Bo = wload(wsrc(w_out, 0, D, ch * 512, (ch + 1) * 512), 8, 512)
                for dcl in range(4):
                    dc = 4 * ch + dcl
                    for sl, s in enumerate(spans):
                        sl_ = slice(s * 512, (s + 1) * 512)
                        ll = slice(sl * 512, (sl + 1) * 512)
                        bd = next_bank()
                        mm_group(PS[:, bd, :], [(Wo[:, kc, dcl * 128:(dcl + 1) * 128], merged[:, kc, ll], [Bo, B_merged[kc][sl]])
                                                for kc in range(8)], B_bank[bd])
                        S.op("dve", lambda e, bd=bd, dc=dc, sl_=sl_: e.tensor_tensor(
                            out=xT[:, dc, sl_], in0=PS[:, bd, :], in1=xT[:, dc, sl_], op=ALU.add),
                            reads=[B_bank[bd], B_xT[dc][s]], writes=[B_xT[dc][s]])

        order = ["X", "F1", "ATT", "POST", "F2"]
        lim = order.index(stop_after) if stop_after else len(order) - 1
        tail_done = [False]
        for q in range(nseq):
            if q == 0:
                x_steps(0, range(0, 4))
                if lim >= 1:
                    norm([0], gt[0])
                x_steps(0, range(4, 8))
                flush()
                if lim >= 1:
                    norm([1], gt[0])
            else:
                flush()
                final_steps(q - 1, range(8, 16))
            x_steps(q, range(8, 16))
            if lim >= 1:
                norm([2, 3], gt[0])
                ffn(0, 0, gt[0], first=(q == 0))
                flush()
                if lim >= 2:
                    norm([0, 1], gt[1])
                ffn(0, 1, gt[0])
            flush()
            if lim >= 2:
                S.barrier(["stg0", "stg1", "stg2", "stg3"])
                mixer_attention(q)
            if lim >= 3:
                S.barrier(["augA0", "augA1", "augB0", "augB1"])
                mixer_post(q, 0)
                if lim >= 4:
                    norm([0, 1], gt[2])
                mixer_post(q, 1)
                flush()
            if lim >= 4:
                S.barrier()
                norm([2, 3], gt[2])
                ffn(1, 0, gt[2])
                flush()
                final_steps(q, range(0, 8))
                if q + 1 < nseq:
                    x_steps(q + 1, range(0, 8))
                    norm([0, 1], gt[0])
                if q + 1 < nseq:
                    ffn(1, 1, gt[2])
                else:
                    ffn(1, 1, gt[2], s_outer=True,
                        on_span_done=lambda s_, q=q: final_steps_single(q, range(4 * s_, 4 * s_ + 4)))
                    tail_done[0] = True
            else:
                final_steps(q, range(0, 8))
                if q + 1 < nseq:
                    x_steps(q + 1, range(0, 8))
                    if lim >= 1:
                        norm([0, 1], gt[0])
        flush()
        if not tail_done[0]:
            final_steps_single(nseq - 1, range(8, 16))
            flush()
        S.op("sp", lambda e: e.nop(), extra_deps=[S.dma_last[k] for k in ("stg0", "stg1", "stg2", "stg3") if k in S.dma_last])
        S.emit(nc, sems, dsems)
    return nc


_NC_CACHE = {}


def _prep_inputs(inputs, n_cores=8):
    f = lambda a: np.ascontiguousarray(np.asarray(a, dtype=np.float32))
    x = f(inputs["x"])
    per = x.shape[0] // n_cores
    pk = lambda g: np.ascontiguousarray(f(g).reshape(8, 128).T)
    shared = {
        "g1": pk(inputs["ffn1_norm"]), "gm": pk(inputs["mix_norm"]), "g2": pk(inputs["ffn2_norm"]),
        "gfin": np.ascontiguousarray(np.broadcast_to(f(inputs["final_norm"])[None, :], (128, D))),
        "bfb": np.ascontiguousarray(np.broadcast_to(np.tile(f(inputs["b_forget"]), 16)[None, :], (128, 128))),
        "cw": np.ascontiguousarray(f(inputs["conv_w"]).reshape(3, 4, 128).transpose(2, 1, 0).reshape(128, 12)),
    }
    for k in ("ffn1_gate", "ffn1_up", "ffn1_down", "ffn2_gate", "ffn2_up", "ffn2_down", "w_in", "w_o_attn",
              "w_o_conv", "w_out"):
        shared[k] = f(inputs[k])
    in_maps = []
    for c in range(n_cores):
        m = dict(shared)
        m["x"] = np.ascontiguousarray(x[c * per:(c + 1) * per])
        in_maps.append(m)
    return in_maps


def kernel(**inputs):
    n_cores = 8
    stop = os.environ.get("MK_STOP") or None
    key = stop
    if key not in _NC_CACHE:
        _NC_CACHE[key] = build_nc(stop_after=stop)
    nc = _NC_CACHE[key]
    in_maps = _prep_inputs(inputs, n_cores)
    res = run_bass_kernel_spmd(nc, in_maps, core_ids=list(range(n_cores)))
    return np.concatenate([np.asarray(r["out"]) for r in res.results], axis=0).astype(np.float32)
```

```python
import os
from contextlib import ExitStack
import numpy as np
import concourse.bass as bass
import concourse.mybir as mybir
from concourse.bass_utils import run_bass_kernel_spmd

F32 = mybir.dt.float32
BF16 = mybir.dt.bfloat16
AF = mybir.ActivationFunctionType
ALU = mybir.AluOpType

ENGINES = ("pe", "act", "dve", "pool", "sp")
NSEQ = 2
SEQ = 2048
D = 1024
DFF = 2816
NCH = 22
NSLOT = 4
EPS = 1e-6
MASKVAL = -30000.0


class Buf:
    __slots__ = ("name", "last_w", "reads", "owner")

    def __init__(self, name):
        self.name = name
        self.last_w = None
        self.reads = {}
        self.owner = None


class Op:
    __slots__ = ("eng", "idx", "fn", "waits", "signal", "sigval", "dma_sem", "is_dma", "order")

    def __init__(self, eng, idx, fn, dma_sem=None):
        self.eng = eng
        self.idx = idx
        self.fn = fn
        self.waits = []
        self.signal = False
        self.sigval = None
        self.is_dma = dma_sem is not None
        self.dma_sem = dma_sem
        self.order = idx


class Sched:
    def __init__(self):
        self.prog = {e: [] for e in ENGINES}
        self.waited = {e: {} for e in ENGINES}
        self.dma_count = {}
        self.dma_last = {}
        self.same_engine_sync = {"act", "dve", "pool"}
        self.barrier_deps = []

    def op(self, eng, fn, reads=(), writes=(), dma_sem=None, extra_deps=(), exempt=False):
        o = Op(eng, len(self.prog[eng]), fn, dma_sem=dma_sem)
        if o.is_dma:
            self.dma_count[dma_sem] = self.dma_count.get(dma_sem, 0) + 1
            o.order = self.dma_count[dma_sem]
            self.dma_last[dma_sem] = o
        deps = []
        for b in reads:
            if b.last_w is not None:
                deps.append(b.last_w)
        for b in writes:
            if b.last_w is not None:
                deps.append(b.last_w)
            deps.extend(b.reads.values())
        deps.extend(extra_deps)
        if not exempt:
            deps.extend(self.barrier_deps)
        w = self.waited[eng]
        for d in deps:
            if d is o:
                continue
            if (not d.is_dma) and d.eng == eng and eng not in self.same_engine_sync:
                continue
            key = ("dma", d.dma_sem) if d.is_dma else ("eng", d.eng)
            if w.get(key, -1) >= d.order:
                continue
            w[key] = d.order
            d.signal = True
            o.waits.append(d)
        self.prog[eng].append(o)
        rkey = ("dma", dma_sem) if o.is_dma else eng
        for b in reads:
            b.reads[rkey] = o
        for b in writes:
            b.last_w = o
            b.reads = {}
        return o

    def barrier(self, include_dma_keys=()):
        deps = []
        for e in ("pe", "act", "dve", "sp"):
            for o in reversed(self.prog[e]):
                if not o.is_dma:
                    deps.append(o)
                    break
        for k in include_dma_keys:
            if k in self.dma_last:
                deps.append(self.dma_last[k])
        self.barrier_deps = deps

    def emit(self, nc, sems, dma_sems):
        for e in ENGINES:
            c = 0
            for o in self.prog[e]:
                if o.is_dma:
                    o.sigval = 16 * o.order
                elif o.signal:
                    c += 1
                    o.sigval = c

        def run(engname, eobj):
            for o in self.prog[engname]:
                for d in o.waits:
                    s = dma_sems[d.dma_sem] if d.is_dma else sems[d.eng]
                    eobj.wait_ge(s, d.sigval)
                ins = o.fn(eobj)
                if o.is_dma:
                    ins.then_inc(dma_sems[o.dma_sem], 16)
                elif o.signal:
                    ins.then_inc(sems[o.eng], 1)

        with nc.Block() as block:
            @block.tensor
            def _(e):
                run("pe", e)

            @block.scalar
            def _(e):
                run("act", e)

            @block.vector
            def _(e):
                run("dve", e)

            @block.gpsimd
            def _(e):
                run("pool", e)

            @block.sync
            def _(e):
                run("sp", e)


def build_nc(stop_after=None, nseq=NSEQ):
    nc = bass.Bass("TRN2", target_bir_lowering=False)

    def din(name, shape):
        return nc.dram_tensor(name, list(shape), F32, kind="ExternalInput").ap()

    x = din("x", [nseq, SEQ, D])
    g1_d = din("g1", [128, 8])
    gm_d = din("gm", [128, 8])
    g2_d = din("g2", [128, 8])
    gfin_d = din("gfin", [128, D])
    bfb_d = din("bfb", [128, 128])
    cw_d = din("cw", [128, 12])
    w_g = [din("ffn1_gate", [D, DFF]), din("ffn2_gate", [D, DFF])]
    w_u = [din("ffn1_up", [D, DFF]), din("ffn2_up", [D, DFF])]
    w_d = [din("ffn1_down", [DFF, D]), din("ffn2_down", [DFF, D])]
    w_in = din("w_in", [D, 5128])
    w_oa = din("w_o_attn", [512, D])
    w_oc = din("w_o_conv", [512, D])
    w_out = din("w_out", [D, D])
    out = nc.dram_tensor("out", [nseq, SEQ, D], F32, kind="ExternalOutput").ap()

    S = Sched()
    es = ExitStack()
    with es:
        def sb(name, shape, dt):
            return es.enter_context(nc.sbuf_tensor(name, shape, dt))

        xT = sb("xT", [128, 8, SEQ], F32)
        hT = sb("hT", [128, 8, SEQ], BF16)
        ring = sb("ring", [128, NSLOT, 4096], BF16)
        O = sb("O", [128, 24576], BF16)
        attnT = sb("attnT", [128, 4, SEQ], BF16)
        ident = sb("ident", [128, 128], F32)
        triT = sb("triT", [128, 128], F32)
        onesF = sb("onesF", [128, 128], F32)
        identb = sb("identb", [128, 128], BF16)
        maskb = sb("maskb", [128, 128], BF16)
        negtri = sb("negtri", [128, 128], BF16)
        negones = sb("negones", [128, 128], BF16)
        onesb = sb("onesb", [128, 128], BF16)
        gt = [sb("g1t", [128, 8], F32), sb("gmt", [128, 8], F32), sb("g2t", [128, 8], F32)]
        cwt = sb("cwt", [128, 12], F32)
        bfb = sb("bfbt", [128, 128], F32)
        wf = sb("wf", [128, 8, 8], BF16)
        epst = sb("epst", [128, 1], F32)
        sst = sb("sst", [128, 8], F32)
        sqb = [sb("sqb0", [128, 512], BF16), sb("sqb1", [128, 512], BF16)]
        rstd = sb("rstd", [128, 512], F32)
        gfin = sb("gfin_t", [128, D], F32)
        PTx = sb("PTx", [128, 2, 512], BF16)
        rc2 = sb("rc2", [128, 512], F32)
        PS = es.enter_context(nc.psum_tensor("PS", [128, 8, 512], F32))

        sems = {e: es.enter_context(nc.semaphore("s_" + e)) for e in ENGINES}
        dma_keys = ["slot%d" % i for i in range(NSLOT)] + ["stg0", "stg1", "stg2", "stg3", "augA0", "augA1", "augB0", "augB1", "cst", "cstp", "gfin"]
        dsems = {k: es.enter_context(nc.semaphore("d_" + k)) for k in dma_keys}

        def Obf(a, n):
            return O[:, a:a + n]

        def Of32(a, n):
            return O[:, a:a + 2 * n].bitcast(F32)

        actT = O[:, 0:22528].rearrange("p (c t) -> p c t", c=22)
        sil = [Of32(22528, 512), Of32(23552, 512)]
        Vaug = O[:, 0:12288].rearrange("p (t j k d) -> p t j k d", t=16, j=4, k=3)
        qA = Obf(12288, 2048)
        qB = Obf(14336, 2048)
        kA = Obf(16384, 2048)
        kB = Obf(18432, 2048)
        PT = [Obf(20480 + 512 * i, 512) for i in range(3)] + [PTx[:, i, :] for i in range(2)]
        zt = Of32(22016, 128)
        spt = Of32(22272, 128)
        spx = Of32(22528, 128)
        sp_bf = Obf(22784, 128)
        spx_bf = Obf(22912, 128)
        Ck = Of32(23040, 128)
        rc = Of32(23296, 512)
        convin = O[:, 0:4096].rearrange("p (c t) -> p c t", c=4)
        merged = O[:, 4096:12288].rearrange("p (c t) -> p c t", c=8)
        ubuf = [Of32(12288 + 1028 * i, 514) for i in range(4)]
        cxs = Of32(16400, 512)
        tconv = Of32(17424, 512)
        sa_t = Of32(18448, 512)
        sc_t = Of32(19472, 512)
        m1_t = Of32(20496, 512)
        m2_t = Of32(21520, 512)

        B_xT = [[Buf("xT%d_%d" % (k, s)) for s in range(4)] for k in range(8)]
        B_hT = [[Buf("hT%d_%d" % (k, s)) for s in range(4)] for k in range(8)]
        B_slot = [Buf("slot%d" % i) for i in range(NSLOT)]
        B_bank = [Buf("bank%d" % i) for i in range(8)]
        B_act = [[Buf("act%d_%d" % (c, s)) for s in range(2)] for c in range(NCH)]
        B_sil = [Buf("sil0"), Buf("sil1")]
        B_stg = [Buf("stg%d" % i) for i in range(4)]
        stg = [attnT[:, i, :].bitcast(F32) for i in range(4)]
        stg_i = [0]

        def next_stg():
            i = stg_i[0] % 4
            stg_i[0] += 1
            return i
        B_ssf = [Buf("ssf0"), Buf("ssf1")]
        B_gfin = Buf("gfin")
        B_V = [Buf("V%d" % t) for t in range(16)]
        B_Vones = Buf("Vones")
        B_q = {n: [Buf("%s_%d" % (n, s)) for s in range(4)] for n in ("qA", "qB", "kA", "kB")}
        B_aug = {"A": Buf("augA"), "B": Buf("augB")}
        B_qaux = Buf("qaux")
        B_zero = Buf("qkzero")
        B_PT = [Buf("PT%d" % i) for i in range(6)]
        B_z, B_sp, B_spx, B_spbf, B_spxbf, B_Ck, B_rc = (Buf(n) for n in ("z", "sp", "spx", "spbf", "spxbf", "Ck", "rc"))
        B_rc2 = Buf("rc2")
        B_attn = [[Buf("attn%d_%d" % (j, s)) for s in range(4)] for j in range(4)]
        B_convin = [[Buf("cvi%d_%d" % (j, s)) for s in range(2)] for j in range(4)]
        B_merged = [[Buf("mg%d_%d" % (c, s)) for s in range(2)] for c in range(8)]
        B_u = [Buf("u%d" % j) for j in range(4)]
        B_cxs, B_tconv, B_sa, B_sc, B_m1, B_m2 = (Buf(n) for n in ("cxs", "tconv", "sa", "sc", "m1", "m2"))
        B_sq = [Buf("sq0"), Buf("sq1")]
        B_rstd = Buf("rstd")
        B_ss = Buf("ss")
        B_const = Buf("const")

        bank_i = [0]
        reserved = set()

        def next_bank(exclude=None, pool=None):
            while True:
                b = bank_i[0]
                bank_i[0] = (b + 1) % 8
                if b == exclude or b in reserved:
                    continue
                if pool is not None and b not in pool:
                    continue
                return b

        def next_pair():
            while True:
                if bank_i[0] % 2:
                    bank_i[0] = (bank_i[0] + 1) % 8
                b = bank_i[0]
                bank_i[0] = (b + 2) % 8
                if b in reserved or (b + 1) in reserved:
                    continue
                return b

        pending = []

        def pump(n=1):
            for _ in range(n):
                if pending:
                    pending.pop(0)()

        def flush():
            while pending:
                pending.pop(0)()

        job_i = [0]

        def wload(src, k, c):
            i = job_i[0] % NSLOT
            view = ring[:, i, 0:k * c].rearrange("p (k c) -> p k c", k=k)
            S.op("pool", lambda e: e.dma_start(out=view, in_=src), writes=[B_slot[i]],
                 dma_sem="slot%d" % i, exempt=True)
            B_slot[i].owner = job_i[0]
            tok = (B_slot[i], job_i[0])
            job_i[0] += 1
            return view, tok

        def wload_multi(nelem, parts):
            i = job_i[0] % NSLOT
            flat = ring[:, i, 0:nelem]
            prev = None
            for k, (dstf, src) in enumerate(parts):
                dst = dstf(flat)
                if k == 0:
                    prev = S.op("pool", lambda e, dst=dst, src=src: e.dma_start(out=dst, in_=src), writes=[B_slot[i]],
                                dma_sem="slot%d" % i, exempt=True)
                else:
                    prev = S.op("pool", lambda e, dst=dst, src=src: e.dma_start(out=dst, in_=src),
                                dma_sem="slot%d" % i, exempt=True)
                    B_slot[i].last_w = prev
            B_slot[i].owner = job_i[0]
            tok = (B_slot[i], job_i[0])
            job_i[0] += 1
            return flat, tok

        def wload2(src_a, src_b):
            flat, tok = wload_multi(4096, [
                (lambda f: f.rearrange("p (k c) -> p k c", k=8)[:, 0:4, :], src_a),
                (lambda f: f.rearrange("p (k c) -> p k c", k=8)[:, 4:8, :], src_b)])
            return flat.rearrange("p (k c) -> p k c", k=8), tok

        def wsrc(w, r0, r1, c0, c1):
            return w[r0:r1, c0:c1].rearrange("(k p) c -> p k c", p=128)

        def mm_group(out_ap, pairs, bankbuf):
            n = len(pairs)
            for i, (l, r, rd0) in enumerate(pairs):
                rd = []
                for b in rd0:
                    if isinstance(b, tuple):
                        assert b[0].owner == b[1], "weight slot reused while still live: %s" % b[0].name
                        b = b[0]
                    rd.append(b)
                S.op("pe", lambda e, l=l, r=r, i=i: e.matmul(out_ap, lhsT=l, rhs=r, start=(i == 0), stop=(i == n - 1)),
                     reads=rd, writes=[bankbuf])

        cst_ops = []
        for dst, src in ((gt[0], g1_d), (gt[1], gm_d), (gt[2], g2_d), (cwt, cw_d), (bfb, bfb_d)):
            S.op("sp", lambda e, dst=dst, src=src: e.dma_start(out=dst[:], in_=src), writes=[B_const], dma_sem="cst")
        S.op("pool", lambda e: e.dma_start(out=wf[:], in_=w_in[:, 1536:1544].rearrange("(k p) c -> p k c", p=128)),
             writes=[B_const], dma_sem="cstp")
        S.op("sp", lambda e: e.dma_start(out=gfin[:], in_=gfin_d), writes=[B_gfin], dma_sem="gfin")
        S.op("pool", lambda e: e.memset(ident[:], 1.0), writes=[B_const])
        S.op("pool", lambda e: e.affine_select(out=ident[:], in_=ident[:], pattern=[[-1, 128]], compare_op=ALU.is_equal,
                                               fill=0.0, base=0, channel_multiplier=1), reads=[B_const], writes=[B_const])
        S.op("pool", lambda e: e.memset(onesF[:], 1.0), writes=[B_const])
        S.op("pool", lambda e: e.memset(onesb[:], 1.0), writes=[B_const])
        S.op("pool", lambda e: e.memset(negones[:], -1.0), writes=[B_const])
        S.op("pool", lambda e: e.memset(epst[:], EPS), writes=[B_const])
        S.op("pool", lambda e: e.affine_select(out=triT[:], in_=onesF[:], pattern=[[1, 128]], compare_op=ALU.is_ge,
                                               fill=0.0, base=0, channel_multiplier=-1), reads=[B_const], writes=[B_const])
        S.op("dve", lambda e: e.tensor_scalar(out=negtri[:], in0=triT[:], scalar1=-1.0, scalar2=None, op0=ALU.mult),
             reads=[B_const], writes=[B_const])
        S.op("dve", lambda e: e.tensor_scalar(out=maskb[:], in0=triT[:], scalar1=-MASKVAL, scalar2=MASKVAL,
                                              op0=ALU.mult, op1=ALU.add), reads=[B_const], writes=[B_const])
        S.op("dve", lambda e: e.tensor_copy(out=identb[:], in_=ident[:]), reads=[B_const], writes=[B_const])

        def x_steps(q, tts):
            st = {"tr": [], "ev": None}

            def step(tt):
                if st["ev"] is not None:
                    st["ev"]()
                    st["ev"] = None
                if len(st["tr"]) >= 2 or (tt is None and st["tr"]):
                    st["ev"] = st["tr"].pop(0)()
                if tt is None:
                    return
                si = next_stg()
                S.op("sp", lambda e: e.dma_start(out=stg[si], in_=x[q, tt * 128:(tt + 1) * 128, :]),
                     writes=[B_stg[si]], dma_sem="stg%d" % si)

                def tr():
                    bp = next_pair()
                    reserved.add(bp)
                    reserved.add(bp + 1)
                    for kc in range(8):
                        S.op("pe", lambda e, kc=kc: e.transpose(
                            out=PS[:, bp + kc // 4, (kc % 4) * 128:(kc % 4 + 1) * 128],
                            in_=stg[si][:, kc * 128:(kc + 1) * 128], identity=ident[:]),
                            reads=[B_stg[si], B_const], writes=[B_bank[bp + kc // 4]])

                    def ev():
                        s_ = tt // 4
                        S.op("act", lambda e: e.activation(
                            out=xT[:, 0:4, tt * 128:(tt + 1) * 128], in_=PS[:, bp, :].rearrange("p (k t) -> p k t", k=4),
                            func=AF.Copy), reads=[B_bank[bp]], writes=[B_xT[k][s_] for k in range(4)])
                        S.op("dve", lambda e: e.tensor_copy(
                            out=xT[:, 4:8, tt * 128:(tt + 1) * 128], in_=PS[:, bp + 1, :].rearrange("p (k t) -> p k t", k=4)),
                            reads=[B_bank[bp + 1]], writes=[B_xT[k][s_] for k in range(4, 8)])
                        reserved.discard(bp)
                        reserved.discard(bp + 1)
                    return ev
                st["tr"].append(tr)

            for tt in tts:
                pending.append(lambda tt=tt: step(tt))
            for _ in range(3):
                pending.append(lambda: step(None))

        def final_steps(q, tts):
            tts = list(tts)
            assert len(tts) == 8
            st = {"post": None}

            def transposes(tt):
                s_ = tt // 4
                bp = next_pair()
                reserved.add(bp)
                reserved.add(bp + 1)
                for kc in range(8):
                    S.op("pe", lambda e, kc=kc: e.transpose(
                        out=PS[:, bp + kc // 4, (kc % 4) * 128:(kc % 4 + 1) * 128],
                        in_=xT[:, kc, tt * 128:(tt + 1) * 128], identity=ident[:]),
                        reads=[B_xT[kc][s_], B_const], writes=[B_bank[bp + kc // 4]])
                return bp

            def step1(tt, i):
                if st["post"] is not None:
                    st["post"]()
                    st["post"] = None
                if tt is None:
                    return
                bp = transposes(tt)

                def post():
                    pv = PS[:, bp:bp + 2, :]
                    S.op("act", lambda e: e.activation(out=pv, in_=pv, func=AF.Square, accum_out=sst[:, i:i + 1]),
                         reads=[B_bank[bp], B_bank[bp + 1]], writes=[B_bank[bp], B_bank[bp + 1], B_ssf[0]])
                    reserved.discard(bp)
                    reserved.discard(bp + 1)
                st["post"] = post

            def rstd_step():
                S.op("act", lambda e: e.activation(out=sst[:, 0:8], in_=sst[:, 0:8], func=AF.Sqrt, scale=1.0 / D,
                                                   bias=epst[:, 0:1]), reads=[B_ssf[0], B_const], writes=[B_ssf[0]])
                S.op("dve", lambda e: e.reciprocal(out=sst[:, 0:8], in_=sst[:, 0:8]), reads=[B_ssf[0]], writes=[B_ssf[0]])

            def step2(tt, i):
                if st["post"] is not None:
                    st["post"]()
                    st["post"] = None
                if tt is None:
                    return
                bp = transposes(tt)

                def post():
                    pv = PS[:, bp:bp + 2, :]
                    si = next_stg()
                    ov = stg[si].rearrange("p (a b) -> p a b", a=2)
                    S.op("dve", lambda e: e.scalar_tensor_tensor(
                        out=ov, in0=pv, scalar=sst[:, i:i + 1], in1=gfin[:].rearrange("p (a b) -> p a b", a=2),
                        op0=ALU.mult, op1=ALU.mult),
                        reads=[B_bank[bp], B_bank[bp + 1], B_ssf[0], B_gfin], writes=[B_stg[si]])
                    S.op("sp", lambda e: e.dma_start(out=out[q, tt * 128:(tt + 1) * 128, :], in_=stg[si]),
                         reads=[B_stg[si]], dma_sem="stg%d" % si)
                    reserved.discard(bp)
                    reserved.discard(bp + 1)
                st["post"] = post

            for i, tt in enumerate(tts):
                pending.append(lambda tt=tt, i=i: step1(tt, i))
            pending.append(lambda: step1(None, 0))
            pending.append(rstd_step)
            for i, tt in enumerate(tts):
                pending.append(lambda tt=tt, i=i: step2(tt, i))
            pending.append(lambda: step2(None, 0))

        def final_steps_single(q, tts):
            for n, tt in enumerate(tts):
                def step(tt=tt, n=n):
                    s_ = tt // 4
                    i = n % 2
                    c0 = 4 * i
                    bp = next_pair()
                    for kc in range(8):
                        S.op("pe", lambda e, kc=kc: e.transpose(
                            out=PS[:, bp + kc // 4, (kc % 4) * 128:(kc % 4 + 1) * 128],
                            in_=xT[:, kc, tt * 128:(tt + 1) * 128], identity=ident[:]),
                            reads=[B_xT[kc][s_], B_const], writes=[B_bank[bp + kc // 4]])
                    pv = PS[:, bp:bp + 2, :]
                    si = next_stg()
                    ov = stg[si].rearrange("p (a b) -> p a b", a=2)
                    S.op("act", lambda e: e.activation(out=ov, in_=pv, func=AF.Square, accum_out=sst[:, c0:c0 + 1]),
                         reads=[B_bank[bp], B_bank[bp + 1]], writes=[B_stg[si], B_ssf[i]])
                    S.op("act", lambda e: e.activation(out=sst[:, c0 + 1:c0 + 2], in_=sst[:, c0:c0 + 1], func=AF.Sqrt,
                                                       scale=1.0 / D, bias=epst[:, 0:1]),
                         reads=[B_ssf[i], B_const], writes=[B_ssf[i]])
                    S.op("dve", lambda e: e.reciprocal(out=sst[:, c0 + 2:c0 + 3], in_=sst[:, c0 + 1:c0 + 2]),
                         reads=[B_ssf[i]], writes=[B_ssf[i]])
                    S.op("dve", lambda e: e.scalar_tensor_tensor(
                        out=ov, in0=pv, scalar=sst[:, c0 + 2:c0 + 3], in1=gfin[:].rearrange("p (a b) -> p a b", a=2),
                        op0=ALU.mult, op1=ALU.mult),
                        reads=[B_bank[bp], B_bank[bp + 1], B_ssf[i], B_gfin], writes=[B_stg[si]])
                    S.op("sp", lambda e: e.dma_start(out=out[q, tt * 128:(tt + 1) * 128, :], in_=stg[si]),
                         reads=[B_stg[si]], dma_sem="stg%d" % si)
                pending.append(step)

        def norm(spans, g):
            for s in spans:
                st = {}

                def sq_step(kcs, s=s, st=st):
                    if "bs" not in st:
                        st["bs"] = next_bank()
                        reserved.add(st["bs"])
                        st["mm"] = []
                    bs = st["bs"]
                    for f in st["mm"]:
                        f()
                    st["mm"] = []
                    for kc in kcs:
                        i = kc % 2
                        src = xT[:, kc, s * 512:(s + 1) * 512]
                        if kc % 2 == 0:
                            S.op("act", lambda e, i=i, src=src: e.activation(out=sqb[i][:], in_=src, func=AF.Square),
                                 reads=[B_xT[kc][s]], writes=[B_sq[i]])
                        else:
                            S.op("dve", lambda e, i=i, src=src: e.tensor_tensor(out=sqb[i][:], in0=src, in1=src, op=ALU.mult),
                                 reads=[B_xT[kc][s]], writes=[B_sq[i]])
                        st["mm"].append(lambda i=i, kc=kc, bs=bs: S.op(
                            "pe", lambda e: e.matmul(PS[:, bs, :], lhsT=onesb[:], rhs=sqb[i][:], start=(kc == 0), stop=(kc == 7)),
                            reads=[B_sq[i], B_const], writes=[B_bank[bs]]))

                def rstd_step(s=s, st=st):
                    bs = st["bs"]
                    for f in st["mm"]:
                        f()
                    st["mm"] = []
                    S.op("act", lambda e, bs=bs: e.activation(out=rstd[:], in_=PS[:, bs, :], func=AF.Ln,
                                                              scale=1.0 / D, bias=epst[:, 0:1]),
                         reads=[B_bank[bs], B_const], writes=[B_rstd])
                    reserved.discard(bs)
                    S.op("act", lambda e: e.activation(out=rstd[:], in_=rstd[:], func=AF.Exp, scale=-0.5),
                         reads=[B_rstd], writes=[B_rstd])

                def h_step(kcs, s=s):
                    for kc in kcs:
                        S.op("dve", lambda e, kc=kc, s=s: e.scalar_tensor_tensor(
                            out=hT[:, kc, s * 512:(s + 1) * 512], in0=xT[:, kc, s * 512:(s + 1) * 512], scalar=g[:, kc:kc + 1],
                            in1=rstd[:], op0=ALU.mult, op1=ALU.mult),
                            reads=[B_xT[kc][s], B_rstd, B_const], writes=[B_hT[kc][s]])

                for a in range(4):
                    pending.append(lambda a=a, f=sq_step: f([2 * a, 2 * a + 1]))
                pending.append(rstd_step)
                for a in range(4):
                    pending.append(lambda a=a, f=h_step: f([2 * a, 2 * a + 1]))

        def ffn(which, h, g, first=False, s_outer=False, on_span_done=None):
            spans = [2 * h, 2 * h + 1]
            wg, wu, wd = w_g[which], w_u[which], w_d[which]
            silc = 0
            it = 0
            for grp in range(6):
                c0 = grp * 512
                c1 = min(c0 + 512, DFF)
                Gv, Gb = wload(wsrc(wg, 0, D, c0, c1), 8, c1 - c0)
                Uv, Ub = wload(wsrc(wu, 0, D, c0, c1), 8, c1 - c0)
                for sl, s in enumerate(spans):
                    for cl in range((c1 - c0) // 128):
                        c = grp * 4 + cl
                        bg = next_bank()
                        mm_group(PS[:, bg, :], [(Gv[:, kc, cl * 128:(cl + 1) * 128], hT[:, kc, s * 512:(s + 1) * 512],
                                                 [Gb, B_hT[kc][s]]) for kc in range(8)], B_bank[bg])
                        bu = next_bank()
                        mm_group(PS[:, bu, :], [(Uv[:, kc, cl * 128:(cl + 1) * 128], hT[:, kc, s * 512:(s + 1) * 512],
                                                 [Ub, B_hT[kc][s]]) for kc in range(8)], B_bank[bu])
                        si = silc % 2
                        silc += 1
                        S.op("act", lambda e, si=si, bg=bg: e.activation(out=sil[si], in_=PS[:, bg, :], func=AF.Silu),
                             reads=[B_bank[bg]], writes=[B_sil[si]])
                        S.op("dve", lambda e, si=si, bu=bu, c=c, sl=sl: e.tensor_tensor(
                            out=actT[:, c, sl * 512:(sl + 1) * 512], in0=PS[:, bu, :], in1=sil[si], op=ALU.mult),
                            reads=[B_bank[bu], B_sil[si]], writes=[B_act[c][sl]])
                        pump(3 if (first and it < 3) else 1)
                        it += 1

            def down_unit(cq, Dv, sl, s):
                for dcl in range(2):
                    dc = cq * 2 + dcl
                    bd = next_bank()
                    mm_group(PS[:, bd, :], [(Dv[c // 11][0][:, c % 11, dcl * 128:(dcl + 1) * 128],
                                             actT[:, c, sl * 512:(sl + 1) * 512],
                                             [Dv[c // 11][1], B_act[c][sl]]) for c in range(NCH)], B_bank[bd])
                    S.op("dve", lambda e, bd=bd, dc=dc, s=s: e.scalar_tensor_tensor(
                        out=xT[:, dc, s * 512:(s + 1) * 512], in0=PS[:, bd, :], scalar=0.5,
                        in1=xT[:, dc, s * 512:(s + 1) * 512], op0=ALU.mult, op1=ALU.add),
                        reads=[B_bank[bd], B_xT[dc][s]], writes=[B_xT[dc][s]])
                    pump(1)

            def load_D(cq):
                Dv = []
                for rh in range(2):
                    r0 = rh * 11 * 128
                    Dv.append(wload(wsrc(wd, r0, r0 + 11 * 128, cq * 256, (cq + 1) * 256), 11, 256))
                return Dv

            if s_outer:
                for sl, s in enumerate(spans):
                    for cq in range(4):
                        Dv = load_D(cq)
                        down_unit(cq, Dv, sl, s)
                    if on_span_done is not None:
                        on_span_done(s)
            else:
                for cq in range(4):
                    Dv = load_D(cq)
                    for sl, s in enumerate(spans):
                        down_unit(cq, Dv, sl, s)

        def mixer_attention(q):
            Wv, Wvb = wload(wsrc(w_in, 0, D, 1024, 1536), 8, 512)
            Wq, Wqb = wload(wsrc(w_in, 0, D, 0, 512), 8, 512)
            S.op("pool", lambda e: e.memset(qA[64:128, :], 0.0), writes=[B_zero])
            S.op("pool", lambda e: e.memset(qB[0:64, :], 0.0), writes=[B_zero])
            S.op("pool", lambda e: e.memset(kA[64:96, :], 1.0), writes=[B_zero])
            S.op("pool", lambda e: e.memset(kA[96:128, :], 0.0), writes=[B_zero])
            S.op("pool", lambda e: e.memset(kB[0:32, :], 0.0), writes=[B_zero])
            S.op("pool", lambda e: e.memset(kB[32:64, :], 1.0), writes=[B_zero])
            S.op("pool", lambda e: e.memset(Vaug[:, :, :, 1, :], 1.0), writes=[B_Vones])
            Wk, Wkb = wload(wsrc(w_in, 0, D, 512, 1024), 8, 512)
            norm([2, 3], gt[1])
            for tt in range(16):
                s = tt // 4
                bv = next_bank()
                mm_group(PS[:, bv, :], [(hT[:, kc, tt * 128:(tt + 1) * 128], Wv[:, kc, :], [Wvb, B_hT[kc][s]])
                                        for kc in range(8)], B_bank[bv])
                src = PS[:, bv, :].rearrange("p (a b c) -> p a b c", a=4, b=2)
                dst = Vaug[:, tt, :, 0:3:2, :]
                if tt % 2 == 0:
                    S.op("act", lambda e, src=src, dst=dst: e.activation(out=dst, in_=src, func=AF.Copy),
                         reads=[B_bank[bv]], writes=[B_V[tt]])
                else:
                    S.op("dve", lambda e, src=src, dst=dst: e.tensor_copy(out=dst, in_=src),
                         reads=[B_bank[bv]], writes=[B_V[tt]])
                if tt < 8:
                    pump(3)
                if tt == 7:
                    flush()
            bf_ = next_bank()
            for tt in range(16):
                s = tt // 4
                mm_group(PS[:, bf_, tt * 8:(tt + 1) * 8], [(hT[:, kc, tt * 128:(tt + 1) * 128], wf[:, kc, :],
                                                             [B_const, B_hT[kc][s]]) for kc in range(8)], B_bank[bf_])
            S.op("dve", lambda e: e.tensor_tensor(out=zt, in0=PS[:, bf_, 0:128], in1=bfb[:], op=ALU.add),
                 reads=[B_bank[bf_], B_const], writes=[B_z])
            S.op("act", lambda e: e.activation(out=zt, in_=zt, func=AF.Exp, scale=-1.0), reads=[B_z], writes=[B_z])
            S.op("act", lambda e: e.activation(out=spt, in_=zt, func=AF.Ln, bias=1.0), reads=[B_z], writes=[B_sp])
            S.op("dve", lambda e: e.memset(spx[:, 0:8], 0.0), writes=[B_spx])
            for tt in range(1, 16):
                S.op("dve", lambda e, tt=tt: e.tensor_tensor(out=spx[:, tt * 8:(tt + 1) * 8], in0=spx[:, (tt - 1) * 8:tt * 8],
                                                              in1=spt[:, (tt - 1) * 8:tt * 8], op=ALU.add),
                     reads=[B_sp, B_spx], writes=[B_spx])
            S.op("dve", lambda e: e.tensor_copy(out=sp_bf, in_=spt), reads=[B_sp], writes=[B_spbf])
            S.op("dve", lambda e: e.tensor_copy(out=spx_bf, in_=spx), reads=[B_spx], writes=[B_spxbf])
            bc = next_bank()
            S.op("pe", lambda e: e.matmul(PS[:, bc, 0:128], lhsT=triT[:], rhs=spt, start=True, stop=False),
                 reads=[B_sp, B_const], writes=[B_bank[bc]])
            S.op("pe", lambda e: e.matmul(PS[:, bc, 0:128], lhsT=onesF[:], rhs=spx, start=False, stop=True),
                 reads=[B_spx, B_const], writes=[B_bank[bc]])
            S.op("dve", lambda e: e.tensor_copy(out=Ck, in_=PS[:, bc, 0:128]), reads=[B_bank[bc]], writes=[B_Ck])
            for s in range(4):
                bn = next_bank()
                for kbl in range(4):
                    kb = 4 * s + kbl
                    S.op("pe", lambda e, kb=kb, kbl=kbl, bn=bn: e.matmul(
                        PS[0:8, bn, kbl * 128:(kbl + 1) * 128], lhsT=sp_bf[:, kb * 8:(kb + 1) * 8], rhs=negtri[:],
                        start=True, stop=False), reads=[B_spbf, B_const], writes=[B_bank[bn]])
                    S.op("pe", lambda e, kb=kb, kbl=kbl, bn=bn: e.matmul(
                        PS[0:8, bn, kbl * 128:(kbl + 1) * 128], lhsT=spx_bf[:, kb * 8:(kb + 1) * 8], rhs=negones[:],
                        start=False, stop=True), reads=[B_spxbf, B_const], writes=[B_bank[bn]])
                S.op("dve", lambda e, s=s, bn=bn: e.tensor_copy(out=qA[96:104, s * 512:(s + 1) * 512], in_=PS[0:8, bn, :]),
                     reads=[B_bank[bn], B_zero], writes=[B_qaux])
            def borrow_slot():
                i = job_i[0] % NSLOT
                B_slot[i].owner = job_i[0]
                job_i[0] += 1
                return ring[:, i, :], B_slot[i]

            r0v, r0b = borrow_slot()
            r1v, r1b = borrow_slot()
            TS = [dict(qA=qA, qB=qB, kA=kA, kB=kB, gq=[], gk=[]),
                  dict(qA=r0v[:, 0:2048], qB=r0v[:, 2048:4096], kA=r1v[:, 0:2048], kB=r1v[:, 2048:4096], gq=[r0b], gk=[r1b])]
            B_q1 = {n: [Buf("%s1_%d" % (n, s)) for s in range(4)] for n in ("qA", "qB", "kA", "kB")}
            B_aug1 = {"A": Buf("augA1"), "B": Buf("augB1")}
            B_zero1 = Buf("qkzero1")
            BQ = [B_q, B_q1]
            BAUG = [B_aug, B_aug1]
            BZ = [B_zero, B_zero1]
            t1 = TS[1]
            S.op("pool", lambda e: e.memset(t1["qA"][64:128, :], 0.0), writes=[B_zero1] + t1["gq"])
            S.op("pool", lambda e: e.memset(t1["qB"][0:64, :], 0.0), writes=[B_zero1] + t1["gq"])
            S.op("pool", lambda e: e.memset(t1["kA"][64:96, :], 1.0), writes=[B_zero1] + t1["gk"])
            S.op("pool", lambda e: e.memset(t1["kA"][96:128, :], 0.0), writes=[B_zero1] + t1["gk"])
            S.op("pool", lambda e: e.memset(t1["kB"][0:32, :], 0.0), writes=[B_zero1] + t1["gk"])
            S.op("pool", lambda e: e.memset(t1["kB"][32:64, :], 1.0), writes=[B_zero1] + t1["gk"])

            pti = [0]
            oi = [0]
            SP_ = (0, 1, 2, 3, 4, 5)

            def proj_steps(j):
                ts = j % 2
                T = TS[ts]
                Bq_ = BQ[ts]
                Bz = BZ[ts]

                def first():
                    S.op("sp", lambda e: e.dma_start(out=T["qA"][64:65, :], in_=qA[96 + 2 * j:97 + 2 * j, :]),
                         reads=[B_qaux, Bz], writes=[BAUG[ts]["A"]] + T["gq"], dma_sem="augA%d" % ts)
                    S.op("sp", lambda e: e.dma_start(out=T["qB"][63:64, :], in_=qA[97 + 2 * j:98 + 2 * j, :]),
                         reads=[B_qaux, Bz], writes=[BAUG[ts]["B"]] + T["gq"], dma_sem="augB%d" % ts)

                def grp_steps(W, Wb, s, evac):
                    sl_ = slice(s * 512, (s + 1) * 512)
                    st = {}

                    def sub(a):
                        if a == 0:
                            st["b"] = next_bank(pool=SP_)
                            reserved.add(st["b"])
                        b = st["b"]
                        for kc in (2 * a, 2 * a + 1):
                            S.op("pe", lambda e, kc=kc: e.matmul(PS[:, b, :], lhsT=W[:, kc, j * 128:(j + 1) * 128],
                                                                  rhs=hT[:, kc, sl_], start=(kc == 0), stop=(kc == 7)),
                                 reads=[Wb[0], B_hT[kc][s]], writes=[B_bank[b]])
                        if a == 3:
                            reserved.discard(b)
                            evac(b, s, sl_)
                    return [lambda a=a: sub(a) for a in range(4)]

                def q_evac(bq, s, sl_):
                    S.op("dve", lambda e: e.tensor_scalar(out=T["qA"][0:64, sl_], in0=PS[0:64, bq, :],
                                                          scalar1=0.125, scalar2=None, op0=ALU.mult),
                         reads=[B_bank[bq], Bz], writes=[Bq_["qA"][s]] + T["gq"])
                    S.op("dve", lambda e: e.tensor_scalar(out=T["qB"][64:128, sl_], in0=PS[64:128, bq, :],
                                                          scalar1=0.125, scalar2=None, op0=ALU.mult),
                         reads=[B_bank[bq], Bz], writes=[Bq_["qB"][s]] + T["gq"])

                def k_evac(bk, s, sl_):
                    S.op("dve", lambda e: e.tensor_copy(out=T["kA"][0:64, sl_], in_=PS[0:64, bk, :]),
                         reads=[B_bank[bk], Bz], writes=[Bq_["kA"][s]] + T["gk"])
                    S.op("dve", lambda e: e.tensor_copy(out=T["kB"][64:128, sl_], in_=PS[64:128, bk, :]),
                         reads=[B_bank[bk], Bz], writes=[Bq_["kB"][s]] + T["gk"])

                pending.append(first)
                for s in range(4):
                    pending.extend(grp_steps(Wq, Wqb, s, q_evac))
                    pending.extend(grp_steps(Wk, Wkb, s, k_evac))

            proj_steps(0)
            flush()
            blk_cnt = [0]
            dve_defer = []
            for j in range(4):
                ts = j % 2
                T = TS[ts]
                if j + 1 < 4:
                    proj_steps(j + 1)
                for X in ("A", "B"):
                    h = 2 * j + (0 if X == "A" else 1)
                    qt = T["q" + X]
                    kt = T["k" + X]
                    Bq = BQ[ts]["q" + X]
                    Bk = BQ[ts]["k" + X]
                    Baug = BAUG[ts][X]
                    Bz = BZ[ts]
                    guards = T["gq"] + T["gk"]
                    for qs in range(4):
                        bo = 6 + (oi[0] % 2)
                        oi[0] += 1
                        nblk = 4 * qs + 4
                        pend = []

                        def emit_S(kb, qs=qs, h=h, qt=qt, kt=kt, Bq=Bq, Bk=Bk, Baug=Baug, Bz=Bz, guards=guards):
                            diag = kb >= 4 * qs
                            q0 = kb * 128 if diag else qs * 512
                            N = (qs + 1) * 512 - q0
                            bs_ = next_bank(pool=SP_)
                            pi = pti[0] % 5
                            pti[0] += 1
                            S.op("pe", lambda e: e.matmul(PS[:, bs_, 0:N], lhsT=kt[:, kb * 128:(kb + 1) * 128],
                                                          rhs=qt[:, q0:q0 + N], start=True, stop=not diag),
                                 reads=[Bk[kb // 4], Bq[qs], Baug, Bz] + guards, writes=[B_bank[bs_]])
                            if diag:
                                S.op("pe", lambda e: e.matmul(PS[:, bs_, 0:128], lhsT=identb[:], rhs=maskb[:],
                                                              start=False, stop=True),
                                     reads=[B_const], writes=[B_bank[bs_]])
                            S.op("act", lambda e: e.activation(out=PT[pi][:, 0:N], in_=PS[:, bs_, 0:N], func=AF.Exp,
                                                               bias=Ck[:, kb * 8 + h:kb * 8 + h + 1], scale=1.0),
                                 reads=[B_bank[bs_], B_Ck], writes=[B_PT[pi]])
                            return (kb, pi, q0, N)

                        def emit_PV(info, qs=qs, j=j, X=X, bo=bo, nblk=nblk):
                            kb, pi, q0, N = info
                            lo = q0 - qs * 512
                            vl = Vaug[:, kb, j, 0:2, :] if X == "A" else Vaug[:, kb, j, 1:3, :]
                            S.op("pe", lambda e: e.matmul(PS[:, bo, lo:512], lhsT=vl, rhs=PT[pi][:, 0:N],
                                                          start=(kb == 0), stop=(kb == nblk - 1)),
                                 reads=[B_PT[pi], B_V[kb], B_Vones], writes=[B_bank[bo]])

                        for kb in range(nblk):
                            pend.append(emit_S(kb))
                            if len(pend) > 4:
                                emit_PV(pend.pop(0))
                            if dve_defer:
                                dve_defer.pop(0)()
                            blk_cnt[0] += 1
                            pump(1)
                        while pend:
                            emit_PV(pend.pop(0))
                        while dve_defer:
                            dve_defer.pop(0)()
                        sl_ = slice(qs * 512, (qs + 1) * 512)
                        if X == "A":
                            orow, drow = slice(0, 64), slice(64, 128)
                        else:
                            orow, drow = slice(64, 128), slice(0, 64)
                        if qs == 0:
                            S.op("act", lambda e, bo=bo, drow=drow: e.activation(out=rc2[drow, :], in_=PS[drow, bo, :], func=AF.Ln),
                                 reads=[B_bank[bo]], writes=[B_rc2])
                            S.op("act", lambda e, drow=drow: e.activation(out=rc2[drow, :], in_=rc2[drow, :], func=AF.Exp, scale=-1.0),
                                 reads=[B_rc2], writes=[B_rc2])
                            S.op("dve", lambda e, orow=orow, drow=drow: e.tensor_copy(out=rc2[orow, :], in_=rc2[drow, :]),
                                 reads=[B_rc2], writes=[B_rc2])
                            S.op("dve", lambda e, bo=bo, sl_=sl_, j=j, orow=orow: e.tensor_tensor(
                                out=attnT[orow, j, sl_], in0=PS[orow, bo, :], in1=rc2[orow, :], op=ALU.mult),
                                reads=[B_bank[bo], B_rc2], writes=[B_attn[j][qs]])
                        else:
                            for cp in range(4):
                                dve_defer.append(lambda bo=bo, orow=orow, drow=drow, cp=cp: S.op(
                                    "dve", lambda e: e.reciprocal(out=rc[orow, cp * 128:(cp + 1) * 128],
                                                                  in_=PS[drow, bo, cp * 128:(cp + 1) * 128]),
                                    reads=[B_bank[bo]], writes=[B_rc]))
                            dve_defer.append(lambda bo=bo, sl_=sl_, j=j, orow=orow, qs=qs: S.op(
                                "dve", lambda e: e.tensor_tensor(out=attnT[orow, j, sl_], in0=PS[orow, bo, :], in1=rc[orow, :],
                                                                  op=ALU.mult),
                                reads=[B_bank[bo], B_rc], writes=[B_attn[j][qs]]))
                while dve_defer:
                    dve_defer.pop(0)()
                flush()

        def mixer_post(q, hf):
            spans = [2 * hf, 2 * hf + 1]
            cv_src = w_in[:, 1544:3080].rearrange("(k p) (t j c) -> p k t j c", p=128, t=3, j=4)
            for jc in range(4):
                Wcv, Bcv = wload_multi(3072, [
                    (lambda f, t=t: f.rearrange("p (k t c) -> p k t c", k=8, t=3)[:, :, t, :], cv_src[:, :, t, jc, :])
                    for t in range(3)])
                Wcv = Wcv.rearrange("p (k t c) -> p k t c", k=8, t=3)
                for sl, s in enumerate(spans):
                    sl_ = slice(s * 512, (s + 1) * 512)
                    bx = next_bank()
                    mm_group(PS[:, bx, :], [(Wcv[:, kc, 2, :], hT[:, kc, sl_], [Bcv, B_hT[kc][s]]) for kc in range(8)], B_bank[bx])
                    bcc = next_bank()
                    mm_group(PS[:, bcc, :], [(Wcv[:, kc, 1, :], hT[:, kc, sl_], [Bcv, B_hT[kc][s]]) for kc in range(8)], B_bank[bcc])
                    bcb = next_bank()
                    mm_group(PS[:, bcb, :], [(Wcv[:, kc, 0, :], hT[:, kc, sl_], [Bcv, B_hT[kc][s]]) for kc in range(8)], B_bank[bcb])
                    u = ubuf[jc]
                    if s == 0:
                        S.op("dve", lambda e, u=u: e.memset(u[:, 0:2], 0.0), writes=[B_u[jc]])
                    S.op("act", lambda e, bx=bx: e.activation(out=cxs, in_=PS[:, bx, :], func=AF.Copy),
                         reads=[B_bank[bx]], writes=[B_cxs])
                    S.op("dve", lambda e, u=u, bcc=bcc: e.tensor_tensor(out=u[:, 2:514], in0=PS[:, bcc, :], in1=cxs, op=ALU.mult),
                         reads=[B_bank[bcc], B_cxs], writes=[B_u[jc]])
                    S.op("dve", lambda e, u=u, jc=jc: e.tensor_scalar(out=tconv, in0=u[:, 0:512], scalar1=cwt[:, jc * 3:jc * 3 + 1],
                                                                       scalar2=None, op0=ALU.mult),
                         reads=[B_u[jc], B_const], writes=[B_tconv])
                    S.op("dve", lambda e, u=u, jc=jc: e.scalar_tensor_tensor(
                        out=tconv, in0=u[:, 1:513], scalar=cwt[:, jc * 3 + 1:jc * 3 + 2], in1=tconv, op0=ALU.mult, op1=ALU.add),
                        reads=[B_u[jc], B_tconv, B_const], writes=[B_tconv])
                    S.op("dve", lambda e, u=u, jc=jc: e.scalar_tensor_tensor(
                        out=tconv, in0=u[:, 2:514], scalar=cwt[:, jc * 3 + 2:jc * 3 + 3], in1=tconv, op0=ALU.mult, op1=ALU.add),
                        reads=[B_u[jc], B_tconv, B_const], writes=[B_tconv])
                    S.op("dve", lambda e, bcb=bcb, jc=jc, sl=sl: e.tensor_tensor(
                        out=convin[:, jc, sl * 512:(sl + 1) * 512], in0=PS[:, bcb, :], in1=tconv, op=ALU.mult),
                        reads=[B_bank[bcb], B_tconv], writes=[B_convin[jc][sl]])
                    S.op("dve", lambda e, u=u: e.tensor_copy(out=u[:, 0:2], in_=u[:, 512:514]),
                         reads=[B_u[jc]], writes=[B_u[jc]])
                    pump(1)
            for mq in range(4):
                c0 = mq * 256
                gsrc = lambda base: w_in[:, base + c0:base + c0 + 256].rearrange("(k p) c -> p k c", p=128)
                Wgg, Bgg = wload_multi(4096, [
                    (lambda f: f[:, 0:2048].rearrange("p (k c) -> p k c", k=8), gsrc(3080)),
                    (lambda f: f[:, 2048:4096].rearrange("p (k c) -> p k c", k=8), gsrc(4104))])
                Wga = Wgg[:, 0:2048].rearrange("p (k c) -> p k c", k=8)
                Wgc = Wgg[:, 2048:4096].rearrange("p (k c) -> p k c", k=8)
                Bga = Bgc = Bgg
                Woo, Boca = wload_multi(2048, [
                    (lambda f: f[:, 0:1024].rearrange("p (k c) -> p k c", k=4),
                     w_oc[:, c0:c0 + 256].rearrange("(k p) c -> p k c", p=128)),
                    (lambda f: f[:, 1024:2048].rearrange("p (k c) -> p k c", k=4),
                     w_oa[:, c0:c0 + 256].rearrange("(k p) c -> p k c", p=128))])
                Woca = Woo.rearrange("p (k c) -> p k c", k=8)
                for cl in range(2):
                    c = 2 * mq + cl
                    for sl, s in enumerate(spans):
                        sl_ = slice(s * 512, (s + 1) * 512)
                        ll = slice(sl * 512, (sl + 1) * 512)
                        b_ga = next_bank()
                        mm_group(PS[:, b_ga, :], [(Wga[:, kc, cl * 128:(cl + 1) * 128], hT[:, kc, sl_], [Bga, B_hT[kc][s]])
                                                  for kc in range(8)], B_bank[b_ga])
                        b_gc = next_bank()
                        mm_group(PS[:, b_gc, :], [(Wgc[:, kc, cl * 128:(cl + 1) * 128], hT[:, kc, sl_], [Bgc, B_hT[kc][s]])
                                                  for kc in range(8)], B_bank[b_gc])
                        b_ya = next_bank()
                        mm_group(PS[:, b_ya, :], [(Woca[:, 4 + kc, cl * 128:(cl + 1) * 128], attnT[:, kc, sl_], [Boca, B_attn[kc][s]])
                                                  for kc in range(4)], B_bank[b_ya])
                        b_yc = next_bank()
                        mm_group(PS[:, b_yc, :], [(Woca[:, kc, cl * 128:(cl + 1) * 128], convin[:, kc, ll], [Boca, B_convin[kc][sl]])
                                                  for kc in range(4)], B_bank[b_yc])
                        S.op("act", lambda e, b=b_ga: e.activation(out=sa_t, in_=PS[:, b, :], func=AF.Sigmoid),
                             reads=[B_bank[b_ga]], writes=[B_sa])
                        S.op("act", lambda e, b=b_gc: e.activation(out=sc_t, in_=PS[:, b, :], func=AF.Sigmoid),
                             reads=[B_bank[b_gc]], writes=[B_sc])
                        S.op("dve", lambda e, b=b_ya: e.tensor_tensor(out=m1_t, in0=PS[:, b, :], in1=sa_t, op=ALU.mult),
                             reads=[B_bank[b_ya], B_sa], writes=[B_m1])
                        S.op("dve", lambda e, b=b_yc: e.tensor_tensor(out=m2_t, in0=PS[:, b, :], in1=sc_t, op=ALU.mult),
                             reads=[B_bank[b_yc], B_sc], writes=[B_m2])
                        dbg = os.environ.get("MK_DBG", "")
                        if dbg == "noattn":
                            S.op("dve", lambda e, c=c, ll=ll: e.tensor_copy(out=merged[:, c, ll], in_=m2_t),
                                 reads=[B_m1, B_m2], writes=[B_merged[c][sl]])
                        elif dbg == "noconv":
                            S.op("dve", lambda e, c=c, ll=ll: e.tensor_copy(out=merged[:, c, ll], in_=m1_t),
                                 reads=[B_m1, B_m2], writes=[B_merged[c][sl]])
                        else:
                            S.op("dve", lambda e, c=c, ll=ll: e.tensor_tensor(out=merged[:, c, ll], in0=m1_t, in1=m2_t, op=ALU.add),
                                 reads=[B_m1, B_m2], writes=[B_merged[c][sl]])
                        pump(1)
            for ch in range(2):
                Wo, Bo = wload(wsrc(w_out, 0, D, ch * 512, (ch + 1) * 512), 8, 512)
                for dcl in range(4):
                    dc = 4 * ch + dcl
                    for sl, s in enumerate(spans):
                        sl_ = slice(s * 512, (s + 1) * 512)
                        ll = slice(sl * 512, (sl + 1) * 512)
                        bd = next_bank()
                        mm_group(PS[:, bd, :], [(Wo[:, kc, dcl * 128:(dcl + 1) * 128], merged[:, kc, ll], [Bo, B_merged[kc][sl]])
                                                for kc in range(8)], B_bank[bd])
                        S.op("dve", lambda e, bd=bd, dc=dc, sl_=sl_: e.tensor_tensor(
                            out=xT[:, dc, sl_], in0=PS[:, bd, :], in1=xT[:, dc, sl_], op=ALU.add),
                            reads=[B_bank[bd], B_xT[dc][s]], writes=[B_xT[dc][s]])

        order = ["X", "F1", "ATT", "POST", "F2"]
        lim = order.index(stop_after) if stop_after else len(order) - 1
        tail_done = [False]
        for q in range(nseq):
            if q == 0:
                x_steps(0, range(0, 8))
                if lim >= 1:
                    norm([0], gt[0])
                flush()
                if lim >= 1:
                    norm([1], gt[0])
            else:
                flush()
                final_steps(q - 1, range(8, 16))
            x_steps(q, range(8, 16))
            if lim >= 1:
                norm([2, 3], gt[0])
                ffn(0, 0, gt[0], first=(q == 0))
                flush()
                if lim >= 2:
                    norm([0, 1], gt[1])
                ffn(0, 1, gt[0])
            flush()
            if lim >= 2:
                S.barrier(["stg0", "stg1", "stg2", "stg3"])
                mixer_attention(q)
            if lim >= 3:
                S.barrier(["augA0", "augA1", "augB0", "augB1"])
                mixer_post(q, 0)
                if lim >= 4:
                    norm([0, 1], gt[2])
                mixer_post(q, 1)
                flush()
            if lim >= 4:
                S.barrier()
                norm([2, 3], gt[2])
                ffn(1, 0, gt[2])
                flush()
                final_steps(q, range(0, 8))
                if q + 1 < nseq:
                    x_steps(q + 1, range(0, 8))
                    norm([0, 1], gt[0])
                if q + 1 < nseq:
                    ffn(1, 1, gt[2])
                else:
                    ffn(1, 1, gt[2], s_outer=True,
                        on_span_done=lambda s_, q=q: final_steps_single(q, range(4 * s_, 4 * s_ + 4)))
                    tail_done[0] = True
            else:
                final_steps(q, range(0, 8))
                if q + 1 < nseq:
                    x_steps(q + 1, range(0, 8))
                    if lim >= 1:
                        norm([0, 1], gt[0])
        flush()
        if not tail_done[0]:
            final_steps_single(nseq - 1, range(8, 16))
            flush()
        S.op("sp", lambda e: e.nop(), extra_deps=[S.dma_last[k] for k in ("stg0", "stg1", "stg2", "stg3") if k in S.dma_last])
        S.emit(nc, sems, dsems)
    return nc


_NC_CACHE = {}


def _prep_inputs(inputs, n_cores=8):
    f = lambda a: np.ascontiguousarray(np.asarray(a, dtype=np.float32))
    x = f(inputs["x"])
    per = x.shape[0] // n_cores
    pk = lambda g: np.ascontiguousarray(f(g).reshape(8, 128).T)
    shared = {
        "g1": pk(inputs["ffn1_norm"]), "gm": pk(inputs["mix_norm"]), "g2": pk(inputs["ffn2_norm"]),
        "gfin": np.ascontiguousarray(np.broadcast_to(f(inputs["final_norm"])[None, :], (128, D))),
        "bfb": np.ascontiguousarray(np.broadcast_to(np.tile(f(inputs["b_forget"]), 16)[None, :], (128, 128))),
        "cw": np.ascontiguousarray(f(inputs["conv_w"]).reshape(3, 4, 128).transpose(2, 1, 0).reshape(128, 12)),
    }
    for k in ("ffn1_gate", "ffn1_up", "ffn1_down", "ffn2_gate", "ffn2_up", "ffn2_down", "w_in", "w_o_attn",
              "w_o_conv", "w_out"):
        shared[k] = f(inputs[k])
    in_maps = []
    for c in range(n_cores):
        m = dict(shared)
        m["x"] = np.ascontiguousarray(x[c * per:(c + 1) * per])
        in_maps.append(m)
    return in_maps


def kernel(**inputs):
    n_cores = 8
    stop = os.environ.get("MK_STOP") or None
    key = stop
    if key not in _NC_CACHE:
        _NC_CACHE[key] = build_nc(stop_after=stop)
    nc = _NC_CACHE[key]
    in_maps = _prep_inputs(inputs, n_cores)
    res = run_bass_kernel_spmd(nc, in_maps, core_ids=list(range(n_cores)))
    return np.concatenate([np.asarray(r["out"]) for r in res.results], axis=0).astype(np.float32)
```

```python
import os
from contextlib import ExitStack
import numpy as np
import concourse.bass as bass
import concourse.mybir as mybir
from concourse.bass_utils import run_bass_kernel_spmd

F32 = mybir.dt.float32
BF16 = mybir.dt.bfloat16
AF = mybir.ActivationFunctionType
ALU = mybir.AluOpType

ENGINES = ("pe", "act", "dve", "pool", "sp")
NSEQ = 2
SEQ = 2048
D = 1024
DFF = 2816
NCH = 22
NSLOT = 4
EPS = 1e-6
MASKVAL = -30000.0


class Buf:
    __slots__ = ("name", "last_w", "reads", "owner")

    def __init__(self, name):
        self.name = name
        self.last_w = None
        self.reads = {}
        self.owner = None


class Op:
    __slots__ = ("eng", "idx", "fn", "waits", "signal", "sigval", "dma_sem", "is_dma", "order")

    def __init__(self, eng, idx, fn, dma_sem=None):
        self.eng = eng
        self.idx = idx
        self.fn = fn
        self.waits = []
        self.signal = False
        self.sigval = None
        self.is_dma = dma_sem is not None
        self.dma_sem = dma_sem
        self.order = idx


class Sched:
    def __init__(self):
        self.prog = {e: [] for e in ENGINES}
        self.waited = {e: {} for e in ENGINES}
        self.dma_count = {}
        self.dma_last = {}
        self.same_engine_sync = {"act", "dve", "pool"}
        self.barrier_deps = []

    def op(self, eng, fn, reads=(), writes=(), dma_sem=None, extra_deps=(), exempt=False):
        o = Op(eng, len(self.prog[eng]), fn, dma_sem=dma_sem)
        if o.is_dma:
            self.dma_count[dma_sem] = self.dma_count.get(dma_sem, 0) + 1
            o.order = self.dma_count[dma_sem]
            self.dma_last[dma_sem] = o
        deps = []
        for b in reads:
            if b.last_w is not None:
                deps.append(b.last_w)
        for b in writes:
            if b.last_w is not None:
                deps.append(b.last_w)
            deps.extend(b.reads.values())
        deps.extend(extra_deps)
        if not exempt:
            deps.extend(self.barrier_deps)
        w = self.waited[eng]
        for d in deps:
            if d is o:
                continue
            if (not d.is_dma) and d.eng == eng and eng not in self.same_engine_sync:
                continue
            key = ("dma", d.dma_sem) if d.is_dma else ("eng", d.eng)
            if w.get(key, -1) >= d.order:
                continue
            w[key] = d.order
            d.signal = True
            o.waits.append(d)
        self.prog[eng].append(o)
        rkey = ("dma", dma_sem) if o.is_dma else eng
        for b in reads:
            b.reads[rkey] = o
        for b in writes:
            b.last_w = o
            b.reads = {}
        return o

    def barrier(self, include_dma_keys=()):
        deps = []
        for e in ("pe", "act", "dve", "sp"):
            for o in reversed(self.prog[e]):
                if not o.is_dma:
                    deps.append(o)
                    break
        for k in include_dma_keys:
            if k in self.dma_last:
                deps.append(self.dma_last[k])
        self.barrier_deps = deps

    def emit(self, nc, sems, dma_sems):
        for e in ENGINES:
            c = 0
            for o in self.prog[e]:
                if o.is_dma:
                    o.sigval = 16 * o.order
                elif o.signal:
                    c += 1
                    o.sigval = c

        def run(engname, eobj):
            for o in self.prog[engname]:
                for d in o.waits:
                    s = dma_sems[d.dma_sem] if d.is_dma else sems[d.eng]
                    eobj.wait_ge(s, d.sigval)
                ins = o.fn(eobj)
                if o.is_dma:
                    ins.then_inc(dma_sems[o.dma_sem], 16)
                elif o.signal:
                    ins.then_inc(sems[o.eng], 1)

        with nc.Block() as block:
            @block.tensor
            def _(e):
                run("pe", e)

            @block.scalar
            def _(e):
                run("act", e)

            @block.vector
            def _(e):
                run("dve", e)

            @block.gpsimd
            def _(e):
                run("pool", e)

            @block.sync
            def _(e):
                run("sp", e)


def build_nc(stop_after=None, nseq=NSEQ):
    nc = bass.Bass("TRN2", target_bir_lowering=False)

    def din(name, shape):
        return nc.dram_tensor(name, list(shape), F32, kind="ExternalInput").ap()

    x = din("x", [nseq, SEQ, D])
    g1_d = din("g1", [128, 8])
    gm_d = din("gm", [128, 8])
    g2_d = din("g2", [128, 8])
    gfin_d = din("gfin", [128, D])
    bfb_d = din("bfb", [128, 128])
    cw_d = din("cw", [128, 12])
    w_g = [din("ffn1_gate", [D, DFF]), din("ffn2_gate", [D, DFF])]
    w_u = [din("ffn1_up", [D, DFF]), din("ffn2_up", [D, DFF])]
    w_d = [din("ffn1_down", [DFF, D]), din("ffn2_down", [DFF, D])]
    w_in = din("w_in", [D, 5128])
    w_oa = din("w_o_attn", [512, D])
    w_oc = din("w_o_conv", [512, D])
    w_out = din("w_out", [D, D])
    out = nc.dram_tensor("out", [nseq, SEQ, D], F32, kind="ExternalOutput").ap()

    S = Sched()
    es = ExitStack()
    with es:
        def sb(name, shape, dt):
            return es.enter_context(nc.sbuf_tensor(name, shape, dt))

        xT = sb("xT", [128, 8, SEQ], F32)
        hT = sb("hT", [128, 8, SEQ], BF16)
        ring = sb("ring", [128, NSLOT, 4096], BF16)
        O = sb("O", [128, 24576], BF16)
        attnT = sb("attnT", [128, 4, SEQ], BF16)
        ident = sb("ident", [128, 128], F32)
        triT = sb("triT", [128, 128], F32)
        onesF = sb("onesF", [128, 128], F32)
        identb = sb("identb", [128, 128], BF16)
        maskb = sb("maskb", [128, 128], BF16)
        negtri = sb("negtri", [128, 128], BF16)
        negones = sb("negones", [128, 128], BF16)
        onesb = sb("onesb", [128, 128], BF16)
        gt = [sb("g1t", [128, 8], F32), sb("gmt", [128, 8], F32), sb("g2t", [128, 8], F32)]
        cwt = sb("cwt", [128, 12], F32)
        bfb = sb("bfbt", [128, 128], F32)
        wf = sb("wf", [128, 8, 8], BF16)
        epst = sb("epst", [128, 1], F32)
        sst = sb("sst", [128, 8], F32)
        sqb = [sb("sqb0", [128, 512], BF16), sb("sqb1", [128, 512], BF16)]
        rstd = sb("rstd", [128, 512], F32)
        gfin = sb("gfin_t", [128, D], F32)
        PTx = sb("PTx", [128, 2, 512], BF16)
        rc2 = sb("rc2", [128, 512], F32)
        PS = es.enter_context(nc.psum_tensor("PS", [128, 8, 512], F32))

        sems = {e: es.enter_context(nc.semaphore("s_" + e)) for e in ENGINES}
        dma_keys = ["slot%d" % i for i in range(NSLOT)] + ["stg0", "stg1", "stg2", "stg3", "augA0", "augA1", "augB0", "augB1", "cst", "cstp", "gfin"]
        dsems = {k: es.enter_context(nc.semaphore("d_" + k)) for k in dma_keys}

        def Obf(a, n):
            return O[:, a:a + n]

        def Of32(a, n):
            return O[:, a:a + 2 * n].bitcast(F32)

        actT = O[:, 0:22528].rearrange("p (c t) -> p c t", c=22)
        sil = [Of32(22528, 512), Of32(23552, 512)]
        Vaug = O[:, 0:12288].rearrange("p (t j k d) -> p t j k d", t=16, j=4, k=3)
        qA = Obf(12288, 2048)
        qB = Obf(14336, 2048)
        kA = Obf(16384, 2048)
        kB = Obf(18432, 2048)
        PT = [Obf(20480 + 512 * i, 512) for i in range(3)] + [PTx[:, i, :] for i in range(2)]
        zt = Of32(22016, 128)
        spt = Of32(22272, 128)
        spx = Of32(22528, 128)
        sp_bf = Obf(22784, 128)
        spx_bf = Obf(22912, 128)
        Ck = Of32(23040, 128)
        rc = Of32(23296, 512)
        convin = O[:, 0:4096].rearrange("p (c t) -> p c t", c=4)
        merged = O[:, 4096:12288].rearrange("p (c t) -> p c t", c=8)
        ubuf = [Of32(12288 + 1028 * i, 514) for i in range(4)]
        cxs = Of32(16400, 512)
        tconv = Of32(17424, 512)
        sa_t = Of32(18448, 512)
        sc_t = Of32(19472, 512)
        m1_t = Of32(20496, 512)
        m2_t = Of32(21520, 512)

        B_xT = [[Buf("xT%d_%d" % (k, s)) for s in range(4)] for k in range(8)]
        B_hT = [[Buf("hT%d_%d" % (k, s)) for s in range(4)] for k in range(8)]
        B_slot = [Buf("slot%d" % i) for i in range(NSLOT)]
        B_bank = [Buf("bank%d" % i) for i in range(8)]
        B_act = [[Buf("act%d_%d" % (c, s)) for s in range(2)] for c in range(NCH)]
        B_sil = [Buf("sil0"), Buf("sil1")]
        B_stg = [Buf("stg%d" % i) for i in range(4)]
        stg = [attnT[:, i, :].bitcast(F32) for i in range(4)]
        stg_i = [0]

        def next_stg():
            i = stg_i[0] % 4
            stg_i[0] += 1
            return i
        B_ssf = [Buf("ssf0"), Buf("ssf1")]
        B_gfin = Buf("gfin")
        B_V = [Buf("V%d" % t) for t in range(16)]
        B_Vones = Buf("Vones")
        B_q = {n: [Buf("%s_%d" % (n, s)) for s in range(4)] for n in ("qA", "qB", "kA", "kB")}
        B_aug = {"A": Buf("augA"), "B": Buf("augB")}
        B_qaux = Buf("qaux")
        B_zero = Buf("qkzero")
        B_PT = [Buf("PT%d" % i) for i in range(6)]
        B_z, B_sp, B_spx, B_spbf, B_spxbf, B_Ck, B_rc = (Buf(n) for n in ("z", "sp", "spx", "spbf", "spxbf", "Ck", "rc"))
        B_rc2 = Buf("rc2")
        B_attn = [[Buf("attn%d_%d" % (j, s)) for s in range(4)] for j in range(4)]
        B_convin = [[Buf("cvi%d_%d" % (j, s)) for s in range(2)] for j in range(4)]
        B_merged = [[Buf("mg%d_%d" % (c, s)) for s in range(2)] for c in range(8)]
        B_u = [Buf("u%d" % j) for j in range(4)]
        B_cxs, B_tconv, B_sa, B_sc, B_m1, B_m2 = (Buf(n) for n in ("cxs", "tconv", "sa", "sc", "m1", "m2"))
        B_sq = [Buf("sq0"), Buf("sq1")]
        B_rstd = Buf("rstd")
        B_ss = Buf("ss")
        B_const = Buf("const")

        bank_i = [0]
        reserved = set()

        def next_bank(exclude=None, pool=None):
            while True:
                b = bank_i[0]
                bank_i[0] = (b + 1) % 8
                if b == exclude or b in reserved:
                    continue
                if pool is not None and b not in pool:
                    continue
                return b

        def next_pair():
            while True:
                if bank_i[0] % 2:
                    bank_i[0] = (bank_i[0] + 1) % 8
                b = bank_i[0]
                bank_i[0] = (b + 2) % 8
                if b in reserved or (b + 1) in reserved:
                    continue
                return b

        pending = []

        def pump(n=1):
            for _ in range(n):
                if pending:
                    pending.pop(0)()

        def flush():
            while pending:
                pending.pop(0)()

        job_i = [0]

        def wload(src, k, c):
            i = job_i[0] % NSLOT
            view = ring[:, i, 0:k * c].rearrange("p (k c) -> p k c", k=k)
            S.op("pool", lambda e: e.dma_start(out=view, in_=src), writes=[B_slot[i]],
                 dma_sem="slot%d" % i, exempt=True)
            B_slot[i].owner = job_i[0]
            tok = (B_slot[i], job_i[0])
            job_i[0] += 1
            return view, tok

        def wload_multi(nelem, parts):
            i = job_i[0] % NSLOT
            flat = ring[:, i, 0:nelem]
            prev = None
            for k, (dstf, src) in enumerate(parts):
                dst = dstf(flat)
                if k == 0:
                    prev = S.op("pool", lambda e, dst=dst, src=src: e.dma_start(out=dst, in_=src), writes=[B_slot[i]],
                                dma_sem="slot%d" % i, exempt=True)
                else:
                    prev = S.op("pool", lambda e, dst=dst, src=src: e.dma_start(out=dst, in_=src),
                                dma_sem="slot%d" % i, exempt=True)
                    B_slot[i].last_w = prev
            B_slot[i].owner = job_i[0]
            tok = (B_slot[i], job_i[0])
            job_i[0] += 1
            return flat, tok

        def wload2(src_a, src_b):
            flat, tok = wload_multi(4096, [
                (lambda f: f.rearrange("p (k c) -> p k c", k=8)[:, 0:4, :], src_a),
                (lambda f: f.rearrange("p (k c) -> p k c", k=8)[:, 4:8, :], src_b)])
            return flat.rearrange("p (k c) -> p k c", k=8), tok

        def wsrc(w, r0, r1, c0, c1):
            return w[r0:r1, c0:c1].rearrange("(k p) c -> p k c", p=128)

        def mm_group(out_ap, pairs, bankbuf):
            n = len(pairs)
            for i, (l, r, rd0) in enumerate(pairs):
                rd = []
                for b in rd0:
                    if isinstance(b, tuple):
                        assert b[0].owner == b[1], "weight slot reused while still live: %s" % b[0].name
                        b = b[0]
                    rd.append(b)
                S.op("pe", lambda e, l=l, r=r, i=i: e.matmul(out_ap, lhsT=l, rhs=r, start=(i == 0), stop=(i == n - 1)),
                     reads=rd, writes=[bankbuf])

        cst_ops = []
        for dst, src in ((gt[0], g1_d), (gt[1], gm_d), (gt[2], g2_d), (cwt, cw_d), (bfb, bfb_d)):
            S.op("sp", lambda e, dst=dst, src=src: e.dma_start(out=dst[:], in_=src), writes=[B_const], dma_sem="cst")
        S.op("pool", lambda e: e.dma_start(out=wf[:], in_=w_in[:, 1536:1544].rearrange("(k p) c -> p k c", p=128)),
             writes=[B_const], dma_sem="cstp")
        S.op("sp", lambda e: e.dma_start(out=gfin[:], in_=gfin_d), writes=[B_gfin], dma_sem="gfin")
        S.op("pool", lambda e: e.memset(ident[:], 1.0), writes=[B_const])
        S.op("pool", lambda e: e.affine_select(out=ident[:], in_=ident[:], pattern=[[-1, 128]], compare_op=ALU.is_equal,
                                               fill=0.0, base=0, channel_multiplier=1), reads=[B_const], writes=[B_const])
        S.op("pool", lambda e: e.memset(onesF[:], 1.0), writes=[B_const])
        S.op("pool", lambda e: e.memset(onesb[:], 1.0), writes=[B_const])
        S.op("pool", lambda e: e.memset(negones[:], -1.0), writes=[B_const])
        S.op("pool", lambda e: e.memset(epst[:], EPS), writes=[B_const])
        S.op("pool", lambda e: e.affine_select(out=triT[:], in_=onesF[:], pattern=[[1, 128]], compare_op=ALU.is_ge,
                                               fill=0.0, base=0, channel_multiplier=-1), reads=[B_const], writes=[B_const])
        S.op("dve", lambda e: e.tensor_scalar(out=negtri[:], in0=triT[:], scalar1=-1.0, scalar2=None, op0=ALU.mult),
             reads=[B_const], writes=[B_const])
        S.op("dve", lambda e: e.tensor_scalar(out=maskb[:], in0=triT[:], scalar1=-MASKVAL, scalar2=MASKVAL,
                                              op0=ALU.mult, op1=ALU.add), reads=[B_const], writes=[B_const])
        S.op("dve", lambda e: e.tensor_copy(out=identb[:], in_=ident[:]), reads=[B_const], writes=[B_const])

        def x_steps(q, tts):
            st = {"tr": [], "ev": None}

            def step(tt):
                if st["ev"] is not None:
                    st["ev"]()
                    st["ev"] = None
                if len(st["tr"]) >= 2 or (tt is None and st["tr"]):
                    st["ev"] = st["tr"].pop(0)()
                if tt is None:
                    return
                si = next_stg()
                S.op("sp", lambda e: e.dma_start(out=stg[si], in_=x[q, tt * 128:(tt + 1) * 128, :]),
                     writes=[B_stg[si]], dma_sem="stg%d" % si)

                def tr():
                    bp = next_pair()
                    reserved.add(bp)
                    reserved.add(bp + 1)
                    for kc in range(8):
                        S.op("pe", lambda e, kc=kc: e.transpose(
                            out=PS[:, bp + kc // 4, (kc % 4) * 128:(kc % 4 + 1) * 128],
                            in_=stg[si][:, kc * 128:(kc + 1) * 128], identity=ident[:]),
                            reads=[B_stg[si], B_const], writes=[B_bank[bp + kc // 4]])

                    def ev():
                        s_ = tt // 4
                        S.op("act", lambda e: e.activation(
                            out=xT[:, 0:4, tt * 128:(tt + 1) * 128], in_=PS[:, bp, :].rearrange("p (k t) -> p k t", k=4),
                            func=AF.Copy), reads=[B_bank[bp]], writes=[B_xT[k][s_] for k in range(4)])
                        S.op("dve", lambda e: e.tensor_copy(
                            out=xT[:, 4:8, tt * 128:(tt + 1) * 128], in_=PS[:, bp + 1, :].rearrange("p (k t) -> p k t", k=4)),
                            reads=[B_bank[bp + 1]], writes=[B_xT[k][s_] for k in range(4, 8)])
                        reserved.discard(bp)
                        reserved.discard(bp + 1)
                    return ev
                st["tr"].append(tr)

            for tt in tts:
                pending.append(lambda tt=tt: step(tt))
            for _ in range(3):
                pending.append(lambda: step(None))

        def final_steps(q, tts):
            tts = list(tts)
            assert len(tts) == 8
            st = {"post": None}

            def transposes(tt):
                s_ = tt // 4
                bp = next_pair()
                reserved.add(bp)
                reserved.add(bp + 1)
                for kc in range(8):
                    S.op("pe", lambda e, kc=kc: e.transpose(
                        out=PS[:, bp + kc // 4, (kc % 4) * 128:(kc % 4 + 1) * 128],
                        in_=xT[:, kc, tt * 128:(tt + 1) * 128], identity=ident[:]),
                        reads=[B_xT[kc][s_], B_const], writes=[B_bank[bp + kc // 4]])
                return bp

            def step1(tt, i):
                if st["post"] is not None:
                    st["post"]()
                    st["post"] = None
                if tt is None:
                    return
                bp = transposes(tt)

                def post():
                    pv = PS[:, bp:bp + 2, :]
                    S.op("act", lambda e: e.activation(out=pv, in_=pv, func=AF.Square, accum_out=sst[:, i:i + 1]),
                         reads=[B_bank[bp], B_bank[bp + 1]], writes=[B_bank[bp], B_bank[bp + 1], B_ssf[0]])
                    reserved.discard(bp)
                    reserved.discard(bp + 1)
                st["post"] = post

            def rstd_step():
                S.op("act", lambda e: e.activation(out=sst[:, 0:8], in_=sst[:, 0:8], func=AF.Sqrt, scale=1.0 / D,
                                                   bias=epst[:, 0:1]), reads=[B_ssf[0], B_const], writes=[B_ssf[0]])
                S.op("dve", lambda e: e.reciprocal(out=sst[:, 0:8], in_=sst[:, 0:8]), reads=[B_ssf[0]], writes=[B_ssf[0]])

            def step2(tt, i):
                if st["post"] is not None:
                    st["post"]()
                    st["post"] = None
                if tt is None:
                    return
                bp = transposes(tt)

                def post():
                    pv = PS[:, bp:bp + 2, :]
                    si = next_stg()
                    ov = stg[si].rearrange("p (a b) -> p a b", a=2)
                    S.op("dve", lambda e: e.scalar_tensor_tensor(
                        out=ov, in0=pv, scalar=sst[:, i:i + 1], in1=gfin[:].rearrange("p (a b) -> p a b", a=2),
                        op0=ALU.mult, op1=ALU.mult),
                        reads=[B_bank[bp], B_bank[bp + 1], B_ssf[0], B_gfin], writes=[B_stg[si]])
                    S.op("sp", lambda e: e.dma_start(out=out[q, tt * 128:(tt + 1) * 128, :], in_=stg[si]),
                         reads=[B_stg[si]], dma_sem="stg%d" % si)
                    reserved.discard(bp)
                    reserved.discard(bp + 1)
                st["post"] = post

            for i, tt in enumerate(tts):
                pending.append(lambda tt=tt, i=i: step1(tt, i))
            pending.append(lambda: step1(None, 0))
            pending.append(rstd_step)
            for i, tt in enumerate(tts):
                pending.append(lambda tt=tt, i=i: step2(tt, i))
            pending.append(lambda: step2(None, 0))

        def final_steps_single(q, tts):
            for n, tt in enumerate(tts):
                def step(tt=tt, n=n):
                    s_ = tt // 4
                    i = n % 2
                    c0 = 4 * i
                    bp = next_pair()
                    for kc in range(8):
                        S.op("pe", lambda e, kc=kc: e.transpose(
                            out=PS[:, bp + kc // 4, (kc % 4) * 128:(kc % 4 + 1) * 128],
                            in_=xT[:, kc, tt * 128:(tt + 1) * 128], identity=ident[:]),
                            reads=[B_xT[kc][s_], B_const], writes=[B_bank[bp + kc // 4]])
                    pv = PS[:, bp:bp + 2, :]
                    si = next_stg()
                    ov = stg[si].rearrange("p (a b) -> p a b", a=2)
                    S.op("act", lambda e: e.activation(out=ov, in_=pv, func=AF.Square, accum_out=sst[:, c0:c0 + 1]),
                         reads=[B_bank[bp], B_bank[bp + 1]], writes=[B_stg[si], B_ssf[i]])
                    S.op("act", lambda e: e.activation(out=sst[:, c0 + 1:c0 + 2], in_=sst[:, c0:c0 + 1], func=AF.Sqrt,
                                                       scale=1.0 / D, bias=epst[:, 0:1]),
                         reads=[B_ssf[i], B_const], writes=[B_ssf[i]])
                    S.op("dve", lambda e: e.reciprocal(out=sst[:, c0 + 2:c0 + 3], in_=sst[:, c0 + 1:c0 + 2]),
                         reads=[B_ssf[i]], writes=[B_ssf[i]])
                    S.op("dve", lambda e: e.scalar_tensor_tensor(
                        out=ov, in0=pv, scalar=sst[:, c0 + 2:c0 + 3], in1=gfin[:].rearrange("p (a b) -> p a b", a=2),
                        op0=ALU.mult, op1=ALU.mult),
                        reads=[B_bank[bp], B_bank[bp + 1], B_ssf[i], B_gfin], writes=[B_stg[si]])
                    S.op("sp", lambda e: e.dma_start(out=out[q, tt * 128:(tt + 1) * 128, :], in_=stg[si]),
                         reads=[B_stg[si]], dma_sem="stg%d" % si)
                pending.append(step)

        def norm(spans, g):
            for s in spans:
                st = {}

                def sq_step(kcs, s=s, st=st):
                    if "bs" not in st:
                        st["bs"] = next_bank()
                        reserved.add(st["bs"])
                        st["mm"] = []
                    bs = st["bs"]
                    for f in st["mm"]:
                        f()
                    st["mm"] = []
                    for kc in kcs:
                        i = kc % 2
                        src = xT[:, kc, s * 512:(s + 1) * 512]
                        if kc % 2 == 0:
                            S.op("act", lambda e, i=i, src=src: e.activation(out=sqb[i][:], in_=src, func=AF.Square),
                                 reads=[B_xT[kc][s]], writes=[B_sq[i]])
                        else:
                            S.op("dve", lambda e, i=i, src=src: e.tensor_tensor(out=sqb[i][:], in0=src, in1=src, op=ALU.mult),
                                 reads=[B_xT[kc][s]], writes=[B_sq[i]])
                        st["mm"].append(lambda i=i, kc=kc, bs=bs: S.op(
                            "pe", lambda e: e.matmul(PS[:, bs, :], lhsT=onesb[:], rhs=sqb[i][:], start=(kc == 0), stop=(kc == 7)),
                            reads=[B_sq[i], B_const], writes=[B_bank[bs]]))

                def rstd_step(s=s, st=st):
                    bs = st["bs"]
                    for f in st["mm"]:
                        f()
                    st["mm"] = []
                    S.op("act", lambda e, bs=bs: e.activation(out=rstd[:], in_=PS[:, bs, :], func=AF.Ln,
                                                              scale=1.0 / D, bias=epst[:, 0:1]),
                         reads=[B_bank[bs], B_const], writes=[B_rstd])
                    reserved.discard(bs)
                    S.op("act", lambda e: e.activation(out=rstd[:], in_=rstd[:], func=AF.Exp, scale=-0.5),
                         reads=[B_rstd], writes=[B_rstd])

                def h_step(kcs, s=s):
                    for kc in kcs:
                        S.op("dve", lambda e, kc=kc, s=s: e.scalar_tensor_tensor(
                            out=hT[:, kc, s * 512:(s + 1) * 512], in0=xT[:, kc, s * 512:(s + 1) * 512], scalar=g[:, kc:kc + 1],
                            in1=rstd[:], op0=ALU.mult, op1=ALU.mult),
                            reads=[B_xT[kc][s], B_rstd, B_const], writes=[B_hT[kc][s]])

                for a in range(4):
                    pending.append(lambda a=a, f=sq_step: f([2 * a, 2 * a + 1]))
                pending.append(rstd_step)
                for a in range(4):
                    pending.append(lambda a=a, f=h_step: f([2 * a, 2 * a + 1]))

        def ffn(which, h, g, first=False, s_outer=False, on_span_done=None):
            spans = [2 * h, 2 * h + 1]
            wg, wu, wd = w_g[which], w_u[which], w_d[which]
            silc = 0
            it = 0
            for grp in range(6):
                c0 = grp * 512
                c1 = min(c0 + 512, DFF)
                Gv, Gb = wload(wsrc(wg, 0, D, c0, c1), 8, c1 - c0)
                Uv, Ub = wload(wsrc(wu, 0, D, c0, c1), 8, c1 - c0)
                for sl, s in enumerate(spans):
                    for cl in range((c1 - c0) // 128):
                        c = grp * 4 + cl
                        bg = next_bank()
                        mm_group(PS[:, bg, :], [(Gv[:, kc, cl * 128:(cl + 1) * 128], hT[:, kc, s * 512:(s + 1) * 512],
                                                 [Gb, B_hT[kc][s]]) for kc in range(8)], B_bank[bg])
                        bu = next_bank()
                        mm_group(PS[:, bu, :], [(Uv[:, kc, cl * 128:(cl + 1) * 128], hT[:, kc, s * 512:(s + 1) * 512],
                                                 [Ub, B_hT[kc][s]]) for kc in range(8)], B_bank[bu])
                        si = silc % 2
                        silc += 1
                        S.op("act", lambda e, si=si, bg=bg: e.activation(out=sil[si], in_=PS[:, bg, :], func=AF.Silu),
                             reads=[B_bank[bg]], writes=[B_sil[si]])
                        S.op("dve", lambda e, si=si, bu=bu, c=c, sl=sl: e.tensor_tensor(
                            out=actT[:, c, sl * 512:(sl + 1) * 512], in0=PS[:, bu, :], in1=sil[si], op=ALU.mult),
                            reads=[B_bank[bu], B_sil[si]], writes=[B_act[c][sl]])
                        pump(3 if (first and it < 4) else 1)
                        it += 1

            def down_unit(cq, Dv, sl, s):
                for dcl in range(2):
                    dc = cq * 2 + dcl
                    bd = next_bank()
                    mm_group(PS[:, bd, :], [(Dv[c // 11][0][:, c % 11, dcl * 128:(dcl + 1) * 128],
                                             actT[:, c, sl * 512:(sl + 1) * 512],
                                             [Dv[c // 11][1], B_act[c][sl]]) for c in range(NCH)], B_bank[bd])
                    S.op("dve", lambda e, bd=bd, dc=dc, s=s: e.scalar_tensor_tensor(
                        out=xT[:, dc, s * 512:(s + 1) * 512], in0=PS[:, bd, :], scalar=0.5,
                        in1=xT[:, dc, s * 512:(s + 1) * 512], op0=ALU.mult, op1=ALU.add),
                        reads=[B_bank[bd], B_xT[dc][s]], writes=[B_xT[dc][s]])
                    pump(1)

            def load_D(cq):
                Dv = []
                for rh in range(2):
                    r0 = rh * 11 * 128
                    Dv.append(wload(wsrc(wd, r0, r0 + 11 * 128, cq * 256, (cq + 1) * 256), 11, 256))
                return Dv

            if s_outer:
                for sl, s in enumerate(spans):
                    for cq in range(4):
                        Dv = load_D(cq)
                        down_unit(cq, Dv, sl, s)
                    if on_span_done is not None:
                        on_span_done(s)
            else:
                for cq in range(4):
                    Dv = load_D(cq)
                    for sl, s in enumerate(spans):
                        down_unit(cq, Dv, sl, s)

        def mixer_attention(q):
            Wv, Wvb = wload(wsrc(w_in, 0, D, 1024, 1536), 8, 512)
            Wq, Wqb = wload(wsrc(w_in, 0, D, 0, 512), 8, 512)
            S.op("pool", lambda e: e.memset(qA[64:128, :], 0.0), writes=[B_zero])
            S.op("pool", lambda e: e.memset(qB[0:64, :], 0.0), writes=[B_zero])
            S.op("pool", lambda e: e.memset(kA[64:96, :], 1.0), writes=[B_zero])
            S.op("pool", lambda e: e.memset(kA[96:128, :], 0.0), writes=[B_zero])
            S.op("pool", lambda e: e.memset(kB[0:32, :], 0.0), writes=[B_zero])
            S.op("pool", lambda e: e.memset(kB[32:64, :], 1.0), writes=[B_zero])
            S.op("pool", lambda e: e.memset(Vaug[:, :, :, 1, :], 1.0), writes=[B_Vones])
            Wk, Wkb = wload(wsrc(w_in, 0, D, 512, 1024), 8, 512)
            norm([2, 3], gt[1])
            for tt in range(16):
                s = tt // 4
                bv = next_bank()
                mm_group(PS[:, bv, :], [(hT[:, kc, tt * 128:(tt + 1) * 128], Wv[:, kc, :], [Wvb, B_hT[kc][s]])
                                        for kc in range(8)], B_bank[bv])
                src = PS[:, bv, :].rearrange("p (a b c) -> p a b c", a=4, b=2)
                dst = Vaug[:, tt, :, 0:3:2, :]
                if tt % 2 == 0:
                    S.op("act", lambda e, src=src, dst=dst: e.activation(out=dst, in_=src, func=AF.Copy),
                         reads=[B_bank[bv]], writes=[B_V[tt]])
                else:
                    S.op("dve", lambda e, src=src, dst=dst: e.tensor_copy(out=dst, in_=src),
                         reads=[B_bank[bv]], writes=[B_V[tt]])
                if tt < 8:
                    pump(3)
                if tt == 7:
                    flush()
            bf_ = next_bank()
            for tt in range(16):
                s = tt // 4
                mm_group(PS[:, bf_, tt * 8:(tt + 1) * 8], [(hT[:, kc, tt * 128:(tt + 1) * 128], wf[:, kc, :],
                                                             [B_const, B_hT[kc][s]]) for kc in range(8)], B_bank[bf_])
            S.op("dve", lambda e: e.tensor_tensor(out=zt, in0=PS[:, bf_, 0:128], in1=bfb[:], op=ALU.add),
                 reads=[B_bank[bf_], B_const], writes=[B_z])
            S.op("act", lambda e: e.activation(out=zt, in_=zt, func=AF.Exp, scale=-1.0), reads=[B_z], writes=[B_z])
            S.op("act", lambda e: e.activation(out=spt, in_=zt, func=AF.Ln, bias=1.0), reads=[B_z], writes=[B_sp])
            S.op("dve", lambda e: e.memset(spx[:, 0:8], 0.0), writes=[B_spx])
            for tt in range(1, 16):
                S.op("dve", lambda e, tt=tt: e.tensor_tensor(out=spx[:, tt * 8:(tt + 1) * 8], in0=spx[:, (tt - 1) * 8:tt * 8],
                                                              in1=spt[:, (tt - 1) * 8:tt * 8], op=ALU.add),
                     reads=[B_sp, B_spx], writes=[B_spx])
            S.op("dve", lambda e: e.tensor_copy(out=sp_bf, in_=spt), reads=[B_sp], writes=[B_spbf])
            S.op("dve", lambda e: e.tensor_copy(out=spx_bf, in_=spx), reads=[B_spx], writes=[B_spxbf])
            bc = next_bank()
            S.op("pe", lambda e: e.matmul(PS[:, bc, 0:128], lhsT=triT[:], rhs=spt, start=True, stop=False),
                 reads=[B_sp, B_const], writes=[B_bank[bc]])
            S.op("pe", lambda e: e.matmul(PS[:, bc, 0:128], lhsT=onesF[:], rhs=spx, start=False, stop=True),
                 reads=[B_spx, B_const], writes=[B_bank[bc]])
            S.op("dve", lambda e: e.tensor_copy(out=Ck, in_=PS[:, bc, 0:128]), reads=[B_bank[bc]], writes=[B_Ck])
            for s in range(4):
                bn = next_bank()
                for kbl in range(4):
                    kb = 4 * s + kbl
                    S.op("pe", lambda e, kb=kb, kbl=kbl, bn=bn: e.matmul(
                        PS[0:8, bn, kbl * 128:(kbl + 1) * 128], lhsT=sp_bf[:, kb * 8:(kb + 1) * 8], rhs=negtri[:],
                        start=True, stop=False), reads=[B_spbf, B_const], writes=[B_bank[bn]])
                    S.op("pe", lambda e, kb=kb, kbl=kbl, bn=bn: e.matmul(
                        PS[0:8, bn, kbl * 128:(kbl + 1) * 128], lhsT=spx_bf[:, kb * 8:(kb + 1) * 8], rhs=negones[:],
                        start=False, stop=True), reads=[B_spxbf, B_const], writes=[B_bank[bn]])
                S.op("dve", lambda e, s=s, bn=bn: e.tensor_copy(out=qA[96:104, s * 512:(s + 1) * 512], in_=PS[0:8, bn, :]),
                     reads=[B_bank[bn], B_zero], writes=[B_qaux])
            def borrow_slot():
                i = job_i[0] % NSLOT
                B_slot[i].owner = job_i[0]
                job_i[0] += 1
                return ring[:, i, :], B_slot[i]

            r0v, r0b = borrow_slot()
            r1v, r1b = borrow_slot()
            TS = [dict(qA=qA, qB=qB, kA=kA, kB=kB, gq=[], gk=[]),
                  dict(qA=r0v[:, 0:2048], qB=r0v[:, 2048:4096], kA=r1v[:, 0:2048], kB=r1v[:, 2048:4096], gq=[r0b], gk=[r1b])]
            B_q1 = {n: [Buf("%s1_%d" % (n, s)) for s in range(4)] for n in ("qA", "qB", "kA", "kB")}
            B_aug1 = {"A": Buf("augA1"), "B": Buf("augB1")}
            B_zero1 = Buf("qkzero1")
            BQ = [B_q, B_q1]
            BAUG = [B_aug, B_aug1]
            BZ = [B_zero, B_zero1]
            t1 = TS[1]
            S.op("pool", lambda e: e.memset(t1["qA"][64:128, :], 0.0), writes=[B_zero1] + t1["gq"])
            S.op("pool", lambda e: e.memset(t1["qB"][0:64, :], 0.0), writes=[B_zero1] + t1["gq"])
            S.op("pool", lambda e: e.memset(t1["kA"][64:96, :], 1.0), writes=[B_zero1] + t1["gk"])
            S.op("pool", lambda e: e.memset(t1["kA"][96:128, :], 0.0), writes=[B_zero1] + t1["gk"])
            S.op("pool", lambda e: e.memset(t1["kB"][0:32, :], 0.0), writes=[B_zero1] + t1["gk"])
            S.op("pool", lambda e: e.memset(t1["kB"][32:64, :], 1.0), writes=[B_zero1] + t1["gk"])

            pti = [0]
            oi = [0]
            SP_ = (0, 1, 2, 3, 4, 5)

            def proj_steps(j):
                ts = j % 2
                T = TS[ts]
                Bq_ = BQ[ts]
                Bz = BZ[ts]

                def first():
                    S.op("sp", lambda e: e.dma_start(out=T["qA"][64:65, :], in_=qA[96 + 2 * j:97 + 2 * j, :]),
                         reads=[B_qaux, Bz], writes=[BAUG[ts]["A"]] + T["gq"], dma_sem="augA%d" % ts)
                    S.op("sp", lambda e: e.dma_start(out=T["qB"][63:64, :], in_=qA[97 + 2 * j:98 + 2 * j, :]),
                         reads=[B_qaux, Bz], writes=[BAUG[ts]["B"]] + T["gq"], dma_sem="augB%d" % ts)

                def grp_steps(W, Wb, s, evac):
                    sl_ = slice(s * 512, (s + 1) * 512)
                    st = {}

                    def sub(a):
                        if a == 0:
                            st["b"] = next_bank(pool=SP_)
                            reserved.add(st["b"])
                        b = st["b"]
                        for kc in (2 * a, 2 * a + 1):
                            S.op("pe", lambda e, kc=kc: e.matmul(PS[:, b, :], lhsT=W[:, kc, j * 128:(j + 1) * 128],
                                                                  rhs=hT[:, kc, sl_], start=(kc == 0), stop=(kc == 7)),
                                 reads=[Wb[0], B_hT[kc][s]], writes=[B_bank[b]])
                        if a == 3:
                            reserved.discard(b)
                            evac(b, s, sl_)
                    return [lambda a=a: sub(a) for a in range(4)]

                def q_evac(bq, s, sl_):
                    S.op("dve", lambda e: e.tensor_scalar(out=T["qA"][0:64, sl_], in0=PS[0:64, bq, :],
                                                          scalar1=0.125, scalar2=None, op0=ALU.mult),
                         reads=[B_bank[bq], Bz], writes=[Bq_["qA"][s]] + T["gq"])
                    S.op("dve", lambda e: e.tensor_scalar(out=T["qB"][64:128, sl_], in0=PS[64:128, bq, :],
                                                          scalar1=0.125, scalar2=None, op0=ALU.mult),
                         reads=[B_bank[bq], Bz], writes=[Bq_["qB"][s]] + T["gq"])

                def k_evac(bk, s, sl_):
                    S.op("dve", lambda e: e.tensor_copy(out=T["kA"][0:64, sl_], in_=PS[0:64, bk, :]),
                         reads=[B_bank[bk], Bz], writes=[Bq_["kA"][s]] + T["gk"])
                    S.op("dve", lambda e: e.tensor_copy(out=T["kB"][64:128, sl_], in_=PS[64:128, bk, :]),
                         reads=[B_bank[bk], Bz], writes=[Bq_["kB"][s]] + T["gk"])

                pending.append(first)
                for s in range(4):
                    pending.extend(grp_steps(Wq, Wqb, s, q_evac))
                    pending.extend(grp_steps(Wk, Wkb, s, k_evac))

            proj_steps(0)
            flush()
            blk_cnt = [0]
            dve_defer = []
            for j in range(4):
                ts = j % 2
                T = TS[ts]
                if j + 1 < 4:
                    proj_steps(j + 1)
                for X in ("A", "B"):
                    h = 2 * j + (0 if X == "A" else 1)
                    qt = T["q" + X]
                    kt = T["k" + X]
                    Bq = BQ[ts]["q" + X]
                    Bk = BQ[ts]["k" + X]
                    Baug = BAUG[ts][X]
                    Bz = BZ[ts]
                    guards = T["gq"] + T["gk"]
                    for qs in range(4):
                        bo = 6 + (oi[0] % 2)
                        oi[0] += 1
                        nblk = 4 * qs + 4
                        pend = []

                        def emit_S(kb, qs=qs, h=h, qt=qt, kt=kt, Bq=Bq, Bk=Bk, Baug=Baug, Bz=Bz, guards=guards):
                            diag = kb >= 4 * qs
                            q0 = kb * 128 if diag else qs * 512
                            N = (qs + 1) * 512 - q0
                            bs_ = next_bank(pool=SP_)
                            pi = pti[0] % 5
                            pti[0] += 1
                            S.op("pe", lambda e: e.matmul(PS[:, bs_, 0:N], lhsT=kt[:, kb * 128:(kb + 1) * 128],
                                                          rhs=qt[:, q0:q0 + N], start=True, stop=not diag),
                                 reads=[Bk[kb // 4], Bq[qs], Baug, Bz] + guards, writes=[B_bank[bs_]])
                            if diag:
                                S.op("pe", lambda e: e.matmul(PS[:, bs_, 0:128], lhsT=identb[:], rhs=maskb[:],
                                                              start=False, stop=True),
                                     reads=[B_const], writes=[B_bank[bs_]])
                            S.op("act", lambda e: e.activation(out=PT[pi][:, 0:N], in_=PS[:, bs_, 0:N], func=AF.Exp,
                                                               bias=Ck[:, kb * 8 + h:kb * 8 + h + 1], scale=1.0),
                                 reads=[B_bank[bs_], B_Ck], writes=[B_PT[pi]])
                            return (kb, pi, q0, N)

                        def emit_PV(info, qs=qs, j=j, X=X, bo=bo, nblk=nblk):
                            kb, pi, q0, N = info
                            lo = q0 - qs * 512
                            vl = Vaug[:, kb, j, 0:2, :] if X == "A" else Vaug[:, kb, j, 1:3, :]
                            S.op("pe", lambda e: e.matmul(PS[:, bo, lo:512], lhsT=vl, rhs=PT[pi][:, 0:N],
                                                          start=(kb == 0), stop=(kb == nblk - 1)),
                                 reads=[B_PT[pi], B_V[kb], B_Vones], writes=[B_bank[bo]])

                        for kb in range(nblk):
                            pend.append(emit_S(kb))
                            if len(pend) > 4:
                                emit_PV(pend.pop(0))
                            if dve_defer:
                                dve_defer.pop(0)()
                            blk_cnt[0] += 1
                            if blk_cnt[0] % 2 == 0:
                                pump(1)
                        while pend:
                            emit_PV(pend.pop(0))
                        while dve_defer:
                            dve_defer.pop(0)()
                        sl_ = slice(qs * 512, (qs + 1) * 512)
                        if X == "A":
                            orow, drow = slice(0, 64), slice(64, 128)
                        else:
                            orow, drow = slice(64, 128), slice(0, 64)
                        if qs == 0:
                            S.op("act", lambda e, bo=bo, drow=drow: e.activation(out=rc2[drow, :], in_=PS[drow, bo, :], func=AF.Ln),
                                 reads=[B_bank[bo]], writes=[B_rc2])
                            S.op("act", lambda e, drow=drow: e.activation(out=rc2[drow, :], in_=rc2[drow, :], func=AF.Exp, scale=-1.0),
                                 reads=[B_rc2], writes=[B_rc2])
                            S.op("dve", lambda e, orow=orow, drow=drow: e.tensor_copy(out=rc2[orow, :], in_=rc2[drow, :]),
                                 reads=[B_rc2], writes=[B_rc2])
                            S.op("dve", lambda e, bo=bo, sl_=sl_, j=j, orow=orow: e.tensor_tensor(
                                out=attnT[orow, j, sl_], in0=PS[orow, bo, :], in1=rc2[orow, :], op=ALU.mult),
                                reads=[B_bank[bo], B_rc2], writes=[B_attn[j][qs]])
                        else:
                            for cp in range(4):
                                dve_defer.append(lambda bo=bo, orow=orow, drow=drow, cp=cp: S.op(
                                    "dve", lambda e: e.reciprocal(out=rc[orow, cp * 128:(cp + 1) * 128],
                                                                  in_=PS[drow, bo, cp * 128:(cp + 1) * 128]),
                                    reads=[B_bank[bo]], writes=[B_rc]))
                            dve_defer.append(lambda bo=bo, sl_=sl_, j=j, orow=orow, qs=qs: S.op(
                                "dve", lambda e: e.tensor_tensor(out=attnT[orow, j, sl_], in0=PS[orow, bo, :], in1=rc[orow, :],
                                                                  op=ALU.mult),
                                reads=[B_bank[bo], B_rc], writes=[B_attn[j][qs]]))
                while dve_defer:
                    dve_defer.pop(0)()
                flush()

        def mixer_post(q, hf):
            spans = [2 * hf, 2 * hf + 1]
            cv_src = w_in[:, 1544:3080].rearrange("(k p) (t j c) -> p k t j c", p=128, t=3, j=4)
            for jc in range(4):
                Wcv, Bcv = wload_multi(3072, [
                    (lambda f, t=t: f.rearrange("p (k t c) -> p k t c", k=8, t=3)[:, :, t, :], cv_src[:, :, t, jc, :])
                    for t in range(3)])
                Wcv = Wcv.rearrange("p (k t c) -> p k t c", k=8, t=3)
                for sl, s in enumerate(spans):
                    sl_ = slice(s * 512, (s + 1) * 512)
                    bx = next_bank()
                    mm_group(PS[:, bx, :], [(Wcv[:, kc, 2, :], hT[:, kc, sl_], [Bcv, B_hT[kc][s]]) for kc in range(8)], B_bank[bx])
                    bcc = next_bank()
                    mm_group(PS[:, bcc, :], [(Wcv[:, kc, 1, :], hT[:, kc, sl_], [Bcv, B_hT[kc][s]]) for kc in range(8)], B_bank[bcc])
                    bcb = next_bank()
                    mm_group(PS[:, bcb, :], [(Wcv[:, kc, 0, :], hT[:, kc, sl_], [Bcv, B_hT[kc][s]]) for kc in range(8)], B_bank[bcb])
                    u = ubuf[jc]
                    if s == 0:
                        S.op("dve", lambda e, u=u: e.memset(u[:, 0:2], 0.0), writes=[B_u[jc]])
                    S.op("act", lambda e, bx=bx: e.activation(out=cxs, in_=PS[:, bx, :], func=AF.Copy),
                         reads=[B_bank[bx]], writes=[B_cxs])
                    S.op("dve", lambda e, u=u, bcc=bcc: e.tensor_tensor(out=u[:, 2:514], in0=PS[:, bcc, :], in1=cxs, op=ALU.mult),
                         reads=[B_bank[bcc], B_cxs], writes=[B_u[jc]])
                    S.op("dve", lambda e, u=u, jc=jc: e.tensor_scalar(out=tconv, in0=u[:, 0:512], scalar1=cwt[:, jc * 3:jc * 3 + 1],
                                                                       scalar2=None, op0=ALU.mult),
                         reads=[B_u[jc], B_const], writes=[B_tconv])
                    S.op("dve", lambda e, u=u, jc=jc: e.scalar_tensor_tensor(
                        out=tconv, in0=u[:, 1:513], scalar=cwt[:, jc * 3 + 1:jc * 3 + 2], in1=tconv, op0=ALU.mult, op1=ALU.add),
                        reads=[B_u[jc], B_tconv, B_const], writes=[B_tconv])
                    S.op("dve", lambda e, u=u, jc=jc: e.scalar_tensor_tensor(
                        out=tconv, in0=u[:, 2:514], scalar=cwt[:, jc * 3 + 2:jc * 3 + 3], in1=tconv, op0=ALU.mult, op1=ALU.add),
                        reads=[B_u[jc], B_tconv, B_const], writes=[B_tconv])
                    S.op("dve", lambda e, bcb=bcb, jc=jc, sl=sl: e.tensor_tensor(
                        out=convin[:, jc, sl * 512:(sl + 1) * 512], in0=PS[:, bcb, :], in1=tconv, op=ALU.mult),
                        reads=[B_bank[bcb], B_tconv], writes=[B_convin[jc][sl]])
                    S.op("dve", lambda e, u=u: e.tensor_copy(out=u[:, 0:2], in_=u[:, 512:514]),
                         reads=[B_u[jc]], writes=[B_u[jc]])
                    pump(1)
            for mq in range(4):
                c0 = mq * 256
                gsrc = lambda base: w_in[:, base + c0:base + c0 + 256].rearrange("(k p) c -> p k c", p=128)
                Wgg, Bgg = wload_multi(4096, [
                    (lambda f: f[:, 0:2048].rearrange("p (k c) -> p k c", k=8), gsrc(3080)),
                    (lambda f: f[:, 2048:4096].rearrange("p (k c) -> p k c", k=8), gsrc(4104))])
                Wga = Wgg[:, 0:2048].rearrange("p (k c) -> p k c", k=8)
                Wgc = Wgg[:, 2048:4096].rearrange("p (k c) -> p k c", k=8)
                Bga = Bgc = Bgg
                Woo, Boca = wload_multi(2048, [
                    (lambda f: f[:, 0:1024].rearrange("p (k c) -> p k c", k=4),
                     w_oc[:, c0:c0 + 256].rearrange("(k p) c -> p k c", p=128)),
                    (lambda f: f[:, 1024:2048].rearrange("p (k c) -> p k c", k=4),
                     w_oa[:, c0:c0 + 256].rearrange("(k p) c -> p k c", p=128))])
                Woca = Woo.rearrange("p (k c) -> p k c", k=8)
                for cl in range(2):
                    c = 2 * mq + cl
                    for sl, s in enumerate(spans):
                        sl_ = slice(s * 512, (s + 1) * 512)
                        ll = slice(sl * 512, (sl + 1) * 512)
                        b_ga = next_bank()
                        mm_group(PS[:, b_ga, :], [(Wga[:, kc, cl * 128:(cl + 1) * 128], hT[:, kc, sl_], [Bga, B_hT[kc][s]])
                                                  for kc in range(8)], B_bank[b_ga])
                        b_gc = next_bank()
                        mm_group(PS[:, b_gc, :], [(Wgc[:, kc, cl * 128:(cl + 1) * 128], hT[:, kc, sl_], [Bgc, B_hT[kc][s]])
                                                  for kc in range(8)], B_bank[b_gc])
                        b_ya = next_bank()
                        mm_group(PS[:, b_ya, :], [(Woca[:, 4 + kc, cl * 128:(cl + 1) * 128], attnT[:, kc, sl_], [Boca, B_attn[kc][s]])
                                                  for kc in range(4)], B_bank[b_ya])
                        b_yc = next_bank()
                        mm_group(PS[:, b_yc, :], [(Woca[:, kc, cl * 128:(cl + 1) * 128], convin[:, kc, ll], [Boca, B_convin[kc][sl]])
                                                  for kc in range(4)], B_bank[b_yc])
                        S.op("act", lambda e, b=b_ga: e.activation(out=sa_t, in_=PS[:, b, :], func=AF.Sigmoid),
                             reads=[B_bank[b_ga]], writes=[B_sa])
                        S.op("act", lambda e, b=b_gc: e.activation(out=sc_t, in_=PS[:, b, :], func=AF.Sigmoid),
                             reads=[B_bank[b_gc]], writes=[B_sc])
                        S.op("dve", lambda e, b=b_ya: e.tensor_tensor(out=m1_t, in0=PS[:, b, :], in1=sa_t, op=ALU.mult),
                             reads=[B_bank[b_ya], B_sa], writes=[B_m1])
                        S.op("dve", lambda e, b=b_yc: e.tensor_tensor(out=m2_t, in0=PS[:, b, :], in1=sc_t, op=ALU.mult),
                             reads=[B_bank[b_yc], B_sc], writes=[B_m2])
                        dbg = os.environ.get("MK_DBG", "")
                        if dbg == "noattn":
                            S.op("dve", lambda e, c=c, ll=ll: e.tensor_copy(out=merged[:, c, ll], in_=m2_t),
                                 reads=[B_m1, B_m2], writes=[B_merged[c][sl]])
                        elif dbg == "noconv":
                            S.op("dve", lambda e, c=c, ll=ll: e.tensor_copy(out=merged[:, c, ll], in_=m1_t),
                                 reads=[B_m1, B_m2], writes=[B_merged[c][sl]])
                        else:
                            S.op("dve", lambda e, c=c, ll=ll: e.tensor_tensor(out=merged[:, c, ll], in0=m1_t, in1=m2_t, op=ALU.add),
                                 reads=[B_m1, B_m2], writes=[B_merged[c][sl]])
                        pump(1)
            for ch in range(2):
                Wo, Bo = wload(wsrc(w_out, 0, D, ch * 512, (ch + 1) * 512), 8, 512)
                for dcl in range(4):
                    dc = 4 * ch + dcl
                    for sl, s in enumerate(spans):
                        sl_ = slice(s * 512, (s + 1) * 512)
                        ll = slice(sl * 512, (sl + 1) * 512)
                        bd = next_bank()
                        mm_group(PS[:, bd, :], [(Wo[:, kc, dcl * 128:(dcl + 1) * 128], merged[:, kc, ll], [Bo, B_merged[kc][sl]])
                                                for kc in range(8)], B_bank[bd])
                        S.op("dve", lambda e, bd=bd, dc=dc, sl_=sl_: e.tensor_tensor(
                            out=xT[:, dc, sl_], in0=PS[:, bd, :], in1=xT[:, dc, sl_], op=ALU.add),
                            reads=[B_bank[bd], B_xT[dc][s]], writes=[B_xT[dc][s]])

        order = ["X", "F1", "ATT", "POST", "F2"]
        lim = order.index(stop_after) if stop_after else len(order) - 1
        tail_done = [False]
        for q in range(nseq):
            if q == 0:
                x_steps(0, range(0, 8))
                if lim >= 1:
                    norm([0], gt[0])
                flush()
                if lim >= 1:
                    norm([1], gt[0])
            else:
                flush()
                final_steps(q - 1, range(8, 16))
            x_steps(q, range(8, 16))
            if lim >= 1:
                norm([2, 3], gt[0])
                ffn(0, 0, gt[0], first=(q == 0))
                flush()
                if lim >= 2:
                    norm([0, 1], gt[1])
                ffn(0, 1, gt[0])
            flush()
            if lim >= 2:
                S.barrier(["stg0", "stg1", "stg2", "stg3"])
                mixer_attention(q)
            if lim >= 3:
                S.barrier(["augA0", "augA1", "augB0", "augB1"])
                mixer_post(q, 0)
                if lim >= 4:
                    norm([0, 1], gt[2])
                mixer_post(q, 1)
                flush()
            if lim >= 4:
                S.barrier()
                norm([2, 3], gt[2])
                ffn(1, 0, gt[2])
                flush()
                final_steps(q, range(0, 8))
                if q + 1 < nseq:
                    x_steps(q + 1, range(0, 8))
                    norm([0, 1], gt[0])
                if q + 1 < nseq:
                    ffn(1, 1, gt[2])
                else:
                    ffn(1, 1, gt[2], s_outer=True,
                        on_span_done=lambda s_, q=q: final_steps_single(q, range(4 * s_, 4 * s_ + 4)))
                    tail_done[0] = True
            else:
                final_steps(q, range(0, 8))
                if q + 1 < nseq:
                    x_steps(q + 1, range(0, 8))
                    if lim >= 1:
                        norm([0, 1], gt[0])
        flush()
        if not tail_done[0]:
            final_steps_single(nseq - 1, range(8, 16))
            flush()
        S.op("sp", lambda e: e.nop(), extra_deps=[S.dma_last[k] for k in ("stg0", "stg1", "stg2", "stg3") if k in S.dma_last])
        S.emit(nc, sems, dsems)
    return nc


_NC_CACHE = {}


def _prep_inputs(inputs, n_cores=8):
    f = lambda a: np.ascontiguousarray(np.asarray(a, dtype=np.float32))
    x = f(inputs["x"])
    per = x.shape[0] // n_cores
    pk = lambda g: np.ascontiguousarray(f(g).reshape(8, 128).T)
    shared = {
        "g1": pk(inputs["ffn1_norm"]), "gm": pk(inputs["mix_norm"]), "g2": pk(inputs["ffn2_norm"]),
        "gfin": np.ascontiguousarray(np.broadcast_to(f(inputs["final_norm"])[None, :], (128, D))),
        "bfb": np.ascontiguousarray(np.broadcast_to(np.tile(f(inputs["b_forget"]), 16)[None, :], (128, 128))),
        "cw": np.ascontiguousarray(f(inputs["conv_w"]).reshape(3, 4, 128).transpose(2, 1, 0).reshape(128, 12)),
    }
    for k in ("ffn1_gate", "ffn1_up", "ffn1_down", "ffn2_gate", "ffn2_up", "ffn2_down", "w_in", "w_o_attn",
              "w_o_conv", "w_out"):
        shared[k] = f(inputs[k])
    in_maps = []
    for c in range(n_cores):
        m = dict(shared)
        m["x"] = np.ascontiguousarray(x[c * per:(c + 1) * per])
        in_maps.append(m)
    return in_maps


def kernel(**inputs):
    n_cores = 8
    stop = os.environ.get("MK_STOP") or None
    key = stop
    if key not in _NC_CACHE:
        _NC_CACHE[key] = build_nc(stop_after=stop)
    nc = _NC_CACHE[key]
    in_maps = _prep_inputs(inputs, n_cores)
    res = run_bass_kernel_spmd(nc, in_maps, core_ids=list(range(n_cores)))
    return np.concatenate([np.asarray(r["out"]) for r in res.results], axis=0).astype(np.float32)
```

```python
import os
from contextlib import ExitStack
import numpy as np
import concourse.bass as bass
import concourse.mybir as mybir
from concourse.bass_utils import run_bass_kernel_spmd

F32 = mybir.dt.float32
BF16 = mybir.dt.bfloat16
AF = mybir.ActivationFunctionType
ALU = mybir.AluOpType

ENGINES = ("pe", "act", "dve", "pool", "sp")
NSEQ = 2
SEQ = 2048
D = 1024
DFF = 2816
NCH = 22
NSLOT = 4
EPS = 1e-6
MASKVAL = -30000.0


class Buf:
    __slots__ = ("name", "last_w", "reads", "owner")

    def __init__(self, name):
        self.name = name
        self.last_w = None
        self.reads = {}
        self.owner = None


class Op:
    __slots__ = ("eng", "idx", "fn", "waits", "signal", "sigval", "dma_sem", "is_dma", "order")

    def __init__(self, eng, idx, fn, dma_sem=None):
        self.eng = eng
        self.idx = idx
        self.fn = fn
        self.waits = []
        self.signal = False
        self.sigval = None
        self.is_dma = dma_sem is not None
        self.dma_sem = dma_sem
        self.order = idx


class Sched:
    def __init__(self):
        self.prog = {e: [] for e in ENGINES}
        self.waited = {e: {} for e in ENGINES}
        self.dma_count = {}
        self.dma_last = {}
        self.same_engine_sync = {"act", "dve", "pool"}
        self.barrier_deps = []

    def op(self, eng, fn, reads=(), writes=(), dma_sem=None, extra_deps=(), exempt=False):
        o = Op(eng, len(self.prog[eng]), fn, dma_sem=dma_sem)
        if o.is_dma:
            self.dma_count[dma_sem] = self.dma_count.get(dma_sem, 0) + 1
            o.order = self.dma_count[dma_sem]
            self.dma_last[dma_sem] = o
        deps = []
        for b in reads:
            if b.last_w is not None:
                deps.append(b.last_w)
        for b in writes:
            if b.last_w is not None:
                deps.append(b.last_w)
            deps.extend(b.reads.values())
        deps.extend(extra_deps)
        if not exempt:
            deps.extend(self.barrier_deps)
        w = self.waited[eng]
        for d in deps:
            if d is o:
                continue
            if (not d.is_dma) and d.eng == eng and eng not in self.same_engine_sync:
                continue
            key = ("dma", d.dma_sem) if d.is_dma else ("eng", d.eng)
            if w.get(key, -1) >= d.order:
                continue
            w[key] = d.order
            d.signal = True
            o.waits.append(d)
        self.prog[eng].append(o)
        rkey = ("dma", dma_sem) if o.is_dma else eng
        for b in reads:
            b.reads[rkey] = o
        for b in writes:
            b.last_w = o
            b.reads = {}
        return o

    def barrier(self, include_dma_keys=()):
        deps = []
        for e in ("pe", "act", "dve", "sp"):
            for o in reversed(self.prog[e]):
                if not o.is_dma:
                    deps.append(o)
                    break
        for k in include_dma_keys:
            if k in self.dma_last:
                deps.append(self.dma_last[k])
        self.barrier_deps = deps

    def emit(self, nc, sems, dma_sems):
        for e in ENGINES:
            c = 0
            for o in self.prog[e]:
                if o.is_dma:
                    o.sigval = 16 * o.order
                elif o.signal:
                    c += 1
                    o.sigval = c

        def run(engname, eobj):
            for o in self.prog[engname]:
                for d in o.waits:
                    s = dma_sems[d.dma_sem] if d.is_dma else sems[d.eng]
                    eobj.wait_ge(s, d.sigval)
                ins = o.fn(eobj)
                if o.is_dma:
                    ins.then_inc(dma_sems[o.dma_sem], 16)
                elif o.signal:
                    ins.then_inc(sems[o.eng], 1)

        with nc.Block() as block:
            @block.tensor
            def _(e):
                run("pe", e)

            @block.scalar
            def _(e):
                run("act", e)

            @block.vector
            def _(e):
                run("dve", e)

            @block.gpsimd
            def _(e):
                run("pool", e)

            @block.sync
            def _(e):
                run("sp", e)


def build_nc(stop_after=None, nseq=NSEQ):
    nc = bass.Bass("TRN2", target_bir_lowering=False)

    def din(name, shape):
        return nc.dram_tensor(name, list(shape), F32, kind="ExternalInput").ap()

    x = din("x", [nseq, SEQ, D])
    g1_d = din("g1", [128, 8])
    gm_d = din("gm", [128, 8])
    g2_d = din("g2", [128, 8])
    gfin_d = din("gfin", [128, D])
    bfb_d = din("bfb", [128, 128])
    cw_d = din("cw", [128, 12])
    w_g = [din("ffn1_gate", [D, DFF]), din("ffn2_gate", [D, DFF])]
    w_u = [din("ffn1_up", [D, DFF]), din("ffn2_up", [D, DFF])]
    w_d = [din("ffn1_down", [DFF, D]), din("ffn2_down", [DFF, D])]
    w_in = din("w_in", [D, 5128])
    w_oa = din("w_o_attn", [512, D])
    w_oc = din("w_o_conv", [512, D])
    w_out = din("w_out", [D, D])
    out = nc.dram_tensor("out", [nseq, SEQ, D], F32, kind="ExternalOutput").ap()

    S = Sched()
    es = ExitStack()
    with es:
        def sb(name, shape, dt):
            return es.enter_context(nc.sbuf_tensor(name, shape, dt))

        xT = sb("xT", [128, 8, SEQ], F32)
        hT = sb("hT", [128, 8, SEQ], BF16)
        ring = sb("ring", [128, NSLOT, 4096], BF16)
        O = sb("O", [128, 24576], BF16)
        attnT = sb("attnT", [128, 4, SEQ], BF16)
        ident = sb("ident", [128, 128], F32)
        triT = sb("triT", [128, 128], F32)
        onesF = sb("onesF", [128, 128], F32)
        identb = sb("identb", [128, 128], BF16)
        maskb = sb("maskb", [128, 128], BF16)
        negtri = sb("negtri", [128, 128], BF16)
        negones = sb("negones", [128, 128], BF16)
        onesb = sb("onesb", [128, 128], BF16)
        gt = [sb("g1t", [128, 8], F32), sb("gmt", [128, 8], F32), sb("g2t", [128, 8], F32)]
        cwt = sb("cwt", [128, 12], F32)
        bfb = sb("bfbt", [128, 128], F32)
        wf = sb("wf", [128, 8, 8], BF16)
        epst = sb("epst", [128, 1], F32)
        sst = sb("sst", [128, 8], F32)
        sqb = [sb("sqb0", [128, 512], BF16), sb("sqb1", [128, 512], BF16)]
        rstd = sb("rstd", [128, 512], F32)
        gfin = sb("gfin_t", [128, D], F32)
        PTx = sb("PTx", [128, 2, 512], BF16)
        rc2 = sb("rc2", [128, 512], F32)
        PS = es.enter_context(nc.psum_tensor("PS", [128, 8, 512], F32))

        sems = {e: es.enter_context(nc.semaphore("s_" + e)) for e in ENGINES}
        dma_keys = ["slot%d" % i for i in range(NSLOT)] + ["stg0", "stg1", "stg2", "stg3", "augA0", "augA1", "augB0", "augB1", "cst", "cstp", "gfin"]
        dsems = {k: es.enter_context(nc.semaphore("d_" + k)) for k in dma_keys}

        def Obf(a, n):
            return O[:, a:a + n]

        def Of32(a, n):
            return O[:, a:a + 2 * n].bitcast(F32)

        actT = O[:, 0:22528].rearrange("p (c t) -> p c t", c=22)
        sil = [Of32(22528, 512), Of32(23552, 512)]
        Vaug = O[:, 0:12288].rearrange("p (t j k d) -> p t j k d", t=16, j=4, k=3)
        qA = Obf(12288, 2048)
        qB = Obf(14336, 2048)
        kA = Obf(16384, 2048)
        kB = Obf(18432, 2048)
        PT = [Obf(20480 + 512 * i, 512) for i in range(3)] + [PTx[:, i, :] for i in range(2)]
        zt = Of32(22016, 128)
        spt = Of32(22272, 128)
        spx = Of32(22528, 128)
        sp_bf = Obf(22784, 128)
        spx_bf = Obf(22912, 128)
        Ck = Of32(23040, 128)
        rc = Of32(23296, 512)
        convin = O[:, 0:4096].rearrange("p (c t) -> p c t", c=4)
        merged = O[:, 4096:12288].rearrange("p (c t) -> p c t", c=8)
        ubuf = [Of32(12288 + 1028 * i, 514) for i in range(4)]
        cxs = Of32(16400, 512)
        tconv = Of32(17424, 512)
        sa_t = Of32(18448, 512)
        sc_t = Of32(19472, 512)
        m1_t = Of32(20496, 512)
        m2_t = Of32(21520, 512)

        B_xT = [[Buf("xT%d_%d" % (k, s)) for s in range(4)] for k in range(8)]
        B_hT = [[Buf("hT%d_%d" % (k, s)) for s in range(4)] for k in range(8)]
        B_slot = [Buf("slot%d" % i) for i in range(NSLOT)]
        B_bank = [Buf("bank%d" % i) for i in range(8)]
        B_act = [[Buf("act%d_%d" % (c, s)) for s in range(2)] for c in range(NCH)]
        B_sil = [Buf("sil0"), Buf("sil1")]
        B_stg = [Buf("stg%d" % i) for i in range(4)]
        stg = [attnT[:, i, :].bitcast(F32) for i in range(4)]
        stg_i = [0]

        def next_stg():
            i = stg_i[0] % 4
            stg_i[0] += 1
            return i
        B_ssf = [Buf("ssf0"), Buf("ssf1")]
        B_gfin = Buf("gfin")
        B_V = [Buf("V%d" % t) for t in range(16)]
        B_Vones = Buf("Vones")
        B_q = {n: [Buf("%s_%d" % (n, s)) for s in range(4)] for n in ("qA", "qB", "kA", "kB")}
        B_aug = {"A": Buf("augA"), "B": Buf("augB")}
        B_qaux = Buf("qaux")
        B_zero = Buf("qkzero")
        B_PT = [Buf("PT%d" % i) for i in range(6)]
        B_z, B_sp, B_spx, B_spbf, B_spxbf, B_Ck, B_rc = (Buf(n) for n in ("z", "sp", "spx", "spbf", "spxbf", "Ck", "rc"))
        B_rc2 = Buf("rc2")
        B_attn = [[Buf("attn%d_%d" % (j, s)) for s in range(4)] for j in range(4)]
        B_convin = [[Buf("cvi%d_%d" % (j, s)) for s in range(2)] for j in range(4)]
        B_merged = [[Buf("mg%d_%d" % (c, s)) for s in range(2)] for c in range(8)]
        B_u = [Buf("u%d" % j) for j in range(4)]
        B_cxs, B_tconv, B_sa, B_sc, B_m1, B_m2 = (Buf(n) for n in ("cxs", "tconv", "sa", "sc", "m1", "m2"))
        B_sq = [Buf("sq0"), Buf("sq1")]
        B_rstd = Buf("rstd")
        B_ss = Buf("ss")
        B_const = Buf("const")

        bank_i = [0]
        reserved = set()

        def next_bank(exclude=None, pool=None):
            while True:
                b = bank_i[0]
                bank_i[0] = (b + 1) % 8
                if b == exclude or b in reserved:
                    continue
                if pool is not None and b not in pool:
                    continue
                return b

        def next_pair():
            while True:
                if bank_i[0] % 2:
                    bank_i[0] = (bank_i[0] + 1) % 8
                b = bank_i[0]
                bank_i[0] = (b + 2) % 8
                if b in reserved or (b + 1) in reserved:
                    continue
                return b

        pending = []

        def pump(n=1):
            for _ in range(n):
                if pending:
                    pending.pop(0)()

        def flush():
            while pending:
                pending.pop(0)()

        job_i = [0]

        def wload(src, k, c):
            i = job_i[0] % NSLOT
            view = ring[:, i, 0:k * c].rearrange("p (k c) -> p k c", k=k)
            S.op("pool", lambda e: e.dma_start(out=view, in_=src), writes=[B_slot[i]],
                 dma_sem="slot%d" % i, exempt=True)
            B_slot[i].owner = job_i[0]
            tok = (B_slot[i], job_i[0])
            job_i[0] += 1
            return view, tok

        def wload_multi(nelem, parts):
            i = job_i[0] % NSLOT
            flat = ring[:, i, 0:nelem]
            prev = None
            for k, (dstf, src) in enumerate(parts):
                dst = dstf(flat)
                if k == 0:
                    prev = S.op("pool", lambda e, dst=dst, src=src: e.dma_start(out=dst, in_=src), writes=[B_slot[i]],
                                dma_sem="slot%d" % i, exempt=True)
                else:
                    prev = S.op("pool", lambda e, dst=dst, src=src: e.dma_start(out=dst, in_=src),
                                dma_sem="slot%d" % i, exempt=True)
                    B_slot[i].last_w = prev
            B_slot[i].owner = job_i[0]
            tok = (B_slot[i], job_i[0])
            job_i[0] += 1
            return flat, tok

        def wload2(src_a, src_b):
            flat, tok = wload_multi(4096, [
                (lambda f: f.rearrange("p (k c) -> p k c", k=8)[:, 0:4, :], src_a),
                (lambda f: f.rearrange("p (k c) -> p k c", k=8)[:, 4:8, :], src_b)])
            return flat.rearrange("p (k c) -> p k c", k=8), tok

        def wsrc(w, r0, r1, c0, c1):
            return w[r0:r1, c0:c1].rearrange("(k p) c -> p k c", p=128)

        def mm_group(out_ap, pairs, bankbuf):
            n = len(pairs)
            for i, (l, r, rd0) in enumerate(pairs):
                rd = []
                for b in rd0:
                    if isinstance(b, tuple):
                        assert b[0].owner == b[1], "weight slot reused while still live: %s" % b[0].name
                        b = b[0]
                    rd.append(b)
                S.op("pe", lambda e, l=l, r=r, i=i: e.matmul(out_ap, lhsT=l, rhs=r, start=(i == 0), stop=(i == n - 1)),
                     reads=rd, writes=[bankbuf])

        cst_ops = []
        for dst, src in ((gt[0], g1_d), (gt[1], gm_d), (gt[2], g2_d), (cwt, cw_d), (bfb, bfb_d)):
            S.op("sp", lambda e, dst=dst, src=src: e.dma_start(out=dst[:], in_=src), writes=[B_const], dma_sem="cst")
        S.op("pool", lambda e: e.dma_start(out=wf[:], in_=w_in[:, 1536:1544].rearrange("(k p) c -> p k c", p=128)),
             writes=[B_const], dma_sem="cstp")
        S.op("sp", lambda e: e.dma_start(out=gfin[:], in_=gfin_d), writes=[B_gfin], dma_sem="gfin")
        S.op("pool", lambda e: e.memset(ident[:], 1.0), writes=[B_const])
        S.op("pool", lambda e: e.affine_select(out=ident[:], in_=ident[:], pattern=[[-1, 128]], compare_op=ALU.is_equal,
                                               fill=0.0, base=0, channel_multiplier=1), reads=[B_const], writes=[B_const])
        S.op("pool", lambda e: e.memset(onesF[:], 1.0), writes=[B_const])
        S.op("pool", lambda e: e.memset(onesb[:], 1.0), writes=[B_const])
        S.op("pool", lambda e: e.memset(negones[:], -1.0), writes=[B_const])
        S.op("pool", lambda e: e.memset(epst[:], EPS), writes=[B_const])
        S.op("pool", lambda e: e.affine_select(out=triT[:], in_=onesF[:], pattern=[[1, 128]], compare_op=ALU.is_ge,
                                               fill=0.0, base=0, channel_multiplier=-1), reads=[B_const], writes=[B_const])
        S.op("dve", lambda e: e.tensor_scalar(out=negtri[:], in0=triT[:], scalar1=-1.0, scalar2=None, op0=ALU.mult),
             reads=[B_const], writes=[B_const])
        S.op("dve", lambda e: e.tensor_scalar(out=maskb[:], in0=triT[:], scalar1=-MASKVAL, scalar2=MASKVAL,
                                              op0=ALU.mult, op1=ALU.add), reads=[B_const], writes=[B_const])
        S.op("dve", lambda e: e.tensor_copy(out=identb[:], in_=ident[:]), reads=[B_const], writes=[B_const])

        def x_steps(q, tts):
            st = {"tr": [], "ev": None}

            def step(tt):
                if st["ev"] is not None:
                    st["ev"]()
                    st["ev"] = None
                if len(st["tr"]) >= 2 or (tt is None and st["tr"]):
                    st["ev"] = st["tr"].pop(0)()
                if tt is None:
                    return
                si = next_stg()
                S.op("sp", lambda e: e.dma_start(out=stg[si], in_=x[q, tt * 128:(tt + 1) * 128, :]),
                     writes=[B_stg[si]], dma_sem="stg%d" % si)

                def tr():
                    bp = next_pair()
                    reserved.add(bp)
                    reserved.add(bp + 1)
                    for kc in range(8):
                        S.op("pe", lambda e, kc=kc: e.transpose(
                            out=PS[:, bp + kc // 4, (kc % 4) * 128:(kc % 4 + 1) * 128],
                            in_=stg[si][:, kc * 128:(kc + 1) * 128], identity=ident[:]),
                            reads=[B_stg[si], B_const], writes=[B_bank[bp + kc // 4]])

                    def ev():
                        s_ = tt // 4
                        S.op("act", lambda e: e.activation(
                            out=xT[:, 0:4, tt * 128:(tt + 1) * 128], in_=PS[:, bp, :].rearrange("p (k t) -> p k t", k=4),
                            func=AF.Copy), reads=[B_bank[bp]], writes=[B_xT[k][s_] for k in range(4)])
                        S.op("dve", lambda e: e.tensor_copy(
                            out=xT[:, 4:8, tt * 128:(tt + 1) * 128], in_=PS[:, bp + 1, :].rearrange("p (k t) -> p k t", k=4)),
                            reads=[B_bank[bp + 1]], writes=[B_xT[k][s_] for k in range(4, 8)])
                        reserved.discard(bp)
                        reserved.discard(bp + 1)
                    return ev
                st["tr"].append(tr)

            for tt in tts:
                pending.append(lambda tt=tt: step(tt))
            for _ in range(3):
                pending.append(lambda: step(None))

        def final_steps(q, tts):
            tts = list(tts)
            assert len(tts) == 8
            st = {"post": None}

            def transposes(tt):
                s_ = tt // 4
                bp = next_pair()
                reserved.add(bp)
                reserved.add(bp + 1)
                for kc in range(8):
                    S.op("pe", lambda e, kc=kc: e.transpose(
                        out=PS[:, bp + kc // 4, (kc % 4) * 128:(kc % 4 + 1) * 128],
                        in_=xT[:, kc, tt * 128:(tt + 1) * 128], identity=ident[:]),
                        reads=[B_xT[kc][s_], B_const], writes=[B_bank[bp + kc // 4]])
                return bp

            def step1(tt, i):
                if st["post"] is not None:
                    st["post"]()
                    st["post"] = None
                if tt is None:
                    return
                bp = transposes(tt)

                def post():
                    pv = PS[:, bp:bp + 2, :]
                    S.op("act", lambda e: e.activation(out=pv, in_=pv, func=AF.Square, accum_out=sst[:, i:i + 1]),
                         reads=[B_bank[bp], B_bank[bp + 1]], writes=[B_bank[bp], B_bank[bp + 1], B_ssf[0]])
                    reserved.discard(bp)
                    reserved.discard(bp + 1)
                st["post"] = post

            def rstd_step():
                S.op("act", lambda e: e.activation(out=sst[:, 0:8], in_=sst[:, 0:8], func=AF.Sqrt, scale=1.0 / D,
                                                   bias=epst[:, 0:1]), reads=[B_ssf[0], B_const], writes=[B_ssf[0]])
                S.op("dve", lambda e: e.reciprocal(out=sst[:, 0:8], in_=sst[:, 0:8]), reads=[B_ssf[0]], writes=[B_ssf[0]])

            def step2(tt, i):
                if st["post"] is not None:
                    st["post"]()
                    st["post"] = None
                if tt is None:
                    return
                bp = transposes(tt)

                def post():
                    pv = PS[:, bp:bp + 2, :]
                    si = next_stg()
                    ov = stg[si].rearrange("p (a b) -> p a b", a=2)
                    S.op("dve", lambda e: e.scalar_tensor_tensor(
                        out=ov, in0=pv, scalar=sst[:, i:i + 1], in1=gfin[:].rearrange("p (a b) -> p a b", a=2),
                        op0=ALU.mult, op1=ALU.mult),
                        reads=[B_bank[bp], B_bank[bp + 1], B_ssf[0], B_gfin], writes=[B_stg[si]])
                    S.op("sp", lambda e: e.dma_start(out=out[q, tt * 128:(tt + 1) * 128, :], in_=stg[si]),
                         reads=[B_stg[si]], dma_sem="stg%d" % si)
                    reserved.discard(bp)
                    reserved.discard(bp + 1)
                st["post"] = post

            for i, tt in enumerate(tts):
                pending.append(lambda tt=tt, i=i: step1(tt, i))
            pending.append(lambda: step1(None, 0))
            pending.append(rstd_step)
            for i, tt in enumerate(tts):
                pending.append(lambda tt=tt, i=i: step2(tt, i))
            pending.append(lambda: step2(None, 0))

        def final_steps_single(q, tts):
            for n, tt in enumerate(tts):
                def step(tt=tt, n=n):
                    s_ = tt // 4
                    i = n % 2
                    c0 = 4 * i
                    bp = next_pair()
                    for kc in range(8):
                        S.op("pe", lambda e, kc=kc: e.transpose(
                            out=PS[:, bp + kc // 4, (kc % 4) * 128:(kc % 4 + 1) * 128],
                            in_=xT[:, kc, tt * 128:(tt + 1) * 128], identity=ident[:]),
                            reads=[B_xT[kc][s_], B_const], writes=[B_bank[bp + kc // 4]])
                    pv = PS[:, bp:bp + 2, :]
                    si = next_stg()
                    ov = stg[si].rearrange("p (a b) -> p a b", a=2)
                    S.op("act", lambda e: e.activation(out=ov, in_=pv, func=AF.Square, accum_out=sst[:, c0:c0 + 1]),
                         reads=[B_bank[bp], B_bank[bp + 1]], writes=[B_stg[si], B_ssf[i]])
                    S.op("act", lambda e: e.activation(out=sst[:, c0 + 1:c0 + 2], in_=sst[:, c0:c0 + 1], func=AF.Sqrt,
                                                       scale=1.0 / D, bias=epst[:, 0:1]),
                         reads=[B_ssf[i], B_const], writes=[B_ssf[i]])
                    S.op("dve", lambda e: e.reciprocal(out=sst[:, c0 + 2:c0 + 3], in_=sst[:, c0 + 1:c0 + 2]),
                         reads=[B_ssf[i]], writes=[B_ssf[i]])
                    S.op("dve", lambda e: e.scalar_tensor_tensor(
                        out=ov, in0=pv, scalar=sst[:, c0 + 2:c0 + 3], in1=gfin[:].rearrange("p (a b) -> p a b", a=2),
                        op0=ALU.mult, op1=ALU.mult),
                        reads=[B_bank[bp], B_bank[bp + 1], B_ssf[i], B_gfin], writes=[B_stg[si]])
                    S.op("sp", lambda e: e.dma_start(out=out[q, tt * 128:(tt + 1) * 128, :], in_=stg[si]),
                         reads=[B_stg[si]], dma_sem="stg%d" % si)
                pending.append(step)

        def norm(spans, g):
            for s in spans:
                st = {}

                def sq_step(kcs, s=s, st=st):
                    if "bs" not in st:
                        st["bs"] = next_bank()
                        reserved.add(st["bs"])
                        st["mm"] = []
                    bs = st["bs"]
                    for f in st["mm"]:
                        f()
                    st["mm"] = []
                    for kc in kcs:
                        i = kc % 2
                        src = xT[:, kc, s * 512:(s + 1) * 512]
                        if kc % 2 == 0:
                            S.op("act", lambda e, i=i, src=src: e.activation(out=sqb[i][:], in_=src, func=AF.Square),
                                 reads=[B_xT[kc][s]], writes=[B_sq[i]])
                        else:
                            S.op("dve", lambda e, i=i, src=src: e.tensor_tensor(out=sqb[i][:], in0=src, in1=src, op=ALU.mult),
                                 reads=[B_xT[kc][s]], writes=[B_sq[i]])
                        st["mm"].append(lambda i=i, kc=kc, bs=bs: S.op(
                            "pe", lambda e: e.matmul(PS[:, bs, :], lhsT=onesb[:], rhs=sqb[i][:], start=(kc == 0), stop=(kc == 7)),
                            reads=[B_sq[i], B_const], writes=[B_bank[bs]]))

                def rstd_step(s=s, st=st):
                    bs = st["bs"]
                    for f in st["mm"]:
                        f()
                    st["mm"] = []
                    S.op("act", lambda e, bs=bs: e.activation(out=rstd[:], in_=PS[:, bs, :], func=AF.Ln,
                                                              scale=1.0 / D, bias=epst[:, 0:1]),
                         reads=[B_bank[bs], B_const], writes=[B_rstd])
                    reserved.discard(bs)
                    S.op("act", lambda e: e.activation(out=rstd[:], in_=rstd[:], func=AF.Exp, scale=-0.5),
                         reads=[B_rstd], writes=[B_rstd])

                def h_step(kcs, s=s):
                    for kc in kcs:
                        S.op("dve", lambda e, kc=kc, s=s: e.scalar_tensor_tensor(
                            out=hT[:, kc, s * 512:(s + 1) * 512], in0=xT[:, kc, s * 512:(s + 1) * 512], scalar=g[:, kc:kc + 1],
                            in1=rstd[:], op0=ALU.mult, op1=ALU.mult),
                            reads=[B_xT[kc][s], B_rstd, B_const], writes=[B_hT[kc][s]])

                for a in range(4):
                    pending.append(lambda a=a, f=sq_step: f([2 * a, 2 * a + 1]))
                pending.append(rstd_step)
                for a in range(4):
                    pending.append(lambda a=a, f=h_step: f([2 * a, 2 * a + 1]))

        def ffn(which, h, g, first=False, s_outer=False, on_span_done=None):
            spans = [2 * h, 2 * h + 1]
            wg, wu, wd = w_g[which], w_u[which], w_d[which]
            silc = 0
            it = 0
            for grp in range(6):
                c0 = grp * 512
                c1 = min(c0 + 512, DFF)
                Gv, Gb = wload(wsrc(wg, 0, D, c0, c1), 8, c1 - c0)
                Uv, Ub = wload(wsrc(wu, 0, D, c0, c1), 8, c1 - c0)
                for sl, s in enumerate(spans):
                    for cl in range((c1 - c0) // 128):
                        c = grp * 4 + cl
                        bg = next_bank()
                        mm_group(PS[:, bg, :], [(Gv[:, kc, cl * 128:(cl + 1) * 128], hT[:, kc, s * 512:(s + 1) * 512],
                                                 [Gb, B_hT[kc][s]]) for kc in range(8)], B_bank[bg])
                        bu = next_bank()
                        mm_group(PS[:, bu, :], [(Uv[:, kc, cl * 128:(cl + 1) * 128], hT[:, kc, s * 512:(s + 1) * 512],
                                                 [Ub, B_hT[kc][s]]) for kc in range(8)], B_bank[bu])
                        si = silc % 2
                        silc += 1
                        S.op("act", lambda e, si=si, bg=bg: e.activation(out=sil[si], in_=PS[:, bg, :], func=AF.Silu),
                             reads=[B_bank[bg]], writes=[B_sil[si]])
                        S.op("dve", lambda e, si=si, bu=bu, c=c, sl=sl: e.tensor_tensor(
                            out=actT[:, c, sl * 512:(sl + 1) * 512], in0=PS[:, bu, :], in1=sil[si], op=ALU.mult),
                            reads=[B_bank[bu], B_sil[si]], writes=[B_act[c][sl]])
                        pump(3 if (first and it < 4) else 1)
                        it += 1

            def down_unit(cq, Dv, sl, s):
                for dcl in range(2):
                    dc = cq * 2 + dcl
                    bd = next_bank()
                    mm_group(PS[:, bd, :], [(Dv[c // 11][0][:, c % 11, dcl * 128:(dcl + 1) * 128],
                                             actT[:, c, sl * 512:(sl + 1) * 512],
                                             [Dv[c // 11][1], B_act[c][sl]]) for c in range(NCH)], B_bank[bd])
                    S.op("dve", lambda e, bd=bd, dc=dc, s=s: e.scalar_tensor_tensor(
                        out=xT[:, dc, s * 512:(s + 1) * 512], in0=PS[:, bd, :], scalar=0.5,
                        in1=xT[:, dc, s * 512:(s + 1) * 512], op0=ALU.mult, op1=ALU.add),
                        reads=[B_bank[bd], B_xT[dc][s]], writes=[B_xT[dc][s]])
                    pump(1)

            def load_D(cq):
                Dv = []
                for rh in range(2):
                    r0 = rh * 11 * 128
                    Dv.append(wload(wsrc(wd, r0, r0 + 11 * 128, cq * 256, (cq + 1) * 256), 11, 256))
                return Dv

            if s_outer:
                for sl, s in enumerate(spans):
                    for cq in range(4):
                        Dv = load_D(cq)
                        down_unit(cq, Dv, sl, s)
                    if on_span_done is not None:
                        on_span_done(s)
            else:
                for cq in range(4):
                    Dv = load_D(cq)
                    for sl, s in enumerate(spans):
                        down_unit(cq, Dv, sl, s)

        def mixer_attention(q):
            Wv, Wvb = wload(wsrc(w_in, 0, D, 1024, 1536), 8, 512)
            Wq, Wqb = wload(wsrc(w_in, 0, D, 0, 512), 8, 512)
            S.op("pool", lambda e: e.memset(qA[64:128, :], 0.0), writes=[B_zero])
            S.op("pool", lambda e: e.memset(qB[0:64, :], 0.0), writes=[B_zero])
            S.op("pool", lambda e: e.memset(kA[64:96, :], 1.0), writes=[B_zero])
            S.op("pool", lambda e: e.memset(kA[96:128, :], 0.0), writes=[B_zero])
            S.op("pool", lambda e: e.memset(kB[0:32, :], 0.0), writes=[B_zero])
            S.op("pool", lambda e: e.memset(kB[32:64, :], 1.0), writes=[B_zero])
            S.op("pool", lambda e: e.memset(Vaug[:, :, :, 1, :], 1.0), writes=[B_Vones])
            Wk, Wkb = wload(wsrc(w_in, 0, D, 512, 1024), 8, 512)
            norm([2, 3], gt[1])
            for tt in range(16):
                s = tt // 4
                bv = next_bank()
                mm_group(PS[:, bv, :], [(hT[:, kc, tt * 128:(tt + 1) * 128], Wv[:, kc, :], [Wvb, B_hT[kc][s]])
                                        for kc in range(8)], B_bank[bv])
                src = PS[:, bv, :].rearrange("p (a b c) -> p a b c", a=4, b=2)
                dst = Vaug[:, tt, :, 0:3:2, :]
                if tt % 2 == 0:
                    S.op("act", lambda e, src=src, dst=dst: e.activation(out=dst, in_=src, func=AF.Copy),
                         reads=[B_bank[bv]], writes=[B_V[tt]])
                else:
                    S.op("dve", lambda e, src=src, dst=dst: e.tensor_copy(out=dst, in_=src),
                         reads=[B_bank[bv]], writes=[B_V[tt]])
                if tt < 8:
                    pump(3)
                if tt == 7:
                    flush()
            bf_ = next_bank()
            for tt in range(16):
                s = tt // 4
                mm_group(PS[:, bf_, tt * 8:(tt + 1) * 8], [(hT[:, kc, tt * 128:(tt + 1) * 128], wf[:, kc, :],
                                                             [B_const, B_hT[kc][s]]) for kc in range(8)], B_bank[bf_])
            S.op("dve", lambda e: e.tensor_tensor(out=zt, in0=PS[:, bf_, 0:128], in1=bfb[:], op=ALU.add),
                 reads=[B_bank[bf_], B_const], writes=[B_z])
            S.op("act", lambda e: e.activation(out=zt, in_=zt, func=AF.Exp, scale=-1.0), reads=[B_z], writes=[B_z])
            S.op("act", lambda e: e.activation(out=spt, in_=zt, func=AF.Ln, bias=1.0), reads=[B_z], writes=[B_sp])
            S.op("dve", lambda e: e.memset(spx[:, 0:8], 0.0), writes=[B_spx])
            for tt in range(1, 16):
                S.op("dve", lambda e, tt=tt: e.tensor_tensor(out=spx[:, tt * 8:(tt + 1) * 8], in0=spx[:, (tt - 1) * 8:tt * 8],
                                                              in1=spt[:, (tt - 1) * 8:tt * 8], op=ALU.add),
                     reads=[B_sp, B_spx], writes=[B_spx])
            S.op("dve", lambda e: e.tensor_copy(out=sp_bf, in_=spt), reads=[B_sp], writes=[B_spbf])
            S.op("dve", lambda e: e.tensor_copy(out=spx_bf, in_=spx), reads=[B_spx], writes=[B_spxbf])
            bc = next_bank()
            S.op("pe", lambda e: e.matmul(PS[:, bc, 0:128], lhsT=triT[:], rhs=spt, start=True, stop=False),
                 reads=[B_sp, B_const], writes=[B_bank[bc]])
            S.op("pe", lambda e: e.matmul(PS[:, bc, 0:128], lhsT=onesF[:], rhs=spx, start=False, stop=True),
                 reads=[B_spx, B_const], writes=[B_bank[bc]])
            S.op("dve", lambda e: e.tensor_copy(out=Ck, in_=PS[:, bc, 0:128]), reads=[B_bank[bc]], writes=[B_Ck])
            for s in range(4):
                bn = next_bank()
                for kbl in range(4):
                    kb = 4 * s + kbl
                    S.op("pe", lambda e, kb=kb, kbl=kbl, bn=bn: e.matmul(
                        PS[0:8, bn, kbl * 128:(kbl + 1) * 128], lhsT=sp_bf[:, kb * 8:(kb + 1) * 8], rhs=negtri[:],
                        start=True, stop=False), reads=[B_spbf, B_const], writes=[B_bank[bn]])
                    S.op("pe", lambda e, kb=kb, kbl=kbl, bn=bn: e.matmul(
                        PS[0:8, bn, kbl * 128:(kbl + 1) * 128], lhsT=spx_bf[:, kb * 8:(kb + 1) * 8], rhs=negones[:],
                        start=False, stop=True), reads=[B_spxbf, B_const], writes=[B_bank[bn]])
                S.op("dve", lambda e, s=s, bn=bn: e.tensor_copy(out=qA[96:104, s * 512:(s + 1) * 512], in_=PS[0:8, bn, :]),
                     reads=[B_bank[bn], B_zero], writes=[B_qaux])
            def borrow_slot():
                i = job_i[0] % NSLOT
                B_slot[i].owner = job_i[0]
                job_i[0] += 1
                return ring[:, i, :], B_slot[i]

            r0v, r0b = borrow_slot()
            r1v, r1b = borrow_slot()
            TS = [dict(qA=qA, qB=qB, kA=kA, kB=kB, gq=[], gk=[]),
                  dict(qA=r0v[:, 0:2048], qB=r0v[:, 2048:4096], kA=r1v[:, 0:2048], kB=r1v[:, 2048:4096], gq=[r0b], gk=[r1b])]
            B_q1 = {n: [Buf("%s1_%d" % (n, s)) for s in range(4)] for n in ("qA", "qB", "kA", "kB")}
            B_aug1 = {"A": Buf("augA1"), "B": Buf("augB1")}
            B_zero1 = Buf("qkzero1")
            BQ = [B_q, B_q1]
            BAUG = [B_aug, B_aug1]
            BZ = [B_zero, B_zero1]
            t1 = TS[1]
            S.op("pool", lambda e: e.memset(t1["qA"][64:128, :], 0.0), writes=[B_zero1] + t1["gq"])
            S.op("pool", lambda e: e.memset(t1["qB"][0:64, :], 0.0), writes=[B_zero1] + t1["gq"])
            S.op("pool", lambda e: e.memset(t1["kA"][64:96, :], 1.0), writes=[B_zero1] + t1["gk"])
            S.op("pool", lambda e: e.memset(t1["kA"][96:128, :], 0.0), writes=[B_zero1] + t1["gk"])
            S.op("pool", lambda e: e.memset(t1["kB"][0:32, :], 0.0), writes=[B_zero1] + t1["gk"])
            S.op("pool", lambda e: e.memset(t1["kB"][32:64, :], 1.0), writes=[B_zero1] + t1["gk"])

            pti = [0]
            oi = [0]
            SP_ = (0, 1, 2, 3, 4, 5)

            def proj_steps(j):
                ts = j % 2
                T = TS[ts]
                Bq_ = BQ[ts]
                Bz = BZ[ts]

                def first():
                    S.op("sp", lambda e: e.dma_start(out=T["qA"][64:65, :], in_=qA[96 + 2 * j:97 + 2 * j, :]),
                         reads=[B_qaux, Bz], writes=[BAUG[ts]["A"]] + T["gq"], dma_sem="augA%d" % ts)
                    S.op("sp", lambda e: e.dma_start(out=T["qB"][63:64, :], in_=qA[97 + 2 * j:98 + 2 * j, :]),
                         reads=[B_qaux, Bz], writes=[BAUG[ts]["B"]] + T["gq"], dma_sem="augB%d" % ts)

                def grp_steps(W, Wb, s, evac):
                    sl_ = slice(s * 512, (s + 1) * 512)
                    st = {}

                    def sub(a):
                        if a == 0:
                            st["b"] = next_bank(pool=SP_)
                            reserved.add(st["b"])
                        b = st["b"]
                        for kc in (2 * a, 2 * a + 1):
                            S.op("pe", lambda e, kc=kc: e.matmul(PS[:, b, :], lhsT=W[:, kc, j * 128:(j + 1) * 128],
                                                                  rhs=hT[:, kc, sl_], start=(kc == 0), stop=(kc == 7)),
                                 reads=[Wb[0], B_hT[kc][s]], writes=[B_bank[b]])
                        if a == 3:
                            reserved.discard(b)
                            evac(b, s, sl_)
                    return [lambda a=a: sub(a) for a in range(4)]

                def q_evac(bq, s, sl_):
                    S.op("dve", lambda e: e.tensor_scalar(out=T["qA"][0:64, sl_], in0=PS[0:64, bq, :],
                                                          scalar1=0.125, scalar2=None, op0=ALU.mult),
                         reads=[B_bank[bq], Bz], writes=[Bq_["qA"][s]] + T["gq"])
                    S.op("dve", lambda e: e.tensor_scalar(out=T["qB"][64:128, sl_], in0=PS[64:128, bq, :],
                                                          scalar1=0.125, scalar2=None, op0=ALU.mult),
                         reads=[B_bank[bq], Bz], writes=[Bq_["qB"][s]] + T["gq"])

                def k_evac(bk, s, sl_):
                    S.op("dve", lambda e: e.tensor_copy(out=T["kA"][0:64, sl_], in_=PS[0:64, bk, :]),
                         reads=[B_bank[bk], Bz], writes=[Bq_["kA"][s]] + T["gk"])
                    S.op("dve", lambda e: e.tensor_copy(out=T["kB"][64:128, sl_], in_=PS[64:128, bk, :]),
                         reads=[B_bank[bk], Bz], writes=[Bq_["kB"][s]] + T["gk"])

                pending.append(first)
                for s in range(4):
                    pending.extend(grp_steps(Wq, Wqb, s, q_evac))
                    pending.extend(grp_steps(Wk, Wkb, s, k_evac))

            proj_steps(0)
            flush()
            blk_cnt = [0]
            dve_defer = []
            for j in range(4):
                ts = j % 2
                T = TS[ts]
                if j + 1 < 4:
                    proj_steps(j + 1)
                for X in ("A", "B"):
                    h = 2 * j + (0 if X == "A" else 1)
                    qt = T["q" + X]
                    kt = T["k" + X]
                    Bq = BQ[ts]["q" + X]
                    Bk = BQ[ts]["k" + X]
                    Baug = BAUG[ts][X]
                    Bz = BZ[ts]
                    guards = T["gq"] + T["gk"]
                    for qs in range(4):
                        bo = 6 + (oi[0] % 2)
                        oi[0] += 1
                        nblk = 4 * qs + 4
                        pend = []

                        def emit_S(kb, qs=qs, h=h, qt=qt, kt=kt, Bq=Bq, Bk=Bk, Baug=Baug, Bz=Bz, guards=guards):
                            diag = kb >= 4 * qs
                            q0 = kb * 128 if diag else qs * 512
                            N = (qs + 1) * 512 - q0
                            bs_ = next_bank(pool=SP_)
                            pi = pti[0] % 5
                            pti[0] += 1
                            S.op("pe", lambda e: e.matmul(PS[:, bs_, 0:N], lhsT=kt[:, kb * 128:(kb + 1) * 128],
                                                          rhs=qt[:, q0:q0 + N], start=True, stop=not diag),
                                 reads=[Bk[kb // 4], Bq[qs], Baug, Bz] + guards, writes=[B_bank[bs_]])
                            if diag:
                                S.op("pe", lambda e: e.matmul(PS[:, bs_, 0:128], lhsT=identb[:], rhs=maskb[:],
                                                              start=False, stop=True),
                                     reads=[B_const], writes=[B_bank[bs_]])
                            S.op("act", lambda e: e.activation(out=PT[pi][:, 0:N], in_=PS[:, bs_, 0:N], func=AF.Exp,
                                                               bias=Ck[:, kb * 8 + h:kb * 8 + h + 1], scale=1.0),
                                 reads=[B_bank[bs_], B_Ck], writes=[B_PT[pi]])
                            return (kb, pi, q0, N)

                        def emit_PV(info, qs=qs, j=j, X=X, bo=bo, nblk=nblk):
                            kb, pi, q0, N = info
                            lo = q0 - qs * 512
                            vl = Vaug[:, kb, j, 0:2, :] if X == "A" else Vaug[:, kb, j, 1:3, :]
                            S.op("pe", lambda e: e.matmul(PS[:, bo, lo:512], lhsT=vl, rhs=PT[pi][:, 0:N],
                                                          start=(kb == 0), stop=(kb == nblk - 1)),
                                 reads=[B_PT[pi], B_V[kb], B_Vones], writes=[B_bank[bo]])

                        for kb in range(nblk):
                            pend.append(emit_S(kb))
                            if len(pend) > 4:
                                emit_PV(pend.pop(0))
                            if dve_defer:
                                dve_defer.pop(0)()
                            blk_cnt[0] += 1
                            if blk_cnt[0] % 2 == 0:
                                pump(1)
                        while pend:
                            emit_PV(pend.pop(0))
                        while dve_defer:
                            dve_defer.pop(0)()
                        sl_ = slice(qs * 512, (qs + 1) * 512)
                        if X == "A":
                            orow, drow = slice(0, 64), slice(64, 128)
                        else:
                            orow, drow = slice(64, 128), slice(0, 64)
                        if qs == 0:
                            S.op("act", lambda e, bo=bo, drow=drow: e.activation(out=rc2[drow, :], in_=PS[drow, bo, :], func=AF.Ln),
                                 reads=[B_bank[bo]], writes=[B_rc2])
                            S.op("act", lambda e, drow=drow: e.activation(out=rc2[drow, :], in_=rc2[drow, :], func=AF.Exp, scale=-1.0),
                                 reads=[B_rc2], writes=[B_rc2])
                            S.op("dve", lambda e, orow=orow, drow=drow: e.tensor_copy(out=rc2[orow, :], in_=rc2[drow, :]),
                                 reads=[B_rc2], writes=[B_rc2])
                            S.op("dve", lambda e, bo=bo, sl_=sl_, j=j, orow=orow: e.tensor_tensor(
                                out=attnT[orow, j, sl_], in0=PS[orow, bo, :], in1=rc2[orow, :], op=ALU.mult),
                                reads=[B_bank[bo], B_rc2], writes=[B_attn[j][qs]])
                        else:
                            for cp in range(4):
                                dve_defer.append(lambda bo=bo, orow=orow, drow=drow, cp=cp: S.op(
                                    "dve", lambda e: e.reciprocal(out=rc[orow, cp * 128:(cp + 1) * 128],
                                                                  in_=PS[drow, bo, cp * 128:(cp + 1) * 128]),
                                    reads=[B_bank[bo]], writes=[B_rc]))
                            dve_defer.append(lambda bo=bo, sl_=sl_, j=j, orow=orow, qs=qs: S.op(
                                "dve", lambda e: e.tensor_tensor(out=attnT[orow, j, sl_], in0=PS[orow, bo, :], in1=rc[orow, :],
                                                                  op=ALU.mult),
                                reads=[B_bank[bo], B_rc], writes=[B_attn[j][qs]]))
                while dve_defer:
                    dve_defer.pop(0)()
                flush()

        def mixer_post(q, hf):
            spans = [2 * hf, 2 * hf + 1]
            cv_src = w_in[:, 1544:3080].rearrange("(k p) (t j c) -> p k t j c", p=128, t=3, j=4)
            for jc in range(4):
                Wcv, Bcv = wload_multi(3072, [
                    (lambda f, t=t: f.rearrange("p (k t c) -> p k t c", k=8, t=3)[:, :, t, :], cv_src[:, :, t, jc, :])
                    for t in range(3)])
                Wcv = Wcv.rearrange("p (k t c) -> p k t c", k=8, t=3)
                for sl, s in enumerate(spans):
                    sl_ = slice(s * 512, (s + 1) * 512)
                    bx = next_bank()
                    mm_group(PS[:, bx, :], [(Wcv[:, kc, 2, :], hT[:, kc, sl_], [Bcv, B_hT[kc][s]]) for kc in range(8)], B_bank[bx])
                    bcc = next_bank()
                    mm_group(PS[:, bcc, :], [(Wcv[:, kc, 1, :], hT[:, kc, sl_], [Bcv, B_hT[kc][s]]) for kc in range(8)], B_bank[bcc])
                    bcb = next_bank()
                    mm_group(PS[:, bcb, :], [(Wcv[:, kc, 0, :], hT[:, kc, sl_], [Bcv, B_hT[kc][s]]) for kc in range(8)], B_bank[bcb])
                    u = ubuf[jc]
                    if s == 0:
                        S.op("dve", lambda e, u=u: e.memset(u[:, 0:2], 0.0), writes=[B_u[jc]])
                    S.op("act", lambda e, bx=bx: e.activation(out=cxs, in_=PS[:, bx, :], func=AF.Copy),
                         reads=[B_bank[bx]], writes=[B_cxs])
                    S.op("dve", lambda e, u=u, bcc=bcc: e.tensor_tensor(out=u[:, 2:514], in0=PS[:, bcc, :], in1=cxs, op=ALU.mult),
                         reads=[B_bank[bcc], B_cxs], writes=[B_u[jc]])
                    S.op("dve", lambda e, u=u, jc=jc: e.tensor_scalar(out=tconv, in0=u[:, 0:512], scalar1=cwt[:, jc * 3:jc * 3 + 1],
                                                                       scalar2=None, op0=ALU.mult),
                         reads=[B_u[jc], B_const], writes=[B_tconv])
                    S.op("dve", lambda e, u=u, jc=jc: e.scalar_tensor_tensor(
                        out=tconv, in0=u[:, 1:513], scalar=cwt[:, jc * 3 + 1:jc * 3 + 2], in1=tconv, op0=ALU.mult, op1=ALU.add),
                        reads=[B_u[jc], B_tconv, B_const], writes=[B_tconv])
                    S.op("dve", lambda e, u=u, jc=jc: e.scalar_tensor_tensor(
                        out=tconv, in0=u[:, 2:514], scalar=cwt[:, jc * 3 + 2:jc * 3 + 3], in1=tconv, op0=ALU.mult, op1=ALU.add),
                        reads=[B_u[jc], B_tconv, B_const], writes=[B_tconv])
                    S.op("dve", lambda e, bcb=bcb, jc=jc, sl=sl: e.tensor_tensor(
                        out=convin[:, jc, sl * 512:(sl + 1) * 512], in0=PS[:, bcb, :], in1=tconv, op=ALU.mult),
                        reads=[B_bank[bcb], B_tconv], writes=[B_convin[jc][sl]])
                    S.op("dve", lambda e, u=u: e.tensor_copy(out=u[:, 0:2], in_=u[:, 512:514]),
                         reads=[B_u[jc]], writes=[B_u[jc]])
                    pump(1)
            for mq in range(4):
                c0 = mq * 256
                gsrc = lambda base: w_in[:, base + c0:base + c0 + 256].rearrange("(k p) c -> p k c", p=128)
                Wgg, Bgg = wload_multi(4096, [
                    (lambda f: f[:, 0:2048].rearrange("p (k c) -> p k c", k=8), gsrc(3080)),
                    (lambda f: f[:, 2048:4096].rearrange("p (k c) -> p k c", k=8), gsrc(4104))])
                Wga = Wgg[:, 0:2048].rearrange("p (k c) -> p k c", k=8)
                Wgc = Wgg[:, 2048:4096].rearrange("p (k c) -> p k c", k=8)
                Bga = Bgc = Bgg
                Woo, Boca = wload_multi(2048, [
                    (lambda f: f[:, 0:1024].rearrange("p (k c) -> p k c", k=4),
                     w_oc[:, c0:c0 + 256].rearrange("(k p) c -> p k c", p=128)),
                    (lambda f: f[:, 1024:2048].rearrange("p (k c) -> p k c", k=4),
                     w_oa[:, c0:c0 + 256].rearrange("(k p) c -> p k c", p=128))])
                Woca = Woo.rearrange("p (k c) -> p k c", k=8)
                for cl in range(2):
                    c = 2 * mq + cl
                    for sl, s in enumerate(spans):
                        sl_ = slice(s * 512, (s + 1) * 512)
                        ll = slice(sl * 512, (sl + 1) * 512)
                        b_ga = next_bank()
                        mm_group(PS[:, b_ga, :], [(Wga[:, kc, cl * 128:(cl + 1) * 128], hT[:, kc, sl_], [Bga, B_hT[kc][s]])
                                                  for kc in range(8)], B_bank[b_ga])
                        b_gc = next_bank()
                        mm_group(PS[:, b_gc, :], [(Wgc[:, kc, cl * 128:(cl + 1) * 128], hT[:, kc, sl_], [Bgc, B_hT[kc][s]])
                                                  for kc in range(8)], B_bank[b_gc])
                        b_ya = next_bank()
                        mm_group(PS[:, b_ya, :], [(Woca[:, 4 + kc, cl * 128:(cl + 1) * 128], attnT[:, kc, sl_], [Boca, B_attn[kc][s]])
                                                  for kc in range(4)], B_bank[b_ya])
                        b_yc = next_bank()
                        mm_group(PS[:, b_yc, :], [(Woca[:, kc, cl * 128:(cl + 1) * 128], convin[:, kc, ll], [Boca, B_convin[kc][sl]])
                                                  for kc in range(4)], B_bank[b_yc])
                        S.op("act", lambda e, b=b_ga: e.activation(out=sa_t, in_=PS[:, b, :], func=AF.Sigmoid),
                             reads=[B_bank[b_ga]], writes=[B_sa])
                        S.op("act", lambda e, b=b_gc: e.activation(out=sc_t, in_=PS[:, b, :], func=AF.Sigmoid),
                             reads=[B_bank[b_gc]], writes=[B_sc])
                        S.op("dve", lambda e, b=b_ya: e.tensor_tensor(out=m1_t, in0=PS[:, b, :], in1=sa_t, op=ALU.mult),
                             reads=[B_bank[b_ya], B_sa], writes=[B_m1])
                        S.op("dve", lambda e, b=b_yc: e.tensor_tensor(out=m2_t, in0=PS[:, b, :], in1=sc_t, op=ALU.mult),
                             reads=[B_bank[b_yc], B_sc], writes=[B_m2])
                        dbg = os.environ.get("MK_DBG", "")
                        if dbg == "noattn":
                            S.op("dve", lambda e, c=c, ll=ll: e.tensor_copy(out=merged[:, c, ll], in_=m2_t),
                                 reads=[B_m1, B_m2], writes=[B_merged[c][sl]])
                        elif dbg == "noconv":
                            S.op("dve", lambda e, c=c, ll=ll: e.tensor_copy(out=merged[:, c, ll], in_=m1_t),
                                 reads=[B_m1, B_m2], writes=[B_merged[c][sl]])
                        else:
                            S.op("dve", lambda e, c=c, ll=ll: e.tensor_tensor(out=merged[:, c, ll], in0=m1_t, in1=m2_t, op=ALU.add),
                                 reads=[B_m1, B_m2], writes=[B_merged[c][sl]])
                        pump(1)
            for ch in range(2):
                Wo, Bo = wload(wsrc(w_out, 0, D, ch * 512, (ch + 1) * 512), 8, 512)
                for dcl in range(4):
                    dc = 4 * ch + dcl
                    for sl, s in enumerate(spans):
                        sl_ = slice(s * 512, (s + 1) * 512)
                        ll = slice(sl * 512, (sl + 1) * 512)
                        bd = next_bank()
                        mm_group(PS[:, bd, :], [(Wo[:, kc, dcl * 128:(dcl + 1) * 128], merged[:, kc, ll], [Bo, B_merged[kc][sl]])
                                                for kc in range(8)], B_bank[bd])
                        S.op("dve", lambda e, bd=bd, dc=dc, sl_=sl_: e.tensor_tensor(
                            out=xT[:, dc, sl_], in0=PS[:, bd, :], in1=xT[:, dc, sl_], op=ALU.add),
                            reads=[B_bank[bd], B_xT[dc][s]], writes=[B_xT[dc][s]])

        order = ["X", "F1", "ATT", "POST", "F2"]
        lim = order.index(stop_after) if stop_after else len(order) - 1
        tail_done = [False]
        for q in range(nseq):
            if q == 0:
                x_steps(0, range(0, 4))
                if lim >= 1:
                    norm([0], gt[0])
                x_steps(0, range(4, 8))
                flush()
                if lim >= 1:
                    norm([1], gt[0])
            else:
                flush()
                final_steps(q - 1, range(8, 16))
            x_steps(q, range(8, 16))
            if lim >= 1:
                norm([2, 3], gt[0])
                ffn(0, 0, gt[0], first=(q == 0))
                flush()
                if lim >= 2:
                    norm([0, 1], gt[1])
                ffn(0, 1, gt[0])
            flush()
            if lim >= 2:
                S.barrier(["stg0", "stg1", "stg2", "stg3"])
                mixer_attention(q)
            if lim >= 3:
                S.barrier(["augA0", "augA1", "augB0", "augB1"])
                mixer_post(q, 0)
                if lim >= 4:
                    norm([0, 1], gt[2])
                mixer_post(q, 1)
                flush()
            if lim >= 4:
                S.barrier()
                norm([2, 3], gt[2])
                ffn(1, 0, gt[2])
                flush()
                final_steps(q, range(0, 8))
                if q + 1 < nseq:
                    x_steps(q + 1, range(0, 8))
                    norm([0, 1], gt[0])
                if q + 1 < nseq:
                    ffn(1, 1, gt[2])
                else:
                    ffn(1, 1, gt[2], s_outer=True,
                        on_span_done=lambda s_, q=q: final_steps_single(q, range(4 * s_, 4 * s_ + 4)))
                    tail_done[0] = True
            else:
                final_steps(q, range(0, 8))
                if q + 1 < nseq:
                    x_steps(q + 1, range(0, 8))
                    if lim >= 1:
                        norm([0, 1], gt[0])
        flush()
        if not tail_done[0]:
            final_steps_single(nseq - 1, range(8, 16))
            flush()
        S.op("sp", lambda e: e.nop(), extra_deps=[S.dma_last[k] for k in ("stg0", "stg1", "stg2", "stg3") if k in S.dma_last])
        S.emit(nc, sems, dsems)
    return nc


_NC_CACHE = {}


def _prep_inputs(inputs, n_cores=8):
    f = lambda a: np.ascontiguousarray(np.asarray(a, dtype=np.float32))
    x = f(inputs["x"])
    per = x.shape[0] // n_cores
    pk = lambda g: np.ascontiguousarray(f(g).reshape(8, 128).T)
    shared = {
        "g1": pk(inputs["ffn1_norm"]), "gm": pk(inputs["mix_norm"]), "g2": pk(inputs["ffn2_norm"]),
        "gfin": np.ascontiguousarray(np.broadcast_to(f(inputs["final_norm"])[None, :], (128, D))),
        "bfb": np.ascontiguousarray(np.broadcast_to(np.tile(f(inputs["b_forget"]), 16)[None, :], (128, 128))),
        "cw": np.ascontiguousarray(f(inputs["conv_w"]).reshape(3, 4, 128).transpose(2, 1, 0).reshape(128, 12)),
    }
    for k in ("ffn1_gate", "ffn1_up", "ffn1_down", "ffn2_gate", "ffn2_up", "ffn2_down", "w_in", "w_o_attn",
              "w_o_conv", "w_out"):
        shared[k] = f(inputs[k])
    in_maps = []
    for c in range(n_cores):
        m = dict(shared)
        m["x"] = np.ascontiguousarray(x[c * per:(c + 1) * per])
        in_maps.append(m)
    return in_maps


def kernel(**inputs):
    n_cores = 8
    stop = os.environ.get("MK_STOP") or None
    key = stop
    if key not in _NC_CACHE:
        _NC_CACHE[key] = build_nc(stop_after=stop)
    nc = _NC_CACHE[key]
    in_maps = _prep_inputs(inputs, n_cores)
    res = run_bass_kernel_spmd(nc, in_maps, core_ids=list(range(n_cores)))
    return np.concatenate([np.asarray(r["out"]) for r in res.results], axis=0).astype(np.float32)
```
